# Optimizing a Trainium2 kernel written in Bass

```python
import math
import jax
import jax.numpy as jnp
from jax import lax
import numpy as np

D_MODEL = 1024
BATCH = 2
SEQ = 8192
DEPTH = 2
DEC_BATCH = 32
DEC_SEQ = 1
PAST_LEN = 16384
PAGE_SIZE = 128

N_EVEN = (DEPTH + 1) // 2
N_ODD = DEPTH // 2
ATT_HEADS = 8
HEAD_DIM = 64
ATT_WIDTH = ATT_HEADS * HEAD_DIM
ROT_DIM = HEAD_DIM // 4
ROPE_THETA = 500000.0
MOBA_BLOCK = 256
MOBA_TOPK = 3
Q_BLOCK = 128
SSD_HEADS = 8
SSD_HEADDIM = 64
SSD_INNER = SSD_HEADS * SSD_HEADDIM
SSD_GROUPS = 2
SSD_STATE = 128
SSD_CONV = 4
SSD_CONV_DIM = SSD_INNER + 2 * SSD_GROUPS * SSD_STATE
SSD_CHUNK = 128
MIX_IN_DIM = 3 * ATT_WIDTH + SSD_INNER + SSD_CONV_DIM + SSD_HEADS
MIX_OUT_DIM = ATT_WIDTH + SSD_INNER
MIX_SPLITS = (ATT_WIDTH, 2 * ATT_WIDTH, 3 * ATT_WIDTH, 3 * ATT_WIDTH + SSD_INNER, 3 * ATT_WIDTH + SSD_INNER + SSD_CONV_DIM)
POOL_WINDOWS = (2, 4, 8, 16)
POOL_GROUPS = 4
POOL_GROUP_DIM = D_MODEL // POOL_GROUPS
POOL_MAX = 16
D_FF = 2816
FFN_CONV = 3
EPS = 1e-6

kernel_name = "hybrid_moba_ssd_pool_convffn_step"


def rmsnorm(x, g):
    xf = x.astype(jnp.float32)
    y = xf * lax.rsqrt(jnp.mean(xf * xf, axis=-1, keepdims=True) + EPS)
    return (y * g.astype(jnp.float32)).astype(x.dtype)


def adaln(c, w, b):
    mod = (jax.nn.silu(c) @ w + b)[:, None, :]
    return jnp.split(mod, 3, axis=-1)


def rope_partial(x, pos):
    half = ROT_DIM // 2
    inv = jnp.exp(jnp.arange(half, dtype=jnp.float32) * (-2.0 * math.log(ROPE_THETA) / ROT_DIM))
    ang = pos.astype(jnp.float32)[:, None] * inv[None, :]
    cos = jnp.cos(ang)[None, :, None, :]
    sin = jnp.sin(ang)[None, :, None, :]
    xr = x[..., :ROT_DIM].astype(jnp.float32)
    x1, x2 = xr[..., :half], xr[..., half:]
    rot = jnp.concatenate([x1 * cos - x2 * sin, x2 * cos + x1 * sin], axis=-1).astype(x.dtype)
    return jnp.concatenate([rot, x[..., ROT_DIM:]], axis=-1)


def causal_dwconv(x, prev, w, b):
    xx = jnp.concatenate([prev.astype(x.dtype), x], axis=1)
    y = lax.conv_general_dilated(xx, w.astype(x.dtype)[:, None, :], window_strides=(1,), padding="VALID",
                                 dimension_numbers=("NWC", "WIO", "NWC"), feature_group_count=x.shape[-1])
    return y + b.astype(x.dtype), xx[:, xx.shape[1] - (w.shape[0] - 1):]


def moba_attention(q, k_parts, v_parts, q_pos0):
    bsz, t_q, n_h, hd = q.shape
    seq_k = sum(p.shape[1] for p in k_parts)
    nb = -(-seq_k // MOBA_BLOCK)
    pad = jnp.zeros((bsz, nb * MOBA_BLOCK - seq_k, n_h, hd), k_parts[0].dtype)
    kb = jnp.concatenate(list(k_parts) + [pad], axis=1).reshape(bsz, nb, MOBA_BLOCK, n_h, hd)
    vb = jnp.concatenate(list(v_parts) + [pad], axis=1).reshape(bsz, nb, MOBA_BLOCK, n_h, hd)
    kmean = jnp.mean(kb, axis=2, dtype=jnp.float32)
    qc = Q_BLOCK if t_q % Q_BLOCK == 0 else t_q
    nq = t_q // qc
    qs = q.reshape(bsz, nq, qc, n_h, hd).transpose(1, 0, 3, 2, 4)
    bi = jnp.arange(bsz)[:, None, None, None]
    hi = jnp.arange(n_h)[None, :, None, None]
    blk = jnp.arange(nb)
    offs = jnp.arange(MOBA_BLOCK)
    scale = 1.0 / math.sqrt(hd)

    def one_query_block(args):
        qi, ci = args
        tpos = q_pos0 + ci * qc + jnp.arange(qc)
        jown = tpos // MOBA_BLOCK
        gate = jnp.einsum("bhqd,bnhd->bhqn", qi, kmean, preferred_element_type=jnp.float32)
        gate = jnp.where(blk[None, None, None, :] < jown[None, None, :, None], gate, -jnp.inf)
        if nb < MOBA_TOPK:
            gate = jnp.pad(gate, ((0, 0), (0, 0), (0, 0), (0, MOBA_TOPK - nb)), constant_values=-jnp.inf)
        _, sel = lax.top_k(gate, MOBA_TOPK)
        sel = jnp.minimum(sel, nb - 1)
        own = jnp.broadcast_to(jown[None, None, :, None], (bsz, n_h, qc, 1)).astype(sel.dtype)
        blocks = jnp.concatenate([sel, own], axis=-1)
        valid = jnp.concatenate([jnp.arange(MOBA_TOPK)[None, :] < jown[:, None], jnp.ones((qc, 1), dtype=bool)], axis=-1)
        kg = kb[bi, blocks, :, hi]
        vg = vb[bi, blocks, :, hi]
        s = jnp.einsum("bhqd,bhqnkd->bhqnk", qi, kg, preferred_element_type=jnp.float32) * scale
        kpos = blocks[..., None] * MOBA_BLOCK + offs
        mask = (kpos <= tpos[None, None, :, None, None]) & valid[None, None, :, :, None]
        s = jnp.where(mask, s, -jnp.inf)
        p = jax.nn.softmax(s.reshape(bsz, n_h, qc, -1), axis=-1).reshape(s.shape)
        o = jnp.einsum("bhqnk,bhqnkd->bhqd", p.astype(vg.dtype), vg, preferred_element_type=jnp.float32)
        return o.astype(q.dtype)

    out = lax.map(one_query_block, (qs, jnp.arange(nq)))
    return out.transpose(1, 0, 3, 2, 4).reshape(bsz, t_q, n_h, hd)


def ssd_scan(x, dt, a, bm, cm, h0):
    bsz, t, nh, hp = x.shape
    ln = SSD_CHUNK if t % SSD_CHUNK == 0 else t
    nc = t // ln
    rep = nh // SSD_GROUPS
    bh = jnp.repeat(bm, rep, axis=2).reshape(bsz, nc, ln, nh, SSD_STATE)
    ch = jnp.repeat(cm, rep, axis=2).reshape(bsz, nc, ln, nh, SSD_STATE)
    xc = x.reshape(bsz, nc, ln, nh, hp)
    dtc = dt.reshape(bsz, nc, ln, nh)
    acum = jnp.cumsum(dtc * a, axis=2)
    causal = jnp.tril(jnp.ones((ln, ln), dtype=bool))[None, None, :, :, None]
    seg = acum[:, :, :, None, :] - acum[:, :, None, :, :]
    decay = jnp.exp(jnp.where(causal, seg, -jnp.inf))
    scores = jnp.einsum("bclhn,bcshn->bclsh", ch, bh) * decay * dtc[:, :, None, :, :]
    y = jnp.einsum("bclsh,bcshp->bclhp", scores, xc)
    w_end = jnp.exp(acum[:, :, -1:, :] - acum) * dtc
    states = jnp.einsum("bcshn,bcsh,bcshp->bchpn", bh, w_end, xc)
    chunk_decay = jnp.exp(acum[:, :, -1, :])

    def carry_state(h, inp):
        st, dec = inp
        return h * dec[:, :, None, None] + st, h

    h_last, h_start = lax.scan(carry_state, h0, (states.transpose(1, 0, 2, 3, 4), chunk_decay.transpose(1, 0, 2)))
    y = y + jnp.einsum("bclhn,cbhpn,bclh->bclhp", ch, h_start, jnp.exp(acum))
    return y.reshape(bsz, t, nh, hp), h_last


def attn_ssd_mixer(h, pos, pos0, kv_past, conv_prev, ssm_prev, in_w, out_w, conv_w, conv_b, dt_bias, a_log, d_skip, norm_w):
    bsz, t, _ = h.shape
    proj = h @ in_w
    q, k, v, z, xbc, dt_raw = jnp.split(proj, MIX_SPLITS, axis=-1)
    q = rope_partial(q.reshape(bsz, t, ATT_HEADS, HEAD_DIM), pos)
    k = rope_partial(k.reshape(bsz, t, ATT_HEADS, HEAD_DIM), pos)
    v = v.reshape(bsz, t, ATT_HEADS, HEAD_DIM)
    if kv_past is None:
        k_parts, v_parts = [k], [v]
    else:
        k_parts, v_parts = [kv_past[0], k], [kv_past[1], v]
    att = moba_attention(q, k_parts, v_parts, pos0).reshape(bsz, t, ATT_WIDTH)
    xbc, conv_new = causal_dwconv(xbc, conv_prev, conv_w, conv_b)
    xbc = jax.nn.silu(xbc)
    xs, bm, cm = jnp.split(xbc, [SSD_INNER, SSD_INNER + SSD_GROUPS * SSD_STATE], axis=-1)
    xs = xs.reshape(bsz, t, SSD_HEADS, SSD_HEADDIM).astype(jnp.float32)
    bm = bm.reshape(bsz, t, SSD_GROUPS, SSD_STATE).astype(jnp.float32)
    cm = cm.reshape(bsz, t, SSD_GROUPS, SSD_STATE).astype(jnp.float32)
    dt = jax.nn.softplus(dt_raw.astype(jnp.float32) + dt_bias.astype(jnp.float32))
    a = -jnp.exp(a_log.astype(jnp.float32))
    y, ssm_new = ssd_scan(xs, dt, a, bm, cm, ssm_prev)
    y = y + d_skip.astype(jnp.float32)[:, None] * xs
    y = rmsnorm(y.reshape(bsz, t, SSD_INNER) * jax.nn.silu(z.astype(jnp.float32)), norm_w).astype(h.dtype)
    out = jnp.concatenate([att, y], axis=-1) @ out_w
    return out, k, v, conv_new, ssm_new


def pool_mixer(u, prev, pos, w, b, scale):
    bsz, t, d = u.shape
    xx = jnp.concatenate([prev.astype(u.dtype), u], axis=1)
    cs = jnp.cumsum(xx.astype(jnp.float32), axis=1)
    cs = jnp.concatenate([jnp.zeros((bsz, 1, d), jnp.float32), cs], axis=1)
    uf = u.astype(jnp.float32)
    groups = []
    for g, win in enumerate(POOL_WINDOWS):
        lo, hi = g * POOL_GROUP_DIM, (g + 1) * POOL_GROUP_DIM
        wsum = cs[:, POOL_MAX:POOL_MAX + t, lo:hi] - cs[:, POOL_MAX - win:POOL_MAX - win + t, lo:hi]
        cnt = jnp.minimum(pos + 1, win).astype(jnp.float32)[None, :, None]
        groups.append(wsum / cnt - uf[:, :, lo:hi])
    pooled = jnp.stack(groups, axis=2)
    y = jnp.einsum("btgc,gcd->btgd", pooled, w.astype(jnp.float32)) + b.astype(jnp.float32)
    y = y.reshape(bsz, t, d) * scale.astype(jnp.float32)
    return y.astype(u.dtype), xx[:, xx.shape[1] - (POOL_MAX - 1):]


def conv_ffn(u, prev, up_w, conv_w, conv_b, down_w):
    hdn, new_prev = causal_dwconv(u @ up_w, prev, conv_w, conv_b)
    g, v = jnp.split(hdn, 2, axis=-1)
    return (jax.nn.silu(g) * v) @ down_w, new_prev


def run_trunk(x, c, pos0, past, w):
    bsz, t, _ = x.shape
    pos = pos0 + jnp.arange(t, dtype=jnp.int32)
    new_k, new_v, new_ssm, new_sconv, new_pool, new_fconv = [], [], [], [], [], []
    for layer in range(DEPTH):
        e = layer // 2
        shift, scale, gate = adaln(c, w["ada_w"][layer, 0], w["ada_b"][layer, 0])
        h = rmsnorm(x, w["norm_pre"][layer, 0]) * (1 + scale) + shift
        if layer % 2 == 0:
            if past is None:
                kv_past = None
                conv_prev = jnp.zeros((bsz, SSD_CONV - 1, SSD_CONV_DIM), x.dtype)
                ssm_prev = jnp.zeros((bsz, SSD_HEADS, SSD_HEADDIM, SSD_STATE), jnp.float32)
            else:
                pt = past["page_table"]
                kv_past = (past["cache_k"][e, pt].reshape(bsz, -1, ATT_HEADS, HEAD_DIM),
                           past["cache_v"][e, pt].reshape(bsz, -1, ATT_HEADS, HEAD_DIM))
                conv_prev = past["state_ssd_conv"][e]
                ssm_prev = past["state_ssm"][e].astype(jnp.float32)
            out, k_rows, v_rows, conv_new, ssm_new = attn_ssd_mixer(
                h, pos, pos0, kv_past, conv_prev, ssm_prev, w["mix_in_w"][e], w["mix_out_w"][e],
                w["ssd_conv_w"][e], w["ssd_conv_b"][e], w["ssd_dt_bias"][e], w["ssd_a_log"][e],
                w["ssd_d"][e], w["ssd_norm_w"][e])
            new_k.append(k_rows)
            new_v.append(v_rows)
            new_sconv.append(conv_new)
            new_ssm.append(ssm_new)
        else:
            prev = jnp.zeros((bsz, POOL_MAX - 1, D_MODEL), x.dtype) if past is None else past["state_pool"][e]
            out, pool_new = pool_mixer(h, prev, pos, w["pool_w"][e], w["pool_b"][e], w["pool_scale"][e])
            new_pool.append(pool_new)
        x = x + gate * rmsnorm(out, w["norm_post"][layer, 0])
        shift, scale, gate = adaln(c, w["ada_w"][layer, 1], w["ada_b"][layer, 1])
        h = rmsnorm(x, w["norm_pre"][layer, 1]) * (1 + scale) + shift
        prev = jnp.zeros((bsz, FFN_CONV - 1, 2 * D_FF), x.dtype) if past is None else past["state_ffn_conv"][layer]
        out, f_new = conv_ffn(h, prev, w["ffn_up_w"][layer], w["ffn_conv_w"][layer], w["ffn_conv_b"][layer], w["ffn_down_w"][layer])
        new_fconv.append(f_new)
        x = x + gate * rmsnorm(out, w["norm_post"][layer, 1])
    return (x, jnp.stack(new_k), jnp.stack(new_v), jnp.stack(new_ssm), jnp.stack(new_sconv),
            jnp.stack(new_pool), jnp.stack(new_fconv))


def setup_inputs(seed: int = 0) -> dict:
    key = jax.random.key(seed)
    ks = jax.random.split(key, 40)
    f32 = jnp.float32
    n_pages = PAST_LEN // PAGE_SIZE
    n_phys = (DEC_BATCH * n_pages * 5) // 4
    ne, no = N_EVEN, N_ODD

    def nrm(k, shape, s=1.0):
        return jax.random.normal(k, shape, f32) * s

    page_table = jax.random.permutation(ks[0], n_phys)[: DEC_BATCH * n_pages].reshape(DEC_BATCH, n_pages).astype(jnp.int32)
    dt0 = jnp.exp(jax.random.uniform(ks[1], (ne, SSD_HEADS), f32, math.log(1e-3), math.log(1e-1)))
    return {
        "x_prompt": nrm(ks[2], (BATCH, SEQ, D_MODEL)),
        "x_sample": nrm(ks[3], (DEC_BATCH, DEC_SEQ, D_MODEL)),
        "cache_k": nrm(ks[4], (ne, n_phys, PAGE_SIZE, ATT_HEADS, HEAD_DIM)),
        "cache_v": nrm(ks[5], (ne, n_phys, PAGE_SIZE, ATT_HEADS, HEAD_DIM)),
        "state_ssm": nrm(ks[6], (ne, DEC_BATCH, SSD_HEADS, SSD_HEADDIM, SSD_STATE), 0.5),
        "state_ssd_conv": nrm(ks[7], (ne, DEC_BATCH, SSD_CONV - 1, SSD_CONV_DIM)),
        "state_pool": nrm(ks[8], (no, DEC_BATCH, POOL_MAX - 1, D_MODEL)),
        "state_ffn_conv": nrm(ks[9], (DEPTH, DEC_BATCH, FFN_CONV - 1, 2 * D_FF)),
        "page_table": page_table,
        "c_prompt": nrm(ks[10], (BATCH, D_MODEL)),
        "c_sample": nrm(ks[11], (DEC_BATCH, D_MODEL)),
        "ada_w": nrm(ks[12], (DEPTH, 2, D_MODEL, 3 * D_MODEL), 0.5 * D_MODEL ** -0.5),
        "ada_b": nrm(ks[13], (DEPTH, 2, 3 * D_MODEL), 0.02),
        "norm_pre": 1.0 + nrm(ks[14], (DEPTH, 2, D_MODEL), 0.05),
        "norm_post": 1.0 + nrm(ks[15], (DEPTH, 2, D_MODEL), 0.05),
        "mix_in_w": nrm(ks[16], (ne, D_MODEL, MIX_IN_DIM), D_MODEL ** -0.5),
        "mix_out_w": nrm(ks[17], (ne, MIX_OUT_DIM, D_MODEL), MIX_OUT_DIM ** -0.5),
        "ssd_conv_w": nrm(ks[18], (ne, SSD_CONV, SSD_CONV_DIM), SSD_CONV ** -0.5),
        "ssd_conv_b": nrm(ks[19], (ne, SSD_CONV_DIM), 0.02),
        "ssd_dt_bias": dt0 + jnp.log(-jnp.expm1(-dt0)),
        "ssd_a_log": jnp.log(jax.random.uniform(ks[20], (ne, SSD_HEADS), f32, 1.0, 16.0)),
        "ssd_d": 1.0 + nrm(ks[21], (ne, SSD_HEADS), 0.1),
        "ssd_norm_w": 1.0 + nrm(ks[22], (ne, SSD_INNER), 0.05),
        "pool_w": nrm(ks[23], (no, POOL_GROUPS, POOL_GROUP_DIM, POOL_GROUP_DIM), POOL_GROUP_DIM ** -0.5),
        "pool_b": nrm(ks[24], (no, POOL_GROUPS, POOL_GROUP_DIM), 0.02),
        "pool_scale": 1.0 + nrm(ks[25], (no, D_MODEL), 0.1),
        "ffn_up_w": nrm(ks[26], (DEPTH, D_MODEL, 2 * D_FF), D_MODEL ** -0.5),
        "ffn_conv_w": nrm(ks[27], (DEPTH, FFN_CONV, 2 * D_FF), FFN_CONV ** -0.5),
        "ffn_conv_b": nrm(ks[28], (DEPTH, 2 * D_FF), 0.02),
        "ffn_down_w": nrm(ks[29], (DEPTH, D_FF, D_MODEL), D_FF ** -0.5),
    }


def reference(x_prompt, x_sample, cache_k, cache_v, state_ssm, state_ssd_conv, state_pool, state_ffn_conv, page_table,
              c_prompt, c_sample, ada_w, ada_b, norm_pre, norm_post, mix_in_w, mix_out_w, ssd_conv_w, ssd_conv_b,
              ssd_dt_bias, ssd_a_log, ssd_d, ssd_norm_w, pool_w, pool_b, pool_scale, ffn_up_w, ffn_conv_w,
              ffn_conv_b, ffn_down_w):
    weights = {"ada_w": ada_w, "ada_b": ada_b, "norm_pre": norm_pre, "norm_post": norm_post,
               "mix_in_w": mix_in_w, "mix_out_w": mix_out_w, "ssd_conv_w": ssd_conv_w, "ssd_conv_b": ssd_conv_b,
               "ssd_dt_bias": ssd_dt_bias, "ssd_a_log": ssd_a_log, "ssd_d": ssd_d, "ssd_norm_w": ssd_norm_w,
               "pool_w": pool_w, "pool_b": pool_b, "pool_scale": pool_scale, "ffn_up_w": ffn_up_w,
               "ffn_conv_w": ffn_conv_w, "ffn_conv_b": ffn_conv_b, "ffn_down_w": ffn_down_w}
    past = {"cache_k": cache_k, "cache_v": cache_v, "page_table": page_table, "state_ssm": state_ssm,
            "state_ssd_conv": state_ssd_conv, "state_pool": state_pool, "state_ffn_conv": state_ffn_conv}
    past_len = page_table.shape[1] * cache_k.shape[2]
    y_prompt, k_p, v_p, ssm_p, sconv_p, pool_p, fconv_p = run_trunk(x_prompt, c_prompt, 0, None, weights)
    y_sample, k_s, v_s, ssm_s, sconv_s, pool_s, fconv_s = run_trunk(x_sample, c_sample, past_len, past, weights)
    return (y_prompt, y_sample, k_p, v_p, ssm_p, sconv_p, pool_p, fconv_p, k_s, v_s, ssm_s, sconv_s, pool_s, fconv_s)
```

```python
import contextlib
import math
import numpy as np
import concourse.bass as bass
import concourse.mybir as mybir
from concourse.bass_utils import run_bass_kernel_spmd

F32 = mybir.dt.float32
BF16 = mybir.dt.bfloat16
I32 = mybir.dt.int32
U32 = mybir.dt.uint32
AF = mybir.ActivationFunctionType
ALU = mybir.AluOpType
AX = mybir.AxisListType
ENGS = ["pe", "act", "dve", "pool", "sp"]

D = 1024
SEQ = 8192
NCORE = 2
NB_TOT = 2
NS_TOT = 32
NB = NB_TOT // NCORE
NS = NS_TOT // NCORE
HALF = 5120 * 64 // 2
H = 8
HD = 64
MIXIN = 3080
DFF = 2816
T = 256
NTILE = SEQ // T
BIG = 30000.0
EPS = 1e-6
NPAGE = 128
PAST = 16384
FULL = True
DBG = None
DBG_TILES = 2


class Prog:
    def __init__(self, nc, n_dma_sems=24):
        self.nc = nc
        self.ops = []
        self.last_writer = {}
        self.readers = {}
        self.n_dma_sems = n_dma_sems

    def op(self, eng, fn, reads=(), writes=(), dma=False):
        idx = len(self.ops)
        deps = set()
        for k in reads:
            w = self.last_writer.get(k)
            if w is not None:
                deps.add(w)
        for k in writes:
            w = self.last_writer.get(k)
            if w is not None:
                deps.add(w)
            for r in self.readers.get(k, ()):
                deps.add(r)
        deps.discard(idx)
        self.ops.append(dict(eng=eng, fn=fn, deps=deps, dma=dma, has_dep=False))
        for k in writes:
            self.last_writer[k] = idx
            self.readers[k] = []
        for k in reads:
            self.readers.setdefault(k, []).append(idx)
        return idx

    def emit(self):
        nc = self.nc
        ops = self.ops
        for i, o in enumerate(ops):
            for d in o["deps"]:
                p = ops[d]
                if p["eng"] == "pe" and o["eng"] == "pe" and not p["dma"] and not o["dma"]:
                    continue
                p["has_dep"] = True
        cnt = {e: 0 for e in ENGS}
        dma_rr = {e: 0 for e in ENGS}
        dma_target = {}
        for i, o in enumerate(ops):
            if o["dma"]:
                q = o["eng"]
                s = (q, dma_rr[q] % self.n_dma_sems)
                dma_rr[q] += 1
                prev = dma_target.get(s, 0)
                o["dsem"] = s
                o["dprev"] = prev
                dma_target[s] = prev + 16
                o["dtarget"] = prev + 16
            elif o["has_dep"]:
                cnt[o["eng"]] += 1
                o["cnt"] = cnt[o["eng"]]
        per_eng = {e: [i for i, o in enumerate(ops) if o["eng"] == e] for e in ENGS}
        with contextlib.ExitStack() as st:
            esem = {e: st.enter_context(nc.semaphore("s_" + e)) for e in ENGS}
            dsem = {}
            for q in ENGS:
                for j in range(min(self.n_dma_sems, dma_rr[q])):
                    dsem[(q, j)] = st.enter_context(nc.semaphore("d_%s_%d" % (q, j)))
            block = st.enter_context(nc.Block())

            def run(e, handle):
                waited = {}

                def wait(sem_key, sem, val):
                    if waited.get(sem_key, 0) >= val:
                        return
                    handle.wait_ge(sem, val)
                    waited[sem_key] = val

                for i in per_eng[e]:
                    o = ops[i]
                    need = {}
                    for d in o["deps"]:
                        p = ops[d]
                        if p["dma"]:
                            k = ("d",) + p["dsem"]
                            need[k] = max(need.get(k, 0), p["dtarget"])
                        else:
                            if p["eng"] == "pe" and e == "pe" and not o["dma"]:
                                continue
                            k = ("e", p["eng"])
                            need[k] = max(need.get(k, 0), p["cnt"])
                    if o["dma"] and o["dprev"] > 0:
                        k = ("d",) + o["dsem"]
                        need[k] = max(need.get(k, 0), o["dprev"])
                    for k, v in need.items():
                        sem = esem[k[1]] if k[0] == "e" else dsem[(k[1], k[2])]
                        wait(k, sem, v)
                    ins = o["fn"](handle)
                    if o["dma"]:
                        ins.then_inc(dsem[o["dsem"]], 16)
                    elif o["has_dep"]:
                        ins.then_inc(esem[e], 1)
                if e == "sp":
                    for s, tgt in dma_target.items():
                        wait(("d",) + s, dsem[s], tgt)

            @block.tensor
            def _(h):
                run("pe", h)

            @block.scalar
            def _(h):
                run("act", h)

            @block.vector
            def _(h):
                run("dve", h)

            @block.gpsimd
            def _(h):
                run("pool", h)

            @block.sync
            def _(h):
                run("sp", h)


def host_consts():
    c = {}
    c["ident"] = np.eye(128, dtype=np.float32)
    c["ones"] = np.ones((128, 128), np.float32)
    s = np.arange(128)
    c["tri"] = (s[:, None] <= s[None, :]).astype(np.float32)
    c["ssdmask"] = np.where(s[:, None] <= s[None, :], 0.0, -BIG).astype(np.float32)
    q = np.arange(T)
    caus = np.zeros((128, T // 128, T), np.float32)
    for kt in range(T // 128):
        caus[:, kt, :] = np.where((128 * kt + s)[:, None] <= q[None, :], 0.0, -BIG)
    c["caus"] = caus.reshape(128, -1)
    e = np.zeros((32, 32, 128), np.float32)
    for b in range(32):
        e[b, b, :] = 1.0
    c["eall"] = e.reshape(32, -1)
    half = 8
    inv = np.exp(np.arange(half, dtype=np.float32) * np.float32(-2.0 * math.log(500000.0) / 16)).astype(np.float32)
    pos = np.arange(SEQ + 1, dtype=np.float32)
    pos[SEQ] = PAST
    ang = (pos[:, None] * inv[None, :]).astype(np.float32)
    cos = np.cos(ang).astype(np.float32).T
    sin = np.sin(ang).astype(np.float32).T
    c["ropec"] = np.concatenate([cos, cos], 0)
    c["ropes"] = np.concatenate([-sin, sin], 0)
    perm = np.zeros((16, 16), np.float32)
    for m in range(16):
        perm[(m + 8) % 16, m] = 1.0
    c["perm"] = perm
    pm = np.zeros((128, 64), np.float32)
    for p in range(128):
        pm[p, p // 2] = 1.0
    c["pairm"] = pm
    c["pairmT"] = pm.T.copy()
    bd = np.zeros((8, 512), np.float32)
    for h in range(8):
        bd[h, h * 64:(h + 1) * 64] = 1.0
    c["blockdiag"] = bd
    rc = np.zeros((4, T), np.float32)
    for gi, win in enumerate((2, 4, 8, 16)):
        rc[gi] = 1.0 / np.minimum(np.arange(T) + 1, win).astype(np.float32)
    c["rcnt"] = rc.reshape(-1)
    return c


class B:
    pass


def build():
    nc = bass.Bass("TRN2", target_bir_lowering=False)
    P = Prog(nc)
    g = B()
    dram = {}

    def din(name, shape, dt=F32):
        dram[name] = nc.dram_tensor(name, list(shape), dt, kind="ExternalInput").ap()
        return dram[name]

    def dout(name, shape):
        dram[name] = nc.dram_tensor(name, list(shape), F32, kind="ExternalOutput").ap()
        return dram[name]

    def dscr(name, shape, dt=F32):
        dram[name] = nc.dram_tensor(name, list(shape), dt, kind="Internal").ap()
        return dram[name]

    xp = din("x_prompt", [NB * SEQ, D])
    xs_in = din("x_sample", [NS, D])
    ck_h = [din("cache_k%d" % i, [HALF, 1024]) for i in range(2)]
    cv_h = [din("cache_v%d" % i, [HALF, 1024]) for i in range(2)]
    cache_k, cache_v = ck_h, cv_h
    st_ssm = din("state_ssm", [NS, H, HD, 128])
    st_sconv = din("state_ssd_conv", [NS, 3, 1024])
    st_pool = din("state_pool", [NS, 15, D])
    st_fconv = din("state_ffn_conv", [2, NS, 2, 2 * DFF])
    page_table = din("page_table", [NS, NPAGE], I32)
    c_all = din("c_all", [NB + NS, D])
    ada_w = din("ada_w", [2, 2, D, 3 * D])
    ada_b = din("ada_b", [2, 2, 3 * D])
    norm_pre = din("norm_pre", [2, 2, D])
    norm_post = din("norm_post", [2, 2, D])
    mix_in_w = din("mix_in_w", [D, MIXIN])
    mix_out_w = din("mix_out_w", [D, D])
    ssd_conv_w = din("ssd_conv_w", [4, 1024])
    ssd_conv_b = din("ssd_conv_b", [1024])
    ssd_dt_bias = din("ssd_dt_bias", [8])
    ssd_a_log = din("ssd_a_log", [8])
    ssd_d = din("ssd_d", [8])
    ssd_norm_w = din("ssd_norm_w", [512])
    pool_w = din("pool_w", [4, 256, 256])
    pool_b = din("pool_b", [1024])
    pool_scale = din("pool_scale", [1024])
    ffn_up_w = din("ffn_up_w", [2, D, 2 * DFF])
    ffn_conv_w = din("ffn_conv_w", [2, 3, 2 * DFF])
    ffn_conv_b = din("ffn_conv_b", [2, 2 * DFF])
    ffn_down_w = din("ffn_down_w", [2, DFF, D])
    hc = host_consts()
    for k, v in hc.items():
        din("c_" + k, v.shape)

    y_prompt = dout("y_prompt", [NB * SEQ, D])
    y_sample = dout("y_sample", [NS, D])
    k_prompt = dout("k_prompt", [NB * SEQ, 512])
    v_prompt = dout("v_prompt", [NB * SEQ, 512])
    ssm_prompt = dout("ssm_prompt", [NB, H, HD, 128])
    sconv_prompt = dout("sconv_prompt", [NB, 3, 1024])
    pool_prompt = dout("pool_prompt", [NB, 15, D])
    fconv_prompt = dout("fconv_prompt", [2, NB, 2, 2 * DFF])
    k_sample = dout("k_sample", [NS, 512])
    v_sample = dout("v_sample", [NS, 512])
    ssm_sample = dout("ssm_sample", [NS, H, HD, 128])
    sconv_sample = dout("sconv_sample", [NS, 3, 1024])
    pool_sample = dout("pool_sample", [NS, 15, D])
    fconv_sample = dout("fconv_sample", [2, NS, 2, 2 * DFF])

    ksc = dscr("ksc", [H, HD, SEQ], BF16)
    vsc = dscr("vsc", [SEQ, H * 128], BF16)
    rowsc = dscr("rowsc", [NS, 1024])
    dscr("qsc", [NS, 512]); dscr("k2sc", [NS, 512]); dscr("v2sc", [NS, 512])

    def sb(name, shape, dt=F32):
        return nc.alloc_sbuf_tensor(name, list(shape), dt)

    PS = [nc.alloc_psum_tensor("ps%d" % i, [128, 512], F32) for i in range(8)]

    def dve(fn, r, w):
        P.op("dve", fn, r, w)

    def act(fn, r, w):
        P.op("act", fn, r, w)

    def pool(fn, r, w):
        P.op("pool", fn, r, w)

    def pe(fn, r, w):
        P.op("pe", fn, r, w)

    dq = [0]

    def dma(out, in_, r, w, q=None):
        if q is None:
            q = "sp"
        P.op(q, lambda e: e.dma_start(out=out, in_=in_), r, w, dma=True)

    def dma_nc(out, in_, r, w):
        def f(e):
            with nc.allow_non_contiguous_dma(reason="small strided layout change"):
                return e.dma_start(out=out, in_=in_)
        P.op("pool", f, r, w, dma=True)

    cs = {}
    for k, v in hc.items():
        if k in ("ropec", "ropes", "eall", "caus", "rcnt"):
            continue
        t_ = sb("k_" + k, v.shape)
        cs[k] = t_
        dma(t_[:], dram["c_" + k], [], ["k_" + k])
    ident, ones, tri, ssdmask = cs["ident"], cs["ones"], cs["tri"], cs["ssdmask"]
    CK = ["k_" + k for k in hc]
    perm2 = sb("perm2", [128, 16])
    dma(perm2[0:16, :], dram["c_perm"], [], ["perm2"])
    dma(perm2[64:80, :], dram["c_perm"], [], ["perm2"])
    ropec2 = sb("ropec2", [128, T])
    ropes2 = sb("ropes2", [128, T])

    def colvec(name, ap1d, n):
        t_ = sb(name, [128, n // 128])
        dma_nc(t_[:], ap1d.rearrange("(c p) -> p c", p=128), [], [name])
        return t_

    npre = [[colvec("npre%d%d" % (l, j), norm_pre[l, j], D) for j in range(2)] for l in range(2)]
    npost = [[colvec("npost%d%d" % (l, j), norm_post[l, j], D) for j in range(2)] for l in range(2)]
    adab = [[colvec("adab%d%d" % (l, j), ada_b[l, j], 3 * D) for j in range(2)] for l in range(2)]
    scw = [colvec("scw%d" % i, ssd_conv_w[i], 1024) for i in range(4)]
    scb = colvec("scb", ssd_conv_b, 1024)
    snw = colvec("snw", ssd_norm_w, 512)
    pb_c = colvec("pb_c", pool_b, 1024)
    psc_c = colvec("psc_c", pool_scale, 1024)
    fcw = [[colvec("fcw%d%d" % (l, i), ffn_conv_w[l, i], 2 * DFF) for i in range(3)] for l in range(2)]
    fcb = [colvec("fcb%d" % l, ffn_conv_b[l], 2 * DFF) for l in range(2)]
    def rowbc(name, ap1d):
        t_ = sb(name, [128, 8])
        dma_nc(t_[:], ap1d.partition_broadcast(128), [], [name])
        return t_
    dtb = rowbc("dtb", ssd_dt_bias)
    alog = rowbc("alog", ssd_a_log)
    aneg = sb("aneg", [128, 8])
    act(lambda e: e.activation(out=aneg[:], in_=alog[:], func=AF.Exp), ["alog"], ["aneg"])
    dve(lambda e: e.tensor_scalar(out=aneg[:], in0=aneg[:], scalar1=-1.0, scalar2=None, op0=ALU.mult), ["aneg"], ["aneg"])
    dsk = sb("dsk", [128, 4])
    for c in range(4):
        for hh in range(2):
            dma_nc(dsk[hh * 64:(hh + 1) * 64, c:c + 1], ssd_d[2 * c + hh:2 * c + hh + 1].partition_broadcast(64), [], ["dsk"])

    WB = 2816
    wbuf = [sb("wbuf%d" % i, [128, WB]) for i in range(2)]
    wctr = [0]

    def load_w(w2d, kc, col0, ncols):
        i = wctr[0] % 2
        wctr[0] += 1
        key = "wbuf%d" % i
        if w2d.dtype == BF16:
            view = wbuf[i][:].bitcast(BF16)[:, 0:kc * ncols].rearrange("p (k n) -> p k n", k=kc)
            rk = ["wb"]
        else:
            view = wbuf[i][:, 0:kc * ncols].rearrange("p (k n) -> p k n", k=kc)
            rk = []
        src = w2d.rearrange("(k p) n -> p k n", p=128)[:, :, col0:col0 + ncols]
        dma(view, src, rk, [key])
        return view, key

    lctr = [0]

    def linear(xT, xkey, kc, w2d, col0, ncols, Tn, evac):
        nchunk = (ncols + 127) // 128
        grp = max(1, min(4, (WB * (2 if w2d.dtype == BF16 else 1)) // (kc * 128)))
        c = 0
        while c < nchunk:
            gcols = min(grp * 128, ncols - c * 128)
            wv, wkey = load_w(w2d, kc, col0 + c * 128, gcols)
            j = 0
            while j * 128 < gcols:
                m = min(128, gcols - j * 128)
                pi = lctr[0] % 2
                lctr[0] += 1
                ps = PS[pi]
                pskey = "ps%d" % pi

                def f(e, wv=wv, j=j, m=m, ps=ps):
                    ins = None
                    for k in range(kc):
                        ins = e.matmul(ps[0:m, 0:Tn], wv[:, k, j * 128:j * 128 + m], xT[:, k, 0:Tn],
                                       start=(k == 0), stop=(k == kc - 1))
                    return ins
                pe(f, [wkey, xkey], [pskey])
                evac(c + j, ps[0:m, 0:Tn], pskey, m)
                j += 1
            c += (gcols + 127) // 128

    sqb = [sb("sqb%d" % i, [128, T]) for i in range(2)]
    sqc = [0]

    def rstd_of(xT, xkey, kc, Tn, nfeat, name="rstd"):
        ps = PS[3]
        for k in range(kc):
            i = sqc[0] % 2
            sqc[0] += 1
            b_ = sqb[i]
            bb = b_[:].bitcast(BF16)
            act(lambda e, bb=bb, k=k: e.activation(out=bb[:, 0:Tn], in_=xT[:, k, 0:Tn], func=AF.Square), [xkey], ["sqb%d" % i])
            pe(lambda e, bb=bb, k=k: e.matmul(ps[:, 0:Tn], g.onesb[:], bb[:, 0:Tn], start=(k == 0), stop=(k == kc - 1)),
               ["sqb%d" % i, "onesb"], ["ps3"])
        r = g.rstd
        dve(lambda e: e.tensor_scalar(out=r[:, 0:Tn], in0=ps[:, 0:Tn], scalar1=1.0 / nfeat, scalar2=EPS, op0=ALU.mult, op1=ALU.add),
            ["ps3"], ["rstd"])
        act(lambda e: e.activation(out=r[:, 0:Tn], in_=r[:, 0:Tn], func=AF.Sqrt), ["rstd"], ["rstd"])
        dve(lambda e: e.reciprocal(out=r[:, 0:Tn], in_=r[:, 0:Tn]), ["rstd"], ["rstd"])
        return r

    g.rstd = sb("rstd", [128, T])
    g.onesb = sb("onesb", [128, 128], BF16)
    dve(lambda e: e.tensor_copy(out=g.onesb[:], in_=ones[:]), ["k_ones"], ["onesb"])
    g.tmp = sb("tmpT", [128, T])

    def bc(ap_col, Tn, ntok):
        if ntok == 1:
            return ap_col.to_broadcast([128, Tn])
        return ap_col

    def norm_mod(xT, xkey, Tn, A, B_, akey, ntok, tok0, hT, hkey):
        r = rstd_of(xT, xkey, 8, Tn, D)
        tmp = g.tmp
        for c in range(8):
            a_ap = bc(A[:, c, tok0:tok0 + ntok], Tn, ntok)
            b_ap = bc(B_[:, c, tok0:tok0 + ntok], Tn, ntok)
            dve(lambda e, c=c: e.tensor_tensor(out=tmp[:, 0:Tn], in0=xT[:, c, 0:Tn], in1=r[:, 0:Tn], op=ALU.mult), [xkey, "rstd"], ["tmpT"])
            dve(lambda e, a_ap=a_ap: e.tensor_tensor(out=tmp[:, 0:Tn], in0=tmp[:, 0:Tn], in1=a_ap, op=ALU.mult), ["tmpT", akey], ["tmpT"])
            dve(lambda e, c=c, b_ap=b_ap: e.tensor_tensor(out=hT[:, c, 0:Tn], in0=tmp[:, 0:Tn], in1=b_ap, op=ALU.add), ["tmpT", akey], [hkey])

    def post_resid(oT, okey, Tn, G, gkey, ntok, tok0, xT, xkey):
        r = rstd_of(oT, okey, 8, Tn, D)
        tmp = g.tmp
        for c in range(8):
            g_ap = bc(G[:, c, tok0:tok0 + ntok], Tn, ntok)
            dve(lambda e, c=c: e.tensor_tensor(out=tmp[:, 0:Tn], in0=oT[:, c, 0:Tn], in1=r[:, 0:Tn], op=ALU.mult), [okey, "rstd"], ["tmpT"])
            dve(lambda e, g_ap=g_ap: e.tensor_tensor(out=tmp[:, 0:Tn], in0=tmp[:, 0:Tn], in1=g_ap, op=ALU.mult), ["tmpT", gkey], ["tmpT"])
            dve(lambda e, c=c: e.tensor_tensor(out=xT[:, c, 0:Tn], in0=xT[:, c, 0:Tn], in1=tmp[:, 0:Tn], op=ALU.add), ["tmpT", xkey], [xkey])

    trb = sb("trb", [128, 128])

    def to_feature_major(src2d, nrows, ncolchunks, dstT, dkey, col_off=0, rkey=None):
        stg = g.stg
        dma(stg[0:nrows, 0:ncolchunks * 128], src2d, [rkey] if rkey else [], ["stg"])
        for c in range(ncolchunks):
            pe(lambda e, c=c: e.transpose(PS[2][:, 0:nrows], stg[0:nrows, c * 128:(c + 1) * 128], ident[0:nrows, 0:nrows]),
               ["stg", "k_ident"], ["ps2"])
            act(lambda e, c=c: e.copy(out=dstT[:, c, col_off:col_off + nrows], in_=PS[2][:, 0:nrows]), ["ps2"], [dkey])

    g.stg = sb("stg", [128, 1024])
    g.ostg = sb("ostg", [128, 1024])

    def to_token_major_out(srcT, skey, nchunks, ntok, tok_off, dst2d, extra=None, wkey=None):
        ostg = g.ostg
        for c in range(nchunks):
            pe(lambda e, c=c: e.transpose(PS[2][0:ntok, 0:128], srcT[:, c, tok_off:tok_off + ntok], ident[:]),
               [skey, "k_ident"], ["ps2"])
            act(lambda e, c=c: e.copy(out=ostg[0:ntok, c * 128:(c + 1) * 128], in_=PS[2][0:ntok, 0:128]), ["ps2"], ["ostg"])
            if extra is not None:
                extra(c)
        dma(dst2d, ostg[0:ntok, 0:nchunks * 128], ["ostg"], [wkey] if wkey else [])

    NC_ = NB + NS
    cT = sb("cT", [128, 8, NC_])
    to_feature_major(c_all, NC_, 8, cT, "cT")
    act(lambda e: e.activation(out=cT[:], in_=cT[:], func=AF.Silu), ["cT"], ["cT"])
    mod = [[sb("mod%d%d" % (l, j), [128, 24, NC_]) for j in range(2)] for l in range(2)]
    Amod = [[mod[l][j][:, 8:16, :] for j in range(2)] for l in range(2)]
    Gmod = [[mod[l][j][:, 16:24, :] for j in range(2)] for l in range(2)]
    for l in range(2):
        for j in range(2):
            mk = "mod%d%d" % (l, j)
            m_ = mod[l][j]
            ab = adab[l][j]

            def ev(ci, ps_ap, pskey, m, m_=m_, ab=ab, mk=mk):
                dve(lambda e: e.tensor_scalar(out=m_[:, ci, :], in0=ps_ap, scalar1=ab[:, ci:ci + 1], scalar2=None, op0=ALU.add),
                    [pskey, ab.name if hasattr(ab, "name") else mk], [mk])
            linear(cT, "cT", 8, ada_w[l, j], 0, 3 * D, NC_, ev)
            for c in range(8):
                dve(lambda e, c=c, l=l, j=j: e.tensor_scalar(out=Amod[l][j][:, c, :], in0=mod[l][j][:, 8 + c, :], scalar1=1.0,
                                                             scalar2=npre[l][j][:, c:c + 1], op0=ALU.add, op1=ALU.mult),
                    [mk, "npre%d%d" % (l, j)], [mk])
                dve(lambda e, c=c, l=l, j=j: e.tensor_scalar(out=Gmod[l][j][:, c, :], in0=mod[l][j][:, 16 + c, :],
                                                             scalar1=npost[l][j][:, c:c + 1], scalar2=None, op0=ALU.mult),
                    [mk, "npost%d%d" % (l, j)], [mk])

    g.nc, g.P, g.dram, g.cs, g.PS = nc, P, dram, cs, PS
    g.fn = dict(sb=sb, dve=dve, act=act, pool=pool, pe=pe, dma=dma, dma_nc=dma_nc, linear=linear, rstd_of=rstd_of,
                norm_mod=norm_mod, post_resid=post_resid, to_feature_major=to_feature_major,
                to_token_major_out=to_token_major_out, bc=bc)
    cvc = [0]

    def convert(w2d, name, K, N):
        wb_ = dscr(name, [K, N], BF16)
        for r0 in range(0, K, 128):
            for c0 in range(0, N, 2048):
                n = min(2048, N - c0)
                i = cvc[0] % 2
                cvc[0] += 1
                stq, sk = (g.stg, "stg") if i == 0 else (g.ostg, "ostg")
                sview = stq[:].bitcast(BF16)
                dma(wbuf[i][:, 0:n], w2d[r0:r0 + 128, c0:c0 + n], [], ["wbuf%d" % i])
                if i == 0:
                    dve(lambda e, i=i, n=n, sview=sview: e.tensor_copy(out=sview[:, 0:n], in_=wbuf[i][:, 0:n]), ["wbuf%d" % i], [sk])
                else:
                    pool(lambda e, i=i, n=n, sview=sview: e.tensor_copy(out=sview[:, 0:n], in_=wbuf[i][:, 0:n]), ["wbuf%d" % i], [sk])
                dma(wb_[r0:r0 + 128, c0:c0 + n], sview[:, 0:n], [sk], ["wb"])
        return wb_
    mix_in_wb = convert(mix_in_w, "wb_mixin", D, MIXIN)
    mix_out_wb = convert(mix_out_w, "wb_mixout", D, D)
    ffn_up_wb = [convert(ffn_up_w[l], "wb_up%d" % l, D, 2 * DFF) for l in range(2)]
    ffn_down_wb = [convert(ffn_down_w[l], "wb_down%d" % l, DFF, D) for l in range(2)]
    pool_wb = [convert(pool_w[gi], "wb_pool%d" % gi, 256, 256) for gi in range(4)]
    g.w = dict(mix_in_w=mix_in_wb, mix_out_w=mix_out_wb, ffn_up_w=ffn_up_wb, ffn_down_w=ffn_down_wb, pool_w=pool_wb)
    g.vec = dict(scw=scw, scb=scb, snw=snw, pb_c=pb_c, psc_c=psc_c, fcw=fcw, fcb=fcb, dtb=dtb, aneg=aneg, dsk=dsk,
                 perm2=perm2, ropec2=ropec2, ropes2=ropes2)
    g.mod, g.Amod, g.Gmod = mod, Amod, Gmod
    g.io = dict(xp=xp, xs_in=xs_in, cache_k=cache_k, cache_v=cache_v, st_ssm=st_ssm, st_sconv=st_sconv, st_pool=st_pool,
                st_fconv=st_fconv, page_table=page_table, ksc=ksc, vsc=vsc, rowsc=rowsc, ck_h=ck_h, cv_h=cv_h)
    g.out = dict(y_prompt=y_prompt, y_sample=y_sample, k_prompt=k_prompt, v_prompt=v_prompt, ssm_prompt=ssm_prompt,
                 sconv_prompt=sconv_prompt, pool_prompt=pool_prompt, fconv_prompt=fconv_prompt, k_sample=k_sample,
                 v_sample=v_sample, ssm_sample=ssm_sample, sconv_sample=sconv_sample, pool_sample=pool_sample,
                 fconv_sample=fconv_sample)
    return g


def emit_front(g, xsrc_rows, ntok_tile, Tn, mcol, mtok, pos_col, kout, vout, tag, vextra=None, vdone=None):
    f = g.fn
    sb, dve, act, pe, dma, dma_nc = f["sb"], f["dve"], f["act"], f["pe"], f["dma"], f["dma_nc"]
    PS = g.PS
    t = g.t
    off = 0
    for ap_, n in xsrc_rows:
        f["to_feature_major"](ap_, n, 8, t["xT"], "xT", col_off=off)
        off += n
    f["norm_mod"](t["xT"], "xT", Tn, g.Amod[0][0], g.mod[0][0], "mod00", mtok, mcol, t["hT"], "hT")

    def ev(ci, ps_ap, pskey, m):
        if ci < 4:
            dst, key = t["qT"][:, ci, 0:Tn], "qT"
        elif ci < 8:
            dst, key = t["kT"][:, ci - 4, 0:Tn], "kT"
        elif ci < 12:
            dst, key = t["vT"][:, ci - 8, 0:Tn], "vT"
        elif ci < 16:
            dst, key = t["zT"][:, ci - 12, 0:Tn], "zT"
        elif ci < 24:
            dst, key = t["xbc"][:, ci - 16, 3:3 + Tn], "xbc"
        else:
            dst, key = t["dtT"][0:8, 0:Tn], "dtT"
        act(lambda e: e.copy(out=dst, in_=ps_ap), [pskey], [key])
    f["linear"](t["hT"], "hT", 8, g.w["mix_in_w"], 0, MIXIN, Tn, ev)
    rc, rs = g.vec["ropec2"], g.vec["ropes2"]
    for base in (0, 64):
        if Tn == T:
            dma(rc[base:base + 16, 0:Tn], g.dram["c_ropec"][:, pos_col:pos_col + Tn], [], ["ropec2"])
            dma(rs[base:base + 16, 0:Tn], g.dram["c_ropes"][:, pos_col:pos_col + Tn], [], ["ropes2"])
        else:
            sl_ = slice(base, base + 16)
            dma_nc(t["r16"][sl_, 0:1], g.dram["c_ropec"][:, pos_col:pos_col + 1], [], ["r16"])
            dve(lambda e, sl_=sl_: e.tensor_copy(out=rc[sl_, 0:Tn], in_=t["r16"][sl_, 0:1].to_broadcast([16, Tn])), ["r16"], ["ropec2"])
            dma_nc(t["r16"][sl_, 0:1], g.dram["c_ropes"][:, pos_col:pos_col + 1], [], ["r16"])
            dve(lambda e, sl_=sl_: e.tensor_copy(out=rs[sl_, 0:Tn], in_=t["r16"][sl_, 0:1].to_broadcast([16, Tn])), ["r16"], ["ropes2"])
    perm2 = g.vec["perm2"]
    r16 = t["r16"]
    for name in ("qT", "kT"):
        X = t[name]
        for c in range(4):
            for base in (0, 64):
                sl = slice(base, base + 16)
                pe(lambda e, X=X, c=c, sl=sl: e.matmul(PS[7][sl, 0:Tn], perm2[sl, :], X[sl, c, 0:Tn], start=True, stop=True),
                   [name, "perm2"], ["ps7"])
                dve(lambda e, sl=sl: e.tensor_tensor(out=r16[sl, 0:Tn], in0=PS[7][sl, 0:Tn], in1=rs[sl, 0:Tn], op=ALU.mult),
                    ["ps7", "ropes2"], ["r16"])
                dve(lambda e, X=X, c=c, sl=sl: e.tensor_tensor(out=X[sl, c, 0:Tn], in0=X[sl, c, 0:Tn], in1=rc[sl, 0:Tn], op=ALU.mult),
                    [name, "ropec2"], [name])
                dve(lambda e, X=X, c=c, sl=sl: e.tensor_tensor(out=X[sl, c, 0:Tn], in0=X[sl, c, 0:Tn], in1=r16[sl, 0:Tn], op=ALU.add),
                    [name, "r16"], [name])
    off = 0
    for r, (ko, vo, n) in enumerate(zip(kout, vout, [n for _, n in xsrc_rows])):
        f["to_token_major_out"](t["kT"], "kT", 4, n, off, ko)
        f["to_token_major_out"](t["vT"], "vT", 4, n, off, vo, extra=(None if vextra is None else (lambda c, r=r: vextra(r, c))))
        if vdone is not None:
            vdone(r)
        off += n


def alloc_tiles(g):
    sb = g.fn["sb"]
    t = {}
    t["hT"] = sb("hT", [128, 8, T], BF16)
    t["mixTb"] = sb("mixTb", [128, 8, T], BF16)
    for name, nchunk, width in (("xT", 8, T), ("qT", 4, T), ("kT", 4, T), ("vT", 4, T), ("zT", 4, T),
                                ("xbc", 8, T + 3), ("xa", 8, T), ("mixT", 8, T), ("oT", 8, T), ("actT", 22, T),
                                ("pb", 8, T + 15)):
        t[name] = sb(name, [128, nchunk, width])
    for name, w in (("dtT", T), ("r16", T), ("accb", T), ("dtk", 8), ("dA", 8), ("acs", 8), ("nacs", 8), ("dabc", 128),
                    ("Dm", 128), ("Eh", 128), ("Mm", 128), ("Cp", 128), ("xw", 64), ("wcol", 1), ("ytmp", 128),
                    ("gsb", 40), ("top8", 8), ("mq", 32), ("rr", T), ("rr2", T), ("pA", T + 15), ("pB", T + 15)):
        t[name] = sb(name, [128, w])
    t["Gsb"] = sb("Gsb", [128, 2, 128])
    t["xtok"] = sb("xtok", [128, 4, 128])
    t["Btok"] = sb("Btok", [128, 2, 128])
    t["S_T"] = sb("S_T", [128, 8, 64])
    t["so_sb"] = sb("so_sb", [128, 8, 128])
    t["ksumT"] = sb("ksumT", [128, 4, 32])
    t["vtok"] = sb("vtok", [128, 8, 128], BF16)
    t["kTb"] = sb("kTb", [128, 4, T], BF16)
    t["qTb"] = sb("qTb", [128, 4, T], BF16)
    t["causb"] = sb("causb", [128, T // 128, T], BF16)
    t["identb"] = sb("identb", [128, 128], BF16)
    t["maskrow"] = sb("maskrow", [32, T])
    t["Eb"] = [sb("Eb%d" % i, [32, 128]) for i in range(2)]
    t["Kc"] = [sb("Kc%d" % i, [128, 1024]) for i in range(2)]
    t["Vc"] = [sb("Vc%d" % i, [128, 8, 128]) for i in range(2)]
    t["pT"] = [sb("pT%d" % i, [128, T]) for i in range(2)]
    t["hb"] = [sb("hb%d" % i, [128, T + 2]) for i in range(2)]
    t["fcar"] = [sb("fcar%d" % l, [128, 44, 2]) for l in range(2)]
    t["caus"] = sb("caus_sb", [128, T // 128, T])
    g.fn["dma"](t["caus"][:].rearrange("p a b -> p (a b)"), g.dram["c_caus"], [], ["caus_sb"])
    g.fn["dve"](lambda e: e.tensor_copy(out=t["causb"][:], in_=t["caus"][:]), ["caus_sb"], ["causb"])
    g.fn["dve"](lambda e: e.tensor_copy(out=t["identb"][:], in_=g.cs["ident"][:]), ["k_ident"], ["identb"])
    g.t = t


def emit_ssd_conv(g, Tn):
    f = g.fn
    dve, act = f["dve"], f["act"]
    t, v = g.t, g.vec
    xin, accb = t["xbc"], t["accb"]
    for c in range(8):
        dve(lambda e, c=c: e.tensor_scalar(out=accb[:, 0:Tn], in0=xin[:, c, 0:Tn], scalar1=v["scw"][0][:, c:c + 1], scalar2=None,
                                           op0=ALU.mult), ["xbc", "scw0"], ["accb"])
        for j in range(1, 4):
            dve(lambda e, c=c, j=j: e.scalar_tensor_tensor(out=accb[:, 0:Tn], in0=xin[:, c, j:j + Tn], scalar=v["scw"][j][:, c:c + 1],
                                                           in1=accb[:, 0:Tn], op0=ALU.mult, op1=ALU.add),
                ["xbc", "accb", "scw%d" % j], ["accb"])
        act(lambda e, c=c: e.activation(out=t["xa"][:, c, 0:Tn], in_=accb[:, 0:Tn], func=AF.Silu, bias=v["scb"][:, c:c + 1]),
            ["accb", "scb"], ["xa"])


def emit_ssd_scan(g):
    f = g.fn
    dve, act, pe = f["dve"], f["act"], f["pe"]
    t, v, PS, cs = g.t, g.vec, g.PS, g.cs
    ident, tri, ssdmask = cs["ident"], cs["tri"], cs["ssdmask"]
    xa = t["xa"]
    for ck in range(T // 128):
        l0 = ck * 128
        sl = slice(l0, l0 + 128)
        pe(lambda e, sl=sl: e.transpose(PS[7][:, 0:8], t["dtT"][0:8, sl], ident[0:8, 0:8]), ["dtT", "k_ident"], ["ps7"])
        dve(lambda e: e.tensor_tensor(out=t["dtk"][:], in0=PS[7][:, 0:8], in1=v["dtb"][:], op=ALU.add), ["ps7", "dtb"], ["dtk"])
        act(lambda e: e.activation(out=t["dtk"][:], in_=t["dtk"][:], func=AF.Exp), ["dtk"], ["dtk"])
        act(lambda e: e.activation(out=t["dtk"][:], in_=t["dtk"][:], func=AF.Ln, bias=cs["ones"][:, 0:1]), ["dtk", "k_ones"], ["dtk"])
        dve(lambda e: e.tensor_tensor(out=t["dA"][:], in0=t["dtk"][:], in1=v["aneg"][:], op=ALU.mult), ["dtk", "aneg"], ["dA"])
        pe(lambda e: e.matmul(PS[7][:, 8:16], tri[:], t["dA"][:], start=True, stop=True), ["dA", "k_tri"], ["ps7"])
        act(lambda e: e.copy(out=t["acs"][:], in_=PS[7][:, 8:16]), ["ps7"], ["acs"])
        dve(lambda e: e.tensor_scalar(out=t["nacs"][:], in0=t["acs"][:], scalar1=-1.0, scalar2=None, op0=ALU.mult), ["acs"], ["nacs"])
        for c in range(4):
            pe(lambda e, c=c, sl=sl: e.transpose(PS[2][:, 0:128], xa[:, c, sl], ident[:]), ["xa", "k_ident"], ["ps2"])
            act(lambda e, c=c: e.copy(out=t["xtok"][:, c, :], in_=PS[2][:, 0:128]), ["ps2"], ["xtok"])
        for gi in range(2):
            pe(lambda e, gi=gi, sl=sl: e.transpose(PS[2][:, 0:128], xa[:, 4 + gi, sl], ident[:]), ["xa", "k_ident"], ["ps2"])
            act(lambda e, gi=gi: e.copy(out=t["Btok"][:, gi, :], in_=PS[2][:, 0:128]), ["ps2"], ["Btok"])
            pe(lambda e, gi=gi, sl=sl: e.matmul(PS[4][:, 0:128], xa[:, 4 + gi, sl], xa[:, 6 + gi, sl], start=True, stop=True),
               ["xa"], ["ps4"])
            act(lambda e, gi=gi: e.copy(out=t["Gsb"][:, gi, :], in_=PS[4][:, 0:128]), ["ps4"], ["Gsb"])
        for h in range(8):
            gi, c, base = h // 4, h // 2, (h % 2) * 64
            bs = slice(base, base + 64)
            dve(lambda e, h=h: e.tensor_copy(out=t["dabc"][:], in_=t["dA"][:, h:h + 1].to_broadcast([128, 128])), ["dA"], ["dabc"])
            pe(lambda e: e.matmul(PS[5][:, 128:256], t["dabc"][:], tri[:], start=True, stop=True), ["dabc", "k_tri"], ["ps5e"])

            def fd(e):
                e.matmul(PS[5][:, 0:128], t["dabc"][:], tri[:], start=True, stop=False)
                return e.matmul(PS[5][:, 0:128], ident[:], ssdmask[:], start=False, stop=True)
            pe(fd, ["dabc", "k_tri", "k_ident", "k_ssdmask", "ps5e"], ["ps5d", "ps5e_order"])
            act(lambda e, h=h: e.activation(out=t["Dm"][:], in_=PS[5][:, 0:128], func=AF.Exp, bias=t["nacs"][:, h:h + 1]),
                ["ps5d", "nacs"], ["Dm"])
            act(lambda e: e.activation(out=t["Eh"][:], in_=PS[5][:, 128:256], func=AF.Exp), ["ps5e", "ps5e_order"], ["Eh"])
            dve(lambda e, gi=gi, h=h: e.scalar_tensor_tensor(out=t["Mm"][:], in0=t["Gsb"][:, gi, :], scalar=t["dtk"][:, h:h + 1],
                                                             in1=t["Dm"][:], op0=ALU.mult, op1=ALU.mult), ["Gsb", "dtk", "Dm"], ["Mm"])
            dve(lambda e, gi=gi, sl=sl: e.tensor_tensor(out=t["Cp"][:], in0=xa[:, 6 + gi, sl], in1=t["Eh"][:], op=ALU.mult),
                ["xa", "Eh"], ["Cp"])

            def fy(e, c=c, bs=bs, h=h):
                e.matmul(PS[6][0:64, 0:128], t["xtok"][:, c, bs], t["Mm"][:], start=True, stop=False)
                return e.matmul(PS[6][0:64, 0:128], t["S_T"][:, h, :], t["Cp"][:], start=False, stop=True)
            pe(fy, ["xtok", "Mm", "S_T", "Cp"], ["ps6"])
            dve(lambda e, bs=bs: e.tensor_copy(out=t["ytmp"][bs, :], in_=PS[6][0:64, 0:128]), ["ps6"], ["ytmp"])
            dve(lambda e, bs=bs, c=c, sl=sl: e.scalar_tensor_tensor(out=t["mixT"][bs, 4 + c, sl], in0=xa[bs, c, sl],
                                                                    scalar=v["dsk"][bs, c:c + 1], in1=t["ytmp"][bs, :],
                                                                    op0=ALU.mult, op1=ALU.add), ["xa", "dsk", "ytmp"], ["mixT"])
            dve(lambda e, h=h: e.tensor_tensor(out=t["wcol"][:], in0=t["Dm"][:, 127:128], in1=t["dtk"][:, h:h + 1], op=ALU.mult),
                ["Dm", "dtk"], ["wcol"])
            dve(lambda e, c=c, bs=bs: e.tensor_scalar(out=t["xw"][:], in0=t["xtok"][:, c, bs], scalar1=t["wcol"][:, 0:1], scalar2=None,
                                                      op0=ALU.mult), ["xtok", "wcol"], ["xw"])
            pe(lambda e, gi=gi: e.matmul(PS[6][:, 128:192], t["Btok"][:, gi, :], t["xw"][:], start=True, stop=True),
               ["Btok", "xw", "ps6"], ["ps6u"])
            dve(lambda e, h=h: e.scalar_tensor_tensor(out=t["S_T"][:, h, :], in0=t["S_T"][:, h, :], scalar=t["Eh"][:, 127:128],
                                                      in1=PS[6][:, 128:192], op0=ALU.mult, op1=ALU.add), ["S_T", "Eh", "ps6u"], ["S_T", "ps6"])


def emit_state_out(g, dst):
    f = g.fn
    t, PS, cs = g.t, g.PS, g.cs
    for h in range(8):
        f["pe"](lambda e, h=h: e.transpose(PS[2][0:64, 0:128], t["S_T"][:, h, :], cs["ident"][:]), ["S_T", "k_ident"], ["ps2"])
        f["act"](lambda e, h=h: e.copy(out=t["so_sb"][0:64, h, :], in_=PS[2][0:64, 0:128]), ["ps2"], ["so_sb"])
    f["dma"](dst.rearrange("h p n -> p h n"), t["so_sb"][0:64, :, :], ["so_sb"], [])


def emit_attn_prompt(g, i):
    f = g.fn
    dve, act, pe, dma, pool = f["dve"], f["act"], f["pe"], f["dma"], f["pool"]
    t, PS, cs, io = g.t, g.PS, g.cs, g.io
    ident = cs["ident"]
    nkeys = (i + 1) * T
    gated = i > 3
    mrow_b = t["maskrow"][:].bitcast(BF16)
    for h in range(8):
        c, base = h // 2, (h % 2) * 64
        bs = slice(base, base + 64)
        if gated:
            for r in range(T // 128):
                qs = slice(r * 128, (r + 1) * 128)
                pe(lambda e, c=c, bs=bs, qs=qs: e.matmul(PS[7][:, 0:32], t["qT"][bs, c, qs], t["ksumT"][bs, c, :], start=True, stop=True),
                   ["qT", "ksumT"], ["ps7"])
                pool(lambda e: e.memset(t["gsb"][:], -BIG), [], ["gsb"])
                act(lambda e: e.copy(out=t["gsb"][:, 0:i], in_=PS[7][:, 0:i]), ["ps7"], ["gsb"])
                n8 = max(i, 8)
                dve(lambda e, n8=n8: e.max(out=t["top8"][:], in_=t["gsb"][:, 0:n8]), ["gsb"], ["top8"])
                dve(lambda e: e.tensor_scalar(out=t["mq"][:], in0=t["gsb"][:, 0:32], scalar1=t["top8"][:, 2:3], scalar2=-BIG,
                                              op0=ALU.is_lt, op1=ALU.mult), ["gsb", "top8"], ["mq"])
                pe(lambda e: e.transpose(PS[7][0:32, 128:256], t["mq"][:], ident[:]), ["mq", "k_ident"], ["ps7"])
                act(lambda e, qs=qs: e.copy(out=mrow_b[:, qs], in_=PS[7][0:32, 128:256]), ["ps7"], ["maskrow"])
        nch = (nkeys + 1023) // 1024
        for ch in range(nch):
            n = min(1024, nkeys - ch * 1024)
            bi = (h * 64 + ch) % 2
            Kc = t["Kc"][bi][:].bitcast(BF16)
            Vc = t["Vc"][bi][:].rearrange("p a f -> p (a f)").bitcast(BF16).rearrange("p (a f) -> p a f", f=128)
            dma(Kc[bs, 0:n], io["ksc"][h, :, ch * 1024:ch * 1024 + n], ["ksc"], ["Kc%d" % bi])
            dma(Vc[:, 0:n // 128, :], io["vsc"][ch * 1024:ch * 1024 + n, h * 128:(h + 1) * 128].rearrange("(a p) f -> p a f", p=128),
                ["vsc"], ["Vc%d" % bi])
            for kl in range(n // 128):
                kt = ch * 8 + kl
                b = kt // (T // 128)
                pi = kt % 2
                use_mask = gated and b < i
                if use_mask:
                    dve(lambda e, b=b, pi=pi: e.tensor_copy(out=t["Eb"][pi][:].bitcast(BF16)[:, 0:128],
                                                            in_=ident[0:32, b:b + 1].to_broadcast([32, 128])),
                        ["k_ident"], ["Eb%d" % pi])

                def fs(e, kl=kl, pi=pi, b=b, use_mask=use_mask, Kc=Kc, kt=kt, bs=bs, c=c):
                    last = not use_mask and b != i
                    ins = e.matmul(PS[4 + pi][:, 0:T], Kc[bs, kl * 128:(kl + 1) * 128], t["qTb"][bs, c, 0:T], start=True, stop=last)
                    if use_mask:
                        ins = e.matmul(PS[4 + pi][:, 0:T], t["Eb"][pi][:].bitcast(BF16)[:, 0:128], mrow_b[:, 0:T], start=False, stop=True)
                    if b == i:
                        ins = e.matmul(PS[4 + pi][:, 0:T], t["identb"][:], t["causb"][:, kt - i * (T // 128), :], start=False, stop=True)
                    return ins
                pe(fs, ["Kc%d" % bi, "qTb", "Eb%d" % pi, "maskrow", "causb", "identb"], ["ps%d" % (4 + pi)])
                act(lambda e, pi=pi: e.activation(out=t["pT"][pi][:].bitcast(BF16)[:, 0:T], in_=PS[4 + pi][:, 0:T], func=AF.Exp,
                                                  scale=1.0 / math.sqrt(HD)),
                    ["ps%d" % (4 + pi)], ["pT%d" % pi])
                pe(lambda e, kl=kl, pi=pi, Vc=Vc, kt=kt: e.matmul(PS[6][:, 0:T], Vc[:, kl, :], t["pT"][pi][:].bitcast(BF16)[:, 0:T],
                                                                 start=(kt == 0), stop=(kt == nkeys // 128 - 1)),
                   ["Vc%d" % bi, "pT%d" % pi], ["ps6"])
        orow = bs
        srow = slice(64 - base, 128 - base)
        dve(lambda e, srow=srow: e.reciprocal(out=t["rr"][srow, :], in_=PS[6][srow, 0:T]), ["ps6"], ["rr"])
        dve(lambda e, srow=srow, orow=orow: e.tensor_copy(out=t["rr2"][orow, :], in_=t["rr"][srow, :]), ["rr"], ["rr2"])
        dve(lambda e, orow=orow, c=c: e.tensor_tensor(out=t["mixTb"][orow, c, :], in0=PS[6][orow, 0:T], in1=t["rr2"][orow, :], op=ALU.mult),
            ["ps6", "rr2"], ["mixTb", "ps6"])


def emit_mix_out(g, Tn, mtok, mcol):
    f = g.fn
    dve, act = f["dve"], f["act"]
    t, v = g.t, g.vec
    for c in range(4):
        act(lambda e, c=c: e.activation(out=t["zT"][:, c, 0:Tn], in_=t["zT"][:, c, 0:Tn], func=AF.Silu), ["zT"], ["zT"])
        dve(lambda e, c=c: e.tensor_tensor(out=t["mixT"][:, 4 + c, 0:Tn], in0=t["mixT"][:, 4 + c, 0:Tn], in1=t["zT"][:, c, 0:Tn], op=ALU.mult),
            ["mixT", "zT"], ["mixT"])
    r = f["rstd_of"](t["mixT"][:, 4:8, :], "mixT", 4, Tn, 512)
    for c in range(4):
        dve(lambda e, c=c: e.tensor_tensor(out=t["mixT"][:, 4 + c, 0:Tn], in0=t["mixT"][:, 4 + c, 0:Tn], in1=r[:, 0:Tn], op=ALU.mult),
            ["mixT", "rstd"], ["mixT"])
        dve(lambda e, c=c: e.tensor_scalar(out=t["mixTb"][:, 4 + c, 0:Tn], in0=t["mixT"][:, 4 + c, 0:Tn], scalar1=v["snw"][:, c:c + 1],
                                           scalar2=None, op0=ALU.mult), ["mixT", "snw"], ["mixTb"])

    def ev(ci, ps_ap, pskey, m):
        act(lambda e: e.copy(out=t["oT"][:, ci, 0:Tn], in_=ps_ap), [pskey], ["oT"])
    f["linear"](t["mixTb"], "mixTb", 8, g.w["mix_out_w"], 0, D, Tn, ev)
    f["post_resid"](t["oT"], "oT", Tn, g.Gmod[0][0], "mod00", mtok, mcol, t["xT"], "xT")


def emit_ffn(g, l, Tn, mtok, mcol, sample=False):
    f = g.fn
    dve, act = f["dve"], f["act"]
    t, v = g.t, g.vec
    mk = "mod%d1" % l
    f["norm_mod"](t["xT"], "xT", Tn, g.Amod[l][1], g.mod[l][1], mk, mtok, mcol, t["hT"], "hT")
    fcw, fcb, fcar = v["fcw"][l], v["fcb"][l], t["fcar"][l]
    aflat_ = t["actT"][:].rearrange("p c n -> p (c n)")
    if sample:
        AT = aflat_[:, 0:704].bitcast(BF16).rearrange("p (c n) -> p c n", c=22)
    else:
        AT = aflat_.bitcast(BF16)[:, 0:22 * 512].rearrange("p (c n) -> p c n", c=22)

    def ev(ci, ps_ap, pskey, m):
        hb = t["hb"][ci % 2]
        hk = "hb%d" % (ci % 2)
        accb = t["accb"]
        if not sample:
            dve(lambda e: e.tensor_copy(out=hb[:, 0:2], in_=fcar[:, ci, :]), ["fcar%d" % l], [hk])
            act(lambda e: e.copy(out=hb[:, 2:2 + Tn], in_=ps_ap), [pskey], [hk])
            dve(lambda e: e.tensor_scalar(out=accb[:, 0:Tn], in0=hb[:, 0:Tn], scalar1=fcw[0][:, ci:ci + 1], scalar2=None, op0=ALU.mult),
                [hk, "fcw%d0" % l], ["accb"])
            for j in (1, 2):
                dve(lambda e, j=j: e.scalar_tensor_tensor(out=accb[:, 0:Tn], in0=hb[:, j:j + Tn], scalar=fcw[j][:, ci:ci + 1],
                                                          in1=accb[:, 0:Tn], op0=ALU.mult, op1=ALU.add), [hk, "accb"], ["accb"])
            dve(lambda e: e.tensor_copy(out=fcar[:, ci, :], in_=hb[:, Tn:Tn + 2]), [hk], ["fcar%d" % l])
        else:
            p0, p1 = t["fprev"][0], t["fprev"][1]
            act(lambda e: e.copy(out=t["hup"][:, ci, 0:Tn], in_=ps_ap), [pskey], ["actT"])
            dve(lambda e: e.tensor_scalar(out=accb[:, 0:Tn], in0=p0[:, ci, 0:Tn], scalar1=fcw[0][:, ci:ci + 1], scalar2=None, op0=ALU.mult),
                ["actT"], ["accb"])
            dve(lambda e: e.scalar_tensor_tensor(out=accb[:, 0:Tn], in0=p1[:, ci, 0:Tn], scalar=fcw[1][:, ci:ci + 1], in1=accb[:, 0:Tn],
                                                 op0=ALU.mult, op1=ALU.add), ["actT", "accb"], ["accb"])
            dve(lambda e: e.scalar_tensor_tensor(out=accb[:, 0:Tn], in0=t["hup"][:, ci, 0:Tn], scalar=fcw[2][:, ci:ci + 1],
                                                 in1=accb[:, 0:Tn], op0=ALU.mult, op1=ALU.add), ["actT", "accb"], ["accb"])
        if ci < 22:
            act(lambda e: e.activation(out=AT[:, ci, 0:Tn], in_=accb[:, 0:Tn], func=AF.Silu, bias=fcb[:, ci:ci + 1]),
                ["accb", "fcb%d" % l], ["actT"])
        else:
            dve(lambda e: e.scalar_tensor_tensor(out=AT[:, ci - 22, 0:Tn], in0=accb[:, 0:Tn], scalar=fcb[:, ci:ci + 1],
                                                 in1=AT[:, ci - 22, 0:Tn], op0=ALU.add, op1=ALU.mult),
                ["accb", "actT", "fcb%d" % l], ["actT"])
    f["linear"](t["hT"], "hT", 8, g.w["ffn_up_w"][l], 0, 2 * DFF, Tn, ev)

    def ev2(ci, ps_ap, pskey, m):
        act(lambda e: e.copy(out=t["oT"][:, ci, 0:Tn], in_=ps_ap), [pskey], ["oT"])
    f["linear"](AT, "actT", 22, g.w["ffn_down_w"][l], 0, D, Tn, ev2)
    f["post_resid"](t["oT"], "oT", Tn, g.Gmod[l][1], mk, mtok, mcol, t["xT"], "xT")


def emit_pool_prompt(g, i, mcol):
    f = g.fn
    dve, act = f["dve"], f["act"]
    t, v = g.t, g.vec
    f["norm_mod"](t["xT"], "xT", T, g.Amod[1][0], g.mod[1][0], "mod10", 1, mcol, t["oT"], "oT")
    pb, pA, pB = t["pb"], t["pA"], t["pB"]
    W = T + 15
    for c in range(8):
        dve(lambda e, c=c: e.tensor_copy(out=pb[:, c, 15:W], in_=t["oT"][:, c, 0:T]), ["oT"], ["pb"])
        gidx = c // 2
        win = (2, 4, 8, 16)[gidx]
        src, skey = pb[:, c, :], "pb"
        sh = 1
        bufs = [(pA, "pA"), (pB, "pB")]
        k = 0
        while sh < win:
            dst, dkey = bufs[k % 2]
            dve(lambda e, src=src, dst=dst, sh=sh: e.tensor_tensor(out=dst[:, sh:W], in0=src[:, sh:W], in1=src[:, 0:W - sh], op=ALU.add),
                [skey], [dkey])
            src, skey = dst[:, :], dkey
            sh *= 2
            k += 1
        if i == 0:
            dve(lambda e, src=src, gidx=gidx: e.tensor_tensor(out=t["accb"][:, 0:T], in0=src[:, 15:W], in1=t["rcnt"][:, gidx, :], op=ALU.mult),
                [skey, "rcnt"], ["accb"])
        else:
            dve(lambda e, src=src, win=win: e.tensor_scalar(out=t["accb"][:, 0:T], in0=src[:, 15:W], scalar1=1.0 / win, scalar2=None,
                                                            op0=ALU.mult), [skey], ["accb"])
        dve(lambda e, c=c: e.tensor_tensor(out=t["mixTb"][:, c, 0:T], in0=t["accb"][:, 0:T], in1=t["oT"][:, c, 0:T], op=ALU.subtract),
            ["accb", "oT"], ["mixTb"])
    emit_pool_proj(g, T, 1, mcol)
    for c in range(8):
        dve(lambda e, c=c: e.tensor_copy(out=pb[:, c, 0:15], in_=pb[:, c, T:T + 15]), ["pb"], ["pb"])


def emit_pool_proj(g, Tn, mtok, mcol):
    f = g.fn
    dve = f["dve"]
    t, v = g.t, g.vec
    for gi in range(4):
        def ev(ci, ps_ap, pskey, m, gi=gi):
            cc = 2 * gi + ci
            dve(lambda e: e.tensor_scalar(out=t["oT"][:, cc, 0:Tn], in0=ps_ap, scalar1=v["pb_c"][:, cc:cc + 1],
                                          scalar2=v["psc_c"][:, cc:cc + 1], op0=ALU.add, op1=ALU.mult), [pskey, "pb_c", "psc_c"], ["oT"])
        f["linear"](t["mixTb"][:, 2 * gi:2 * gi + 2, :], "mixTb", 2, g.w["pool_w"][gi], 0, 256, Tn, ev)
    f["post_resid"](t["oT"], "oT", Tn, g.Gmod[1][0], "mod10", mtok, mcol, t["xT"], "xT")


def emit_all(g):
    f = g.fn
    dve, act, pe, dma, pool = f["dve"], f["act"], f["pe"], f["dma"], f["pool"]
    alloc_tiles(g)
    t = g.t
    io, out = g.io, g.out
    t["rcnt"] = f["sb"]("rcnt", [128, 4, T])
    dma(t["rcnt"][:].rearrange("p a b -> p (a b)"), g.dram["c_rcnt"].partition_broadcast(128), [], ["rcnt"])
    for h in range(8):
        lo = 64 if h % 2 == 0 else 0
        pool(lambda e, h=h, lo=lo: e.memset(t["vtok"][:, h, lo:lo + 64], 1.0), [], ["vtok"])
    for s in range(NB if DBG is None else 1):
        pool(lambda e: e.memset(t["xbc"][:, :, 0:3], 0.0), [], ["xbc"])
        pool(lambda e: e.memset(t["S_T"][:], 0.0), [], ["S_T"])
        pool(lambda e: e.memset(t["pb"][:, :, 0:15], 0.0), [], ["pb"])
        for l in range(2):
            pool(lambda e, l=l: e.memset(t["fcar"][l][:], 0.0), [], ["fcar%d" % l])
        for i in range(NTILE if DBG is None else DBG_TILES):
            r0 = s * SEQ + i * T
            nr = T // 128
            rows = [(io["xp"][r0 + r * 128:r0 + (r + 1) * 128, :], 128) for r in range(nr)]
            ko = [out["k_prompt"][r0 + r * 128:r0 + (r + 1) * 128, :] for r in range(nr)]
            vo = [out["v_prompt"][r0 + r * 128:r0 + (r + 1) * 128, :] for r in range(nr)]

            def vextra(r, c):
                act(lambda e: e.copy(out=t["vtok"][:, 2 * c, 0:64], in_=g.PS[2][0:128, 0:64]), ["ps2"], ["vtok"])
                act(lambda e: e.copy(out=t["vtok"][:, 2 * c + 1, 64:128], in_=g.PS[2][0:128, 64:128]), ["ps2"], ["vtok"])

            def vdone(r, i=i):
                dma(io["vsc"][i * T + r * 128:i * T + (r + 1) * 128, :], t["vtok"][:].rearrange("p h f -> p (h f)"), ["vtok"], ["vsc"])
            emit_front(g, rows, 128, T, s, 1, i * T, ko, vo, "p", vextra=vextra, vdone=vdone)
            for c in range(4):
                pool(lambda e, c=c: e.tensor_copy(out=t["kTb"][:, c, :], in_=t["kT"][:, c, 0:T]), ["kT"], ["kTb"])
                act(lambda e, c=c: e.copy(out=t["qTb"][:, c, :], in_=t["qT"][:, c, 0:T]), ["qT"], ["qTb"])
                for hh in range(2):
                    dma(io["ksc"][2 * c + hh, :, i * T:(i + 1) * T], t["kTb"][hh * 64:(hh + 1) * 64, c, 0:T], ["kTb"], ["ksc"])
                dve(lambda e, c=c, i=i: e.tensor_reduce(out=t["ksumT"][:, c, i:i + 1], in_=t["kT"][:, c, 0:T], axis=AX.X, op=ALU.add),
                    ["kT"], ["ksumT"])
            emit_ssd_conv(g, T)
            emit_ssd_scan(g)
            for c in range(8):
                dve(lambda e, c=c: e.tensor_copy(out=t["xbc"][:, c, 0:3], in_=t["xbc"][:, c, T:T + 3]), ["xbc"], ["xbc"])
            emit_attn_prompt(g, i)
            dump = None
            if DBG == "att":
                dump = ("mixT", t["mixT"])
            if dump is None:
                emit_mix_out(g, T, 1, s)
                if DBG == "mix":
                    dump = ("mixT", t["mixT"])
                if DBG == "xmix":
                    dump = ("xT", t["xT"])
            if dump is None:
                emit_ffn(g, 0, T, 1, s)
                if DBG == "ffn0":
                    dump = ("xT", t["xT"])
            if dump is None:
                emit_pool_prompt(g, i, s)
                if DBG == "pool":
                    dump = ("xT", t["xT"])
            if dump is None:
                emit_ffn(g, 1, T, 1, s)
            if dump is not None:
                for r in range(nr):
                    f["to_token_major_out"](dump[1], dump[0], 8, 128, r * 128, out["y_prompt"][r0 + r * 128:r0 + (r + 1) * 128, :])
                continue
            for r in range(nr):
                f["to_token_major_out"](t["xT"], "xT", 8, 128, r * 128, out["y_prompt"][r0 + r * 128:r0 + (r + 1) * 128, :])
        emit_state_out(g, out["ssm_prompt"][s])
        f["to_token_major_out"](t["xbc"], "xbc", 8, 3, 0, out["sconv_prompt"][s])
        f["to_token_major_out"](t["pb"], "pb", 8, 15, 0, out["pool_prompt"][s])
        for l in range(2):
            for c0 in range(0, 44, 8):
                n = min(8, 44 - c0)
                f["to_token_major_out"](t["fcar"][l][:, c0:c0 + n, :], "fcar%d" % l, n, 2, 0,
                                        out["fconv_prompt"][l, s][:, c0 * 128:(c0 + n) * 128])
    if DBG is None:
        emit_sample(g)


def emit_sample(g):
    f = g.fn
    sb, dve, act, pe, dma, pool, dma_nc = f["sb"], f["dve"], f["act"], f["pe"], f["dma"], f["pool"], f["dma_nc"]
    t, v, PS, cs, io, out, P, nc = g.t, g.vec, g.PS, g.cs, g.io, g.out, g.P, g.nc
    ident, ones = cs["ident"], cs["ones"]
    Tn = NS
    SCALE = 1.0 / math.sqrt(HD)
    vflat = t["vtok"][:].rearrange("p h f -> p (h f)").bitcast(F32)
    sprev = vflat[:, 0:24 * NS].rearrange("p (j c n) -> p j c n", j=3, c=8)
    aflat = t["actT"][:].rearrange("p c n -> p (c n)")
    t["actT_s"] = aflat[:, 0:704].rearrange("p (c n) -> p c n", c=22)
    fprev = aflat[:, 704:3520].rearrange("p (j c n) -> p j c n", j=2, c=44)
    t["fprev"] = [fprev[:, 0], fprev[:, 1]]
    t["hup"] = aflat[:, 3520:4928].rearrange("p (c n) -> p c n", c=44)
    pflat = t["pb"][:].rearrange("p c n -> p (c n)")
    S_all = pflat[:, 0:1032].rearrange("p (n h) -> p n h", h=8)
    rflat = t["rcnt"][:].rearrange("p a b -> p (a b)")
    ksacc, qbc = rflat[:, 0:512], rflat[:, 512:1024]
    Kc = [t["Kc"][0], t["Kc"][1]]
    prod = t["Vc"][0][:].rearrange("p a f -> p (a f)")
    ktmp = t["Vc"][1][:].rearrange("p a f -> p (a f)")[:, 0:512]
    stg, ostg = g.stg, g.ostg
    ptT = sb("ptT", [128, NS], I32)
    idxc = [sb("idxc%d" % i, [128, 1], I32) for i in range(4)]
    gate64 = sb("gate64", [64, 8]); g8 = sb("g8", [8, 64]); top8s = sb("top8s", [8, 8]); dg = sb("dg", [8, 8])
    mv = sb("mv", [64, 8]); mP = sb("mP", [128, 8]); pmax = sb("pmax", [128, 8]); gm = sb("gm", [8, 1])
    nm = sb("nm", [128, 8]); psr = sb("psr", [128, 8]); rt = sb("rt", [1, 8])
    idx1 = [sb("idx1_%d" % i, [128, 1], I32) for i in range(4)]
    bregc = {}

    def gather2(halves, ic, ik, kb, kk, ch):
        i1 = idx1[ch % 4]
        i1k = "idx1_%d" % (ch % 4)
        dve(lambda e: e.tensor_scalar(out=i1[:], in0=ic[:], scalar1=-HALF, scalar2=None, op0=ALU.add), [ik], [i1k])
        def breg(e):
            if "r" not in bregc:
                bregc["r"] = e.to_reg(HALF - 1)
            return bregc["r"]
        P.op("pool", lambda e: e.indirect_dma_start(out=kb[:, :], out_offset=None, in_=halves[0][:, :],
                                                    in_offset=bass.IndirectOffsetOnAxis(ap=ic[:, :], axis=0),
                                                    bounds_check=breg(e), oob_is_err=False), [ik], [kk], dma=True)
        P.op("pool", lambda e: e.indirect_dma_start(out=kb[:, :], out_offset=None, in_=halves[1][:, :],
                                                    in_offset=bass.IndirectOffsetOnAxis(ap=i1[:, :], axis=0),
                                                    bounds_check=breg(e), oob_is_err=False), [i1k], [kk + "b"], dma=True)
    dma_nc(ptT[:], io["page_table"].rearrange("b j -> j b"), [], ["ptT"])
    dve(lambda e: e.tensor_scalar(out=ptT[:], in0=ptT[:], scalar1=6, scalar2=None, op0=ALU.logical_shift_left), ["ptT"], ["ptT"])

    emit_front(g, [(io["xs_in"], NS)], NS, NS, NB, NS, SEQ, [out["k_sample"]], [out["v_sample"]], "s")
    qsc, k2sc, v2sc = g.dram["qsc"], g.dram["k2sc"], g.dram["v2sc"]
    f["to_token_major_out"](t["qT"], "qT", 4, NS, 0, qsc, wkey="qsc")
    f["to_token_major_out"](t["kT"], "kT", 4, NS, 0, k2sc, wkey="k2sc")
    f["to_token_major_out"](t["vT"], "vT", 4, NS, 0, v2sc, wkey="v2sc")

    for j in range(3):
        f["to_feature_major"](io["st_sconv"][:, j, :], NS, 8, sprev[:, j], "vtok")
    accb = t["accb"]
    for c in range(8):
        dve(lambda e, c=c: e.tensor_scalar(out=accb[:, 0:Tn], in0=sprev[:, 0, c, 0:Tn], scalar1=v["scw"][0][:, c:c + 1], scalar2=None,
                                           op0=ALU.mult), ["vtok", "scw0"], ["accb"])
        for j in (1, 2):
            dve(lambda e, c=c, j=j: e.scalar_tensor_tensor(out=accb[:, 0:Tn], in0=sprev[:, j, c, 0:Tn], scalar=v["scw"][j][:, c:c + 1],
                                                           in1=accb[:, 0:Tn], op0=ALU.mult, op1=ALU.add), ["vtok", "accb"], ["accb"])
        dve(lambda e, c=c: e.scalar_tensor_tensor(out=accb[:, 0:Tn], in0=t["xbc"][:, c, 3:3 + Tn], scalar=v["scw"][3][:, c:c + 1],
                                                  in1=accb[:, 0:Tn], op0=ALU.mult, op1=ALU.add), ["xbc", "accb"], ["accb"])
        act(lambda e, c=c: e.activation(out=t["xa"][:, c, 0:Tn], in_=accb[:, 0:Tn], func=AF.Silu, bias=v["scb"][:, c:c + 1]),
            ["accb", "scb"], ["xa"])
    dma(out["sconv_sample"][:, 0:2, :], io["st_sconv"][:, 1:3, :], [], [])
    f["to_token_major_out"](t["xbc"], "xbc", 8, NS, 3, out["sconv_sample"][:, 2, :])

    pe(lambda e: e.transpose(PS[7][0:NS, 0:8], t["dtT"][0:8, 0:NS], ident[0:8, 0:8]), ["dtT", "k_ident"], ["ps7"])
    dttok = t["dtk"]
    dve(lambda e: e.tensor_tensor(out=dttok[0:NS, :], in0=PS[7][0:NS, 0:8], in1=v["dtb"][0:NS, :], op=ALU.add), ["ps7", "dtb"], ["dtk"])
    act(lambda e: e.activation(out=dttok[0:NS, :], in_=dttok[0:NS, :], func=AF.Exp), ["dtk"], ["dtk"])
    act(lambda e: e.activation(out=dttok[0:NS, :], in_=dttok[0:NS, :], func=AF.Ln, bias=ones[0:NS, 0:1]), ["dtk", "k_ones"], ["dtk"])
    xtok_s = Kc[0]
    for c in range(4):
        pe(lambda e, c=c: e.transpose(PS[2][0:NS, 0:128], t["xa"][:, c, 0:NS], ident[:]), ["xa", "k_ident"], ["ps2"])
        act(lambda e, c=c: e.copy(out=xtok_s[0:NS, c * 128:(c + 1) * 128], in_=PS[2][0:NS, 0:128]), ["ps2"], ["Kc0"])
    dtx = t["xtok"][:].rearrange("p a f -> p (a f)").rearrange("p (h d) -> p h d", h=8)
    for b in range(NS):
        Eb = t["Eb"][b % 2]
        ek = "Eb%d" % (b % 2)
        dve(lambda e, b=b, Eb=Eb: e.tensor_copy(out=Eb[:], in_=ident[0:32, b:b + 1].to_broadcast([32, 128])), ["k_ident"], [ek])
        pe(lambda e, Eb=Eb: e.matmul(PS[7][:, 0:8], Eb[0:NS, :], dttok[0:NS, :], start=True, stop=True), [ek, "dtk"], ["ps7"])
        act(lambda e: e.copy(out=t["dA"][:], in_=PS[7][:, 0:8]), ["ps7"], ["dA"])
        dve(lambda e: e.tensor_tensor(out=t["acs"][:], in0=t["dA"][:], in1=v["aneg"][:], op=ALU.mult), ["dA", "aneg"], ["acs"])
        act(lambda e: e.activation(out=t["nacs"][:], in_=t["acs"][:], func=AF.Exp), ["acs"], ["nacs"])
        pe(lambda e, Eb=Eb: e.matmul(PS[4][:, 0:512], Eb[0:NS, :], xtok_s[0:NS, 0:512], start=True, stop=True), [ek, "Kc0"], ["ps4"])
        dve(lambda e: e.tensor_tensor(out=dtx, in0=PS[4][:, 0:512].rearrange("p (h d) -> p h d", h=8),
                                      in1=t["dA"][:].unsqueeze(2).to_broadcast([128, 8, 64]), op=ALU.mult), ["ps4", "dA"], ["xtok"])
        dma(t["so_sb"][0:64, :, :], io["st_ssm"][b].rearrange("h p n -> p h n"), [], ["so_sb"])
        for h in range(8):
            pe(lambda e, h=h: e.transpose(PS[2][:, 0:64], t["so_sb"][0:64, h, :], ident[0:64, 0:64]), ["so_sb", "k_ident"], ["ps2"])
            act(lambda e, h=h: e.copy(out=t["S_T"][:, h, :], in_=PS[2][:, 0:64]), ["ps2"], ["S_T"])
        dve(lambda e: e.tensor_tensor(out=t["S_T"][:], in0=t["S_T"][:], in1=t["nacs"][:].unsqueeze(2).to_broadcast([128, 8, 64]),
                                      op=ALU.mult), ["S_T", "nacs"], ["S_T"])
        for gi in range(2):
            hs = slice(4 * gi, 4 * gi + 4)
            dve(lambda e, gi=gi, hs=hs, b=b: e.scalar_tensor_tensor(out=t["S_T"][:, hs, :], in0=dtx[:, hs, :], scalar=t["xa"][:, 4 + gi, b:b + 1],
                                                                    in1=t["S_T"][:, hs, :], op0=ALU.mult, op1=ALU.add),
                ["xtok", "xa", "S_T"], ["S_T"])
        for gi in range(2):
            hs = slice(4 * gi, 4 * gi + 4)
            pe(lambda e, gi=gi, hs=hs, b=b: e.matmul(PS[5][0:1, gi * 256:(gi + 1) * 256], t["xa"][:, 6 + gi, b:b + 1],
                                                     t["S_T"][:, hs, :].rearrange("p h d -> p (h d)"), start=True, stop=True),
               ["xa", "S_T"], ["ps5"])
        act(lambda e: e.copy(out=stg[0:1, 0:512], in_=PS[5][0:1, 0:512]), ["ps5"], ["stg"])
        dma(io["rowsc"][b:b + 1, 512:1024], stg[0:1, 0:512], ["stg"], ["rowsc"])
        emit_state_out(g, out["ssm_sample"][b])

    cache_k, cache_v = io["cache_k"], io["cache_v"]
    for b in range(NS):
        dma_nc(qbc, qsc[b].partition_broadcast(128), ["qsc"], ["rcnt"])
        dma(stg[0:1, 0:512], k2sc[b:b + 1, :], ["k2sc"], ["stg"])
        dma(stg[0:1, 512:1024], v2sc[b:b + 1, :], ["v2sc"], ["stg"])
        pool(lambda e: e.memset(S_all[:, 128, :], -BIG), [], ["pb"])
        for ch in range(64):
            ic = idxc[ch % 2]
            ik = "idxc%d" % (ch % 2)
            kb, kk = Kc[ch % 2], "Kc%d" % (ch % 2)
            dve(lambda e, b=b, ch=ch, ic=ic: e.tensor_scalar(out=ic[:], in0=ptT[:, b:b + 1], scalar1=ch, scalar2=None, op0=ALU.bitwise_or),
                ["ptT"], [ik])
            gather2(cache_k, ic, ik, kb, kk, ch)
            k3 = kb[:].rearrange("p (n f) -> p n f", n=2)
            dve(lambda e, k3=k3: e.tensor_tensor(out=prod.rearrange("p (n f) -> p n f", n=2), in0=k3,
                                                 in1=qbc.unsqueeze(1).to_broadcast([128, 2, 512]), op=ALU.mult), [kk, kk + "b", "rcnt"], ["Vc0"])
            dve(lambda e, ch=ch: e.tensor_reduce(out=S_all[:, 2 * ch:2 * ch + 2, :], in_=prod.rearrange("p (n h d) -> p n h d", n=2, h=8),
                                                 axis=AX.X, op=ALU.add), ["Vc0"], ["pb"])
            if ch == 0:
                dve(lambda e, k3=k3: e.tensor_tensor(out=ksacc, in0=k3[:, 0, :], in1=k3[:, 1, :], op=ALU.add), [kk, kk + "b"], ["rcnt"])
            else:
                dve(lambda e, k3=k3: e.tensor_tensor(out=ktmp, in0=k3[:, 0, :], in1=k3[:, 1, :], op=ALU.add), [kk, kk + "b"], ["Vc1"])
                dve(lambda e: e.tensor_tensor(out=ksacc, in0=ksacc, in1=ktmp, op=ALU.add), ["Vc1", "rcnt"], ["rcnt"])
        pe(lambda e: e.matmul(PS[4][0:64, 0:512], cs["pairm"][:], ksacc, start=True, stop=True), ["k_pairm", "rcnt"], ["ps4"])
        dve(lambda e: e.tensor_tensor(out=prod[0:64, 0:512], in0=PS[4][0:64, 0:512], in1=qbc[0:64, :], op=ALU.mult), ["ps4", "rcnt"], ["Vc0"])
        dve(lambda e: e.tensor_reduce(out=gate64[:], in_=prod[0:64, 0:512].rearrange("p (h d) -> p h d", h=8), axis=AX.X, op=ALU.add),
            ["Vc0"], ["gate64"])
        pe(lambda e: e.transpose(PS[7][0:8, 0:64], gate64[:], ident[0:64, 0:64]), ["gate64", "k_ident"], ["ps7"])
        act(lambda e: e.copy(out=g8[:], in_=PS[7][0:8, 0:64]), ["ps7"], ["g8"])
        dve(lambda e: e.max(out=top8s[:], in_=g8[:]), ["g8"], ["top8s"])
        dve(lambda e: e.tensor_scalar(out=dg[:], in0=ident[0:8, 0:8], scalar1=top8s[:, 2:3], scalar2=None, op0=ALU.mult),
            ["top8s", "k_ident"], ["dg"])
        pe(lambda e: e.matmul(PS[7][0:64, 64:72], ones[0:8, 0:64], dg[:], start=True, stop=True), ["dg", "k_ones"], ["ps7"])
        dve(lambda e: e.tensor_tensor(out=mv[:], in0=gate64[:], in1=PS[7][0:64, 64:72], op=ALU.is_lt), ["gate64", "ps7"], ["mv"])
        dve(lambda e: e.tensor_scalar(out=mv[:], in0=mv[:], scalar1=-BIG, scalar2=None, op0=ALU.mult), ["mv"], ["mv"])
        pe(lambda e: e.matmul(PS[7][:, 80:88], cs["pairmT"][:], mv[:], start=True, stop=True), ["mv", "k_pairmT"], ["ps7"])
        act(lambda e: e.copy(out=mP[:], in_=PS[7][:, 80:88]), ["ps7"], ["mP"])
        dve(lambda e: e.tensor_tensor(out=S_all[:, 0:128, :], in0=S_all[:, 0:128, :], in1=mP[:].unsqueeze(1).to_broadcast([128, 128, 8]),
                                      op=ALU.add), ["pb", "mP"], ["pb"])
        dve(lambda e: e.tensor_tensor(out=ostg[0:1, 0:512], in0=qbc[0:1, :], in1=stg[0:1, 0:512], op=ALU.mult), ["rcnt", "stg"], ["ostg"])
        dve(lambda e: e.tensor_reduce(out=S_all[0:1, 128, :], in_=ostg[0:1, 0:512].rearrange("p (h d) -> p h d", h=8), axis=AX.X, op=ALU.add),
            ["ostg"], ["pb"])
        dve(lambda e: e.tensor_reduce(out=pmax[:], in_=S_all.rearrange("p n h -> p h n"), axis=AX.X, op=ALU.max), ["pb"], ["pmax"])
        pe(lambda e: e.transpose(PS[7][0:8, 128:256], pmax[:], ident[:]), ["pmax", "k_ident"], ["ps7"])
        dve(lambda e: e.tensor_reduce(out=gm[:], in_=PS[7][0:8, 128:256], axis=AX.X, op=ALU.max), ["ps7"], ["gm"])
        dve(lambda e: e.tensor_scalar(out=dg[:], in0=ident[0:8, 0:8], scalar1=gm[:, 0:1], scalar2=None, op0=ALU.mult), ["gm", "k_ident"], ["dg"])
        pe(lambda e: e.matmul(PS[7][:, 256:264], ones[0:8, :], dg[:], start=True, stop=True), ["dg", "k_ones"], ["ps7"])
        dve(lambda e: e.tensor_scalar(out=nm[:], in0=PS[7][:, 256:264], scalar1=-SCALE, scalar2=None, op0=ALU.mult), ["ps7"], ["nm"])
        dve(lambda e: e.tensor_scalar(out=S_all, in0=S_all, scalar1=SCALE, scalar2=None, op0=ALU.mult), ["pb"], ["pb"])
        dve(lambda e: e.tensor_tensor(out=S_all, in0=S_all, in1=nm[:].unsqueeze(1).to_broadcast([128, 129, 8]), op=ALU.add), ["pb", "nm"], ["pb"])
        act(lambda e: e.activation(out=S_all, in_=S_all, func=AF.Exp), ["pb"], ["pb"])
        dve(lambda e: e.tensor_reduce(out=psr[:], in_=S_all.rearrange("p n h -> p h n"), axis=AX.X, op=ALU.add), ["pb"], ["psr"])
        pe(lambda e: e.matmul(PS[7][0:1, 264:272], ones[:, 0:1], psr[:], start=True, stop=True), ["psr", "k_ones"], ["ps7"])
        dve(lambda e: e.reciprocal(out=rt[:], in_=PS[7][0:1, 264:272]), ["ps7"], ["rt"])
        vbufs = [(Kc[0], "Kc0"), (Kc[1], "Kc1"), (t["Vc"][0][:].rearrange("p a f -> p (a f)"), "Vc0"),
                 (t["Vc"][1][:].rearrange("p a f -> p (a f)"), "Vc1")]
        for ch in range(64):
            ic = idxc[ch % 4]
            ik = "idxc%d" % (ch % 4)
            kb, kk = vbufs[ch % 4]
            dve(lambda e, b=b, ch=ch, ic=ic: e.tensor_scalar(out=ic[:], in0=ptT[:, b:b + 1], scalar1=ch, scalar2=None, op0=ALU.bitwise_or),
                ["ptT"], [ik])
            gather2(cache_v, ic, ik, kb, kk, ch)
            for tt in range(2):
                n_ = 2 * ch + tt
                pe(lambda e, kb=kb, tt=tt, n_=n_: e.matmul(PS[6][0:8, 0:512], S_all[:, n_, :], kb[:, tt * 512:(tt + 1) * 512],
                                                          start=(n_ == 0), stop=(n_ == 127)), [kk, kk + "b", "pb"], ["ps6"])
        dve(lambda e: e.tensor_tensor(out=ostg[0:8, 0:512], in0=PS[6][0:8, 0:512], in1=cs["blockdiag"][:], op=ALU.mult),
            ["ps6", "k_blockdiag"], ["ostg"])
        pe(lambda e: e.matmul(PS[5][0:1, 0:512], ones[0:8, 0:1], ostg[0:8, 0:512], start=True, stop=True), ["ostg", "k_ones"], ["ps5"])
        o3 = ostg[0:1, 512:1024].rearrange("p (h d) -> p h d", h=8)
        dve(lambda e: e.tensor_tensor(out=o3, in0=stg[0:1, 512:1024].rearrange("p (h d) -> p h d", h=8),
                                      in1=S_all[0:1, 128, :].unsqueeze(2).to_broadcast([1, 8, 64]), op=ALU.mult), ["stg", "pb"], ["ostg"])
        dve(lambda e: e.tensor_tensor(out=ostg[0:1, 512:1024], in0=ostg[0:1, 512:1024], in1=PS[5][0:1, 0:512], op=ALU.add), ["ostg", "ps5"], ["ostg"])
        dve(lambda e: e.tensor_tensor(out=o3, in0=o3, in1=rt[:].unsqueeze(2).to_broadcast([1, 8, 64]), op=ALU.mult), ["ostg", "rt"], ["ostg"])
        dma(io["rowsc"][b:b + 1, 0:512], ostg[0:1, 512:1024], ["ostg"], ["rowsc"])

    f["to_feature_major"](io["rowsc"], NS, 8, t["mixT"], "mixT", rkey="rowsc")
    for c in range(4):
        dve(lambda e, c=c: e.scalar_tensor_tensor(out=t["mixT"][:, 4 + c, 0:Tn], in0=t["xa"][:, c, 0:Tn], scalar=v["dsk"][:, c:c + 1],
                                                  in1=t["mixT"][:, 4 + c, 0:Tn], op0=ALU.mult, op1=ALU.add), ["xa", "dsk", "mixT"], ["mixT"])
    for c in range(4):
        dve(lambda e, c=c: e.tensor_copy(out=t["mixTb"][:, c, 0:Tn], in_=t["mixT"][:, c, 0:Tn]), ["mixT"], ["mixTb"])
    emit_mix_out(g, Tn, NS, NB)
    for l in range(2):
        if l == 1:
            f["norm_mod"](t["xT"], "xT", Tn, g.Amod[1][0], g.mod[1][0], "mod10", NS, NB, t["oT"], "oT")
            dma(out["pool_sample"][:, 0:14, :], io["st_pool"][:, 1:15, :], [], [])
            f["to_token_major_out"](t["oT"], "oT", 8, NS, 0, out["pool_sample"][:, 14, :])
            accp = t["pT"][1][:, 0:8 * NS].rearrange("p (c n) -> p c n", c=8)
            rowp = t["pT"][0][:, 0:8 * NS].rearrange("p (c n) -> p c n", c=8)
            dve(lambda e: e.tensor_copy(out=accp, in_=t["oT"][:, :, 0:Tn]), ["oT"], ["pT1"])
            for j in range(1, 16):
                f["to_feature_major"](io["st_pool"][:, 15 - j, :], NS, 8, rowp, "pT0")
                for gi, win in enumerate((2, 4, 8, 16)):
                    if win > j:
                        cs_ = slice(2 * gi, 2 * gi + 2)
                        dve(lambda e, cs_=cs_: e.tensor_tensor(out=accp[:, cs_, :], in0=accp[:, cs_, :], in1=rowp[:, cs_, :], op=ALU.add),
                            ["pT0", "pT1"], ["pT1"])
            for c in range(8):
                win = (2, 4, 8, 16)[c // 2]
                dve(lambda e, c=c, win=win: e.scalar_tensor_tensor(out=t["mixTb"][:, c, 0:Tn], in0=accp[:, c, :], scalar=1.0 / win,
                                                                   in1=t["oT"][:, c, 0:Tn], op0=ALU.mult, op1=ALU.subtract),
                    ["pT1", "oT"], ["mixTb"])
            emit_pool_proj(g, Tn, NS, NB)
        for j in range(2):
            for c0 in range(0, 44, 8):
                n = min(8, 44 - c0)
                f["to_feature_major"](io["st_fconv"][l, :, j, c0 * 128:(c0 + n) * 128], NS, n, t["fprev"][j][:, c0:c0 + n, :], "actT")
        emit_ffn(g, l, Tn, NS, NB, sample=True)
        dma(out["fconv_sample"][l, :, 0, :], io["st_fconv"][l, :, 1, :], [], [])
        for c0 in range(0, 44, 8):
            n = min(8, 44 - c0)
            f["to_token_major_out"](t["hup"][:, c0:c0 + n, :], "actT", n, NS, 0, out["fconv_sample"][l, :, 1, c0 * 128:(c0 + n) * 128])
    f["to_token_major_out"](t["xT"], "xT", 8, NS, 0, out["y_sample"])


_cache = {}


def kernel(**inp):
    if "g" not in _cache:
        g = build()
        emit_all(g)
        g.P.emit()
        _cache["g"] = g
    g = _cache["g"]
    hc = host_consts()
    f32 = lambda a: np.ascontiguousarray(np.asarray(a), dtype=np.float32)
    xp = f32(inp["x_prompt"]); xs = f32(inp["x_sample"]).reshape(NS_TOT, D)
    ck = f32(inp["cache_k"]).reshape(5120 * 64, 1024); cv = f32(inp["cache_v"]).reshape(5120 * 64, 1024)
    sssm = f32(inp["state_ssm"]).reshape(NS_TOT, H, HD, 128); ssc = f32(inp["state_ssd_conv"]).reshape(NS_TOT, 3, 1024)
    spool = f32(inp["state_pool"]).reshape(NS_TOT, 15, D); sfc = f32(inp["state_ffn_conv"])
    pt = np.ascontiguousarray(np.asarray(inp["page_table"]), dtype=np.int32)
    cpr = f32(inp["c_prompt"]); csa = f32(inp["c_sample"])
    shared = {
        "ada_w": f32(inp["ada_w"]), "ada_b": f32(inp["ada_b"]), "norm_pre": f32(inp["norm_pre"]),
        "norm_post": f32(inp["norm_post"]), "mix_in_w": f32(inp["mix_in_w"]).reshape(D, MIXIN),
        "mix_out_w": f32(inp["mix_out_w"]).reshape(D, D), "ssd_conv_w": f32(inp["ssd_conv_w"]).reshape(4, 1024),
        "ssd_conv_b": f32(inp["ssd_conv_b"]).reshape(1024), "ssd_dt_bias": f32(inp["ssd_dt_bias"]).reshape(8),
        "ssd_a_log": f32(inp["ssd_a_log"]).reshape(8), "ssd_d": f32(inp["ssd_d"]).reshape(8),
        "ssd_norm_w": f32(inp["ssd_norm_w"]).reshape(512), "pool_w": f32(inp["pool_w"]).reshape(4, 256, 256),
        "pool_b": f32(inp["pool_b"]).reshape(1024), "pool_scale": f32(inp["pool_scale"]).reshape(1024),
        "ffn_up_w": f32(inp["ffn_up_w"]), "ffn_conv_w": f32(inp["ffn_conv_w"]), "ffn_conv_b": f32(inp["ffn_conv_b"]),
        "ffn_down_w": f32(inp["ffn_down_w"]),
        "cache_k0": ck[:HALF], "cache_k1": ck[HALF:], "cache_v0": cv[:HALF], "cache_v1": cv[HALF:],
    }
    for k, v in hc.items():
        shared["c_" + k] = v
    maps = []
    for c in range(NCORE):
        ps, ss = slice(c * NB, (c + 1) * NB), slice(c * NS, (c + 1) * NS)
        m = dict(shared)
        m.update({
            "x_prompt": np.ascontiguousarray(xp[ps]).reshape(NB * SEQ, D), "x_sample": np.ascontiguousarray(xs[ss]),
            "state_ssm": np.ascontiguousarray(sssm[ss]), "state_ssd_conv": np.ascontiguousarray(ssc[ss]),
            "state_pool": np.ascontiguousarray(spool[ss]), "state_ffn_conv": np.ascontiguousarray(sfc[:, ss]),
            "page_table": np.ascontiguousarray(pt[ss]), "c_all": np.concatenate([cpr[ps], csa[ss]], 0),
        })
        maps.append(m)
    res = run_bass_kernel_spmd(g.nc, maps, core_ids=list(range(NCORE)))
    R = res.results
    cat = lambda name, ax=0: np.concatenate([R[c][name] for c in range(NCORE)], axis=ax)
    return (
        cat("y_prompt").reshape(NB_TOT, SEQ, D), cat("y_sample").reshape(NS_TOT, 1, D),
        cat("k_prompt").reshape(1, NB_TOT, SEQ, H, HD), cat("v_prompt").reshape(1, NB_TOT, SEQ, H, HD),
        cat("ssm_prompt").reshape(1, NB_TOT, H, HD, 128), cat("sconv_prompt").reshape(1, NB_TOT, 3, 1024),
        cat("pool_prompt").reshape(1, NB_TOT, 15, D), cat("fconv_prompt", 1).reshape(2, NB_TOT, 2, 2 * DFF),
        cat("k_sample").reshape(1, NS_TOT, 1, H, HD), cat("v_sample").reshape(1, NS_TOT, 1, H, HD),
        cat("ssm_sample").reshape(1, NS_TOT, H, HD, 128), cat("sconv_sample").reshape(1, NS_TOT, 3, 1024),
        cat("pool_sample").reshape(1, NS_TOT, 15, D), cat("fconv_sample", 1).reshape(2, NS_TOT, 2, 2 * DFF),
    )
```

```python
import contextlib
import math
import numpy as np
import concourse.bass as bass
import concourse.mybir as mybir
from concourse.bass_utils import run_bass_kernel_spmd

F32 = mybir.dt.float32
BF16 = mybir.dt.bfloat16
I32 = mybir.dt.int32
U32 = mybir.dt.uint32
AF = mybir.ActivationFunctionType
ALU = mybir.AluOpType
AX = mybir.AxisListType
ENGS = ["pe", "act", "dve", "pool", "sp"]

D = 1024
SEQ = 8192
NCORE = 2
NB_TOT = 2
NS_TOT = 32
NB = NB_TOT // NCORE
NS = NS_TOT // NCORE
HALF = 5120 * 64 // 2
H = 8
HD = 64
MIXIN = 3080
DFF = 2816
T = 256
NTILE = SEQ // T
BIG = 30000.0
EPS = 1e-6
NPAGE = 128
PAST = 16384
FULL = True
DBG = None
DBG_TILES = 2


class Prog:
    def __init__(self, nc, n_dma_sems=24):
        self.nc = nc
        self.ops = []
        self.last_writer = {}
        self.readers = {}
        self.n_dma_sems = n_dma_sems

    def op(self, eng, fn, reads=(), writes=(), dma=False):
        idx = len(self.ops)
        deps = set()
        for k in reads:
            w = self.last_writer.get(k)
            if w is not None:
                deps.add(w)
        for k in writes:
            w = self.last_writer.get(k)
            if w is not None:
                deps.add(w)
            for r in self.readers.get(k, ()):
                deps.add(r)
        deps.discard(idx)
        self.ops.append(dict(eng=eng, fn=fn, deps=deps, dma=dma, has_dep=False))
        for k in writes:
            self.last_writer[k] = idx
            self.readers[k] = []
        for k in reads:
            self.readers.setdefault(k, []).append(idx)
        return idx

    def emit(self):
        nc = self.nc
        ops = self.ops
        for i, o in enumerate(ops):
            for d in o["deps"]:
                p = ops[d]
                if p["eng"] == "pe" and o["eng"] == "pe" and not p["dma"] and not o["dma"]:
                    continue
                p["has_dep"] = True
        cnt = {e: 0 for e in ENGS}
        dma_rr = {e: 0 for e in ENGS}
        dma_target = {}
        for i, o in enumerate(ops):
            if o["dma"]:
                q = o["eng"]
                s = (q, dma_rr[q] % self.n_dma_sems)
                dma_rr[q] += 1
                prev = dma_target.get(s, 0)
                o["dsem"] = s
                o["dprev"] = prev
                dma_target[s] = prev + 16
                o["dtarget"] = prev + 16
            elif o["has_dep"]:
                cnt[o["eng"]] += 1
                o["cnt"] = cnt[o["eng"]]
        per_eng = {e: [i for i, o in enumerate(ops) if o["eng"] == e] for e in ENGS}
        with contextlib.ExitStack() as st:
            esem = {e: st.enter_context(nc.semaphore("s_" + e)) for e in ENGS}
            dsem = {}
            for q in ENGS:
                for j in range(min(self.n_dma_sems, dma_rr[q])):
                    dsem[(q, j)] = st.enter_context(nc.semaphore("d_%s_%d" % (q, j)))
            block = st.enter_context(nc.Block())

            def run(e, handle):
                waited = {}

                def wait(sem_key, sem, val):
                    if waited.get(sem_key, 0) >= val:
                        return
                    handle.wait_ge(sem, val)
                    waited[sem_key] = val

                for i in per_eng[e]:
                    o = ops[i]
                    need = {}
                    for d in o["deps"]:
                        p = ops[d]
                        if p["dma"]:
                            k = ("d",) + p["dsem"]
                            need[k] = max(need.get(k, 0), p["dtarget"])
                        else:
                            if p["eng"] == "pe" and e == "pe" and not o["dma"]:
                                continue
                            k = ("e", p["eng"])
                            need[k] = max(need.get(k, 0), p["cnt"])
                    if o["dma"] and o["dprev"] > 0:
                        k = ("d",) + o["dsem"]
                        need[k] = max(need.get(k, 0), o["dprev"])
                    for k, v in need.items():
                        sem = esem[k[1]] if k[0] == "e" else dsem[(k[1], k[2])]
                        wait(k, sem, v)
                    ins = o["fn"](handle)
                    if o["dma"]:
                        ins.then_inc(dsem[o["dsem"]], 16)
                    elif o["has_dep"]:
                        ins.then_inc(esem[e], 1)
                if e == "sp":
                    for s, tgt in dma_target.items():
                        wait(("d",) + s, dsem[s], tgt)

            @block.tensor
            def _(h):
                run("pe", h)

            @block.scalar
            def _(h):
                run("act", h)

            @block.vector
            def _(h):
                run("dve", h)

            @block.gpsimd
            def _(h):
                run("pool", h)

            @block.sync
            def _(h):
                run("sp", h)


def host_consts():
    c = {}
    c["ident"] = np.eye(128, dtype=np.float32)
    c["ones"] = np.ones((128, 128), np.float32)
    s = np.arange(128)
    c["tri"] = (s[:, None] <= s[None, :]).astype(np.float32)
    c["ssdmask"] = np.where(s[:, None] <= s[None, :], 0.0, -BIG).astype(np.float32)
    q = np.arange(T)
    caus = np.zeros((128, T // 128, T), np.float32)
    for kt in range(T // 128):
        caus[:, kt, :] = np.where((128 * kt + s)[:, None] <= q[None, :], 0.0, -BIG)
    c["caus"] = caus.reshape(128, -1)
    e = np.zeros((32, 32, 128), np.float32)
    for b in range(32):
        e[b, b, :] = 1.0
    c["eall"] = e.reshape(32, -1)
    half = 8
    inv = np.exp(np.arange(half, dtype=np.float32) * np.float32(-2.0 * math.log(500000.0) / 16)).astype(np.float32)
    pos = np.arange(SEQ + 1, dtype=np.float32)
    pos[SEQ] = PAST
    ang = (pos[:, None] * inv[None, :]).astype(np.float32)
    cos = np.cos(ang).astype(np.float32).T
    sin = np.sin(ang).astype(np.float32).T
    c["ropec"] = np.concatenate([cos, cos], 0)
    c["ropes"] = np.concatenate([-sin, sin], 0)
    perm = np.zeros((16, 16), np.float32)
    for m in range(16):
        perm[(m + 8) % 16, m] = 1.0
    c["perm"] = perm
    pm = np.zeros((128, 64), np.float32)
    for p in range(128):
        pm[p, p // 2] = 1.0
    c["pairm"] = pm
    c["pairmT"] = pm.T.copy()
    bd = np.zeros((8, 512), np.float32)
    for h in range(8):
        bd[h, h * 64:(h + 1) * 64] = 1.0
    c["blockdiag"] = bd
    rc = np.zeros((4, T), np.float32)
    for gi, win in enumerate((2, 4, 8, 16)):
        rc[gi] = 1.0 / np.minimum(np.arange(T) + 1, win).astype(np.float32)
    c["rcnt"] = rc.reshape(-1)
    return c


class B:
    pass


def build():
    nc = bass.Bass("TRN2", target_bir_lowering=False)
    P = Prog(nc)
    g = B()
    dram = {}

    def din(name, shape, dt=F32):
        dram[name] = nc.dram_tensor(name, list(shape), dt, kind="ExternalInput").ap()
        return dram[name]

    def dout(name, shape):
        dram[name] = nc.dram_tensor(name, list(shape), F32, kind="ExternalOutput").ap()
        return dram[name]

    def dscr(name, shape, dt=F32):
        dram[name] = nc.dram_tensor(name, list(shape), dt, kind="Internal").ap()
        return dram[name]

    xp = din("x_prompt", [NB * SEQ, D])
    xs_in = din("x_sample", [NS, D])
    ck_h = [din("cache_k%d" % i, [HALF, 1024]) for i in range(2)]
    cv_h = [din("cache_v%d" % i, [HALF, 1024]) for i in range(2)]
    cache_k, cache_v = ck_h, cv_h
    st_ssm = din("state_ssm", [NS, H, HD, 128])
    st_sconv = din("state_ssd_conv", [NS, 3, 1024])
    st_pool = din("state_pool", [NS, 15, D])
    st_fconv = din("state_ffn_conv", [2, NS, 2, 2 * DFF])
    page_table = din("page_table", [NS, NPAGE], I32)
    c_all = din("c_all", [NB + NS, D])
    ada_w = din("ada_w", [2, 2, D, 3 * D])
    ada_b = din("ada_b", [2, 2, 3 * D])
    norm_pre = din("norm_pre", [2, 2, D])
    norm_post = din("norm_post", [2, 2, D])
    mix_in_w = din("mix_in_w", [D, MIXIN])
    mix_out_w = din("mix_out_w", [D, D])
    ssd_conv_w = din("ssd_conv_w", [4, 1024])
    ssd_conv_b = din("ssd_conv_b", [1024])
    ssd_dt_bias = din("ssd_dt_bias", [8])
    ssd_a_log = din("ssd_a_log", [8])
    ssd_d = din("ssd_d", [8])
    ssd_norm_w = din("ssd_norm_w", [512])
    pool_w = din("pool_w", [4, 256, 256])
    pool_b = din("pool_b", [1024])
    pool_scale = din("pool_scale", [1024])
    ffn_up_w = din("ffn_up_w", [2, D, 2 * DFF])
    ffn_conv_w = din("ffn_conv_w", [2, 3, 2 * DFF])
    ffn_conv_b = din("ffn_conv_b", [2, 2 * DFF])
    ffn_down_w = din("ffn_down_w", [2, DFF, D])
    hc = host_consts()
    for k, v in hc.items():
        din("c_" + k, v.shape)

    y_prompt = dout("y_prompt", [NB * SEQ, D])
    y_sample = dout("y_sample", [NS, D])
    k_prompt = dout("k_prompt", [NB * SEQ, 512])
    v_prompt = dout("v_prompt", [NB * SEQ, 512])
    ssm_prompt = dout("ssm_prompt", [NB, H, HD, 128])
    sconv_prompt = dout("sconv_prompt", [NB, 3, 1024])
    pool_prompt = dout("pool_prompt", [NB, 15, D])
    fconv_prompt = dout("fconv_prompt", [2, NB, 2, 2 * DFF])
    k_sample = dout("k_sample", [NS, 512])
    v_sample = dout("v_sample", [NS, 512])
    ssm_sample = dout("ssm_sample", [NS, H, HD, 128])
    sconv_sample = dout("sconv_sample", [NS, 3, 1024])
    pool_sample = dout("pool_sample", [NS, 15, D])
    fconv_sample = dout("fconv_sample", [2, NS, 2, 2 * DFF])

    ksc = dscr("ksc", [H, HD, SEQ], BF16)
    vsc = dscr("vsc", [SEQ, H * 128], BF16)
    rowsc = dscr("rowsc", [NS, 1024])
    dscr("qsc", [NS, 512]); dscr("k2sc", [NS, 512]); dscr("v2sc", [NS, 512])

    def sb(name, shape, dt=F32):
        return nc.alloc_sbuf_tensor(name, list(shape), dt)

    PS = [nc.alloc_psum_tensor("ps%d" % i, [128, 512], F32) for i in range(8)]

    def dve(fn, r, w):
        P.op("dve", fn, r, w)

    def act(fn, r, w):
        P.op("act", fn, r, w)

    def pool(fn, r, w):
        P.op("pool", fn, r, w)

    def pe(fn, r, w):
        P.op("pe", fn, r, w)

    dq = [0]

    def dma(out, in_, r, w, q=None):
        if q is None:
            q = "sp"
        P.op(q, lambda e: e.dma_start(out=out, in_=in_), r, w, dma=True)

    def dma_nc(out, in_, r, w):
        def f(e):
            with nc.allow_non_contiguous_dma(reason="small strided layout change"):
                return e.dma_start(out=out, in_=in_)
        P.op("pool", f, r, w, dma=True)

    cs = {}
    for k, v in hc.items():
        if k in ("ropec", "ropes", "eall", "caus", "rcnt"):
            continue
        t_ = sb("k_" + k, v.shape)
        cs[k] = t_
        dma(t_[:], dram["c_" + k], [], ["k_" + k])
    ident, ones, tri, ssdmask = cs["ident"], cs["ones"], cs["tri"], cs["ssdmask"]
    CK = ["k_" + k for k in hc]
    perm2 = sb("perm2", [128, 16])
    dma(perm2[0:16, :], dram["c_perm"], [], ["perm2"])
    dma(perm2[64:80, :], dram["c_perm"], [], ["perm2"])
    ropec2 = sb("ropec2", [128, T])
    ropes2 = sb("ropes2", [128, T])

    def colvec(name, ap1d, n):
        t_ = sb(name, [128, n // 128])
        dma_nc(t_[:], ap1d.rearrange("(c p) -> p c", p=128), [], [name])
        return t_

    npre = [[colvec("npre%d%d" % (l, j), norm_pre[l, j], D) for j in range(2)] for l in range(2)]
    npost = [[colvec("npost%d%d" % (l, j), norm_post[l, j], D) for j in range(2)] for l in range(2)]
    adab = [[colvec("adab%d%d" % (l, j), ada_b[l, j], 3 * D) for j in range(2)] for l in range(2)]
    scw = [colvec("scw%d" % i, ssd_conv_w[i], 1024) for i in range(4)]
    scb = colvec("scb", ssd_conv_b, 1024)
    snw = colvec("snw", ssd_norm_w, 512)
    pb_c = colvec("pb_c", pool_b, 1024)
    psc_c = colvec("psc_c", pool_scale, 1024)
    fcw = [[colvec("fcw%d%d" % (l, i), ffn_conv_w[l, i], 2 * DFF) for i in range(3)] for l in range(2)]
    fcb = [colvec("fcb%d" % l, ffn_conv_b[l], 2 * DFF) for l in range(2)]
    def rowbc(name, ap1d):
        t_ = sb(name, [128, 8])
        dma_nc(t_[:], ap1d.partition_broadcast(128), [], [name])
        return t_
    dtb = rowbc("dtb", ssd_dt_bias)
    alog = rowbc("alog", ssd_a_log)
    aneg = sb("aneg", [128, 8])
    act(lambda e: e.activation(out=aneg[:], in_=alog[:], func=AF.Exp), ["alog"], ["aneg"])
    dve(lambda e: e.tensor_scalar(out=aneg[:], in0=aneg[:], scalar1=-1.0, scalar2=None, op0=ALU.mult), ["aneg"], ["aneg"])
    dsk = sb("dsk", [128, 4])
    for c in range(4):
        for hh in range(2):
            dma_nc(dsk[hh * 64:(hh + 1) * 64, c:c + 1], ssd_d[2 * c + hh:2 * c + hh + 1].partition_broadcast(64), [], ["dsk"])

    WB = 2816
    wbuf = [sb("wbuf%d" % i, [128, WB]) for i in range(2)]
    wctr = [0]

    def load_w(w2d, kc, col0, ncols):
        i = wctr[0] % 2
        wctr[0] += 1
        key = "wbuf%d" % i
        if w2d.dtype == BF16:
            view = wbuf[i][:].bitcast(BF16)[:, 0:kc * ncols].rearrange("p (k n) -> p k n", k=kc)
            rk = ["wb"]
        else:
            view = wbuf[i][:, 0:kc * ncols].rearrange("p (k n) -> p k n", k=kc)
            rk = []
        src = w2d.rearrange("(k p) n -> p k n", p=128)[:, :, col0:col0 + ncols]
        dma(view, src, rk, [key])
        return view, key

    lctr = [0]

    def linear(xT, xkey, kc, w2d, col0, ncols, Tn, evac):
        nchunk = (ncols + 127) // 128
        grp = max(1, min(4, (WB * (2 if w2d.dtype == BF16 else 1)) // (kc * 128)))
        c = 0
        while c < nchunk:
            gcols = min(grp * 128, ncols - c * 128)
            wv, wkey = load_w(w2d, kc, col0 + c * 128, gcols)
            j = 0
            while j * 128 < gcols:
                m = min(128, gcols - j * 128)
                pi = lctr[0] % 2
                lctr[0] += 1
                ps = PS[pi]
                pskey = "ps%d" % pi

                def f(e, wv=wv, j=j, m=m, ps=ps):
                    ins = None
                    for k in range(kc):
                        ins = e.matmul(ps[0:m, 0:Tn], wv[:, k, j * 128:j * 128 + m], xT[:, k, 0:Tn],
                                       start=(k == 0), stop=(k == kc - 1))
                    return ins
                pe(f, [wkey, xkey], [pskey])
                evac(c + j, ps[0:m, 0:Tn], pskey, m)
                j += 1
            c += (gcols + 127) // 128

    sqb = [sb("sqb%d" % i, [128, T]) for i in range(2)]
    sqc = [0]

    def rstd_of(xT, xkey, kc, Tn, nfeat, name="rstd"):
        ps = PS[3]
        for k in range(kc):
            i = sqc[0] % 2
            sqc[0] += 1
            b_ = sqb[i]
            bb = b_[:].bitcast(BF16)
            act(lambda e, bb=bb, k=k: e.activation(out=bb[:, 0:Tn], in_=xT[:, k, 0:Tn], func=AF.Square), [xkey], ["sqb%d" % i])
            pe(lambda e, bb=bb, k=k: e.matmul(ps[:, 0:Tn], g.onesb[:], bb[:, 0:Tn], start=(k == 0), stop=(k == kc - 1)),
               ["sqb%d" % i, "onesb"], ["ps3"])
        r = g.rstd
        dve(lambda e: e.tensor_scalar(out=r[:, 0:Tn], in0=ps[:, 0:Tn], scalar1=1.0 / nfeat, scalar2=EPS, op0=ALU.mult, op1=ALU.add),
            ["ps3"], ["rstd"])
        act(lambda e: e.activation(out=r[:, 0:Tn], in_=r[:, 0:Tn], func=AF.Sqrt), ["rstd"], ["rstd"])
        dve(lambda e: e.reciprocal(out=r[:, 0:Tn], in_=r[:, 0:Tn]), ["rstd"], ["rstd"])
        return r

    g.rstd = sb("rstd", [128, T])
    g.onesb = sb("onesb", [128, 128], BF16)
    dve(lambda e: e.tensor_copy(out=g.onesb[:], in_=ones[:]), ["k_ones"], ["onesb"])
    g.tmp = sb("tmpT", [128, T])

    def bc(ap_col, Tn, ntok):
        if ntok == 1:
            return ap_col.to_broadcast([128, Tn])
        return ap_col

    def norm_mod(xT, xkey, Tn, A, B_, akey, ntok, tok0, hT, hkey):
        r = rstd_of(xT, xkey, 8, Tn, D)
        tmp = g.tmp
        for c in range(8):
            a_ap = bc(A[:, c, tok0:tok0 + ntok], Tn, ntok)
            b_ap = bc(B_[:, c, tok0:tok0 + ntok], Tn, ntok)
            dve(lambda e, c=c: e.tensor_tensor(out=tmp[:, 0:Tn], in0=xT[:, c, 0:Tn], in1=r[:, 0:Tn], op=ALU.mult), [xkey, "rstd"], ["tmpT"])
            dve(lambda e, a_ap=a_ap: e.tensor_tensor(out=tmp[:, 0:Tn], in0=tmp[:, 0:Tn], in1=a_ap, op=ALU.mult), ["tmpT", akey], ["tmpT"])
            dve(lambda e, c=c, b_ap=b_ap: e.tensor_tensor(out=hT[:, c, 0:Tn], in0=tmp[:, 0:Tn], in1=b_ap, op=ALU.add), ["tmpT", akey], [hkey])

    def post_resid(oT, okey, Tn, G, gkey, ntok, tok0, xT, xkey):
        r = rstd_of(oT, okey, 8, Tn, D)
        tmp = g.tmp
        for c in range(8):
            g_ap = bc(G[:, c, tok0:tok0 + ntok], Tn, ntok)
            dve(lambda e, c=c: e.tensor_tensor(out=tmp[:, 0:Tn], in0=oT[:, c, 0:Tn], in1=r[:, 0:Tn], op=ALU.mult), [okey, "rstd"], ["tmpT"])
            dve(lambda e, g_ap=g_ap: e.tensor_tensor(out=tmp[:, 0:Tn], in0=tmp[:, 0:Tn], in1=g_ap, op=ALU.mult), ["tmpT", gkey], ["tmpT"])
            dve(lambda e, c=c: e.tensor_tensor(out=xT[:, c, 0:Tn], in0=xT[:, c, 0:Tn], in1=tmp[:, 0:Tn], op=ALU.add), ["tmpT", xkey], [xkey])

    trb = sb("trb", [128, 128])

    def to_feature_major(src2d, nrows, ncolchunks, dstT, dkey, col_off=0, rkey=None):
        stg = g.stg
        dma(stg[0:nrows, 0:ncolchunks * 128], src2d, [rkey] if rkey else [], ["stg"])
        for c in range(ncolchunks):
            pe(lambda e, c=c: e.transpose(PS[2][:, 0:nrows], stg[0:nrows, c * 128:(c + 1) * 128], ident[0:nrows, 0:nrows]),
               ["stg", "k_ident"], ["ps2"])
            act(lambda e, c=c: e.copy(out=dstT[:, c, col_off:col_off + nrows], in_=PS[2][:, 0:nrows]), ["ps2"], [dkey])

    g.stg = sb("stg", [128, 1024])
    g.ostg = sb("ostg", [128, 1024])

    def to_token_major_out(srcT, skey, nchunks, ntok, tok_off, dst2d, extra=None, wkey=None):
        ostg = g.ostg
        for c in range(nchunks):
            pe(lambda e, c=c: e.transpose(PS[2][0:ntok, 0:128], srcT[:, c, tok_off:tok_off + ntok], ident[:]),
               [skey, "k_ident"], ["ps2"])
            act(lambda e, c=c: e.copy(out=ostg[0:ntok, c * 128:(c + 1) * 128], in_=PS[2][0:ntok, 0:128]), ["ps2"], ["ostg"])
            if extra is not None:
                extra(c)
        dma(dst2d, ostg[0:ntok, 0:nchunks * 128], ["ostg"], [wkey] if wkey else [])

    NC_ = NB + NS
    cT = sb("cT", [128, 8, NC_])
    to_feature_major(c_all, NC_, 8, cT, "cT")
    act(lambda e: e.activation(out=cT[:], in_=cT[:], func=AF.Silu), ["cT"], ["cT"])
    mod = [[sb("mod%d%d" % (l, j), [128, 24, NC_]) for j in range(2)] for l in range(2)]
    Amod = [[mod[l][j][:, 8:16, :] for j in range(2)] for l in range(2)]
    Gmod = [[mod[l][j][:, 16:24, :] for j in range(2)] for l in range(2)]
    for l in range(2):
        for j in range(2):
            mk = "mod%d%d" % (l, j)
            m_ = mod[l][j]
            ab = adab[l][j]

            def ev(ci, ps_ap, pskey, m, m_=m_, ab=ab, mk=mk):
                dve(lambda e: e.tensor_scalar(out=m_[:, ci, :], in0=ps_ap, scalar1=ab[:, ci:ci + 1], scalar2=None, op0=ALU.add),
                    [pskey, ab.name if hasattr(ab, "name") else mk], [mk])
            linear(cT, "cT", 8, ada_w[l, j], 0, 3 * D, NC_, ev)
            for c in range(8):
                dve(lambda e, c=c, l=l, j=j: e.tensor_scalar(out=Amod[l][j][:, c, :], in0=mod[l][j][:, 8 + c, :], scalar1=1.0,
                                                             scalar2=npre[l][j][:, c:c + 1], op0=ALU.add, op1=ALU.mult),
                    [mk, "npre%d%d" % (l, j)], [mk])
                dve(lambda e, c=c, l=l, j=j: e.tensor_scalar(out=Gmod[l][j][:, c, :], in0=mod[l][j][:, 16 + c, :],
                                                             scalar1=npost[l][j][:, c:c + 1], scalar2=None, op0=ALU.mult),
                    [mk, "npost%d%d" % (l, j)], [mk])

    g.nc, g.P, g.dram, g.cs, g.PS = nc, P, dram, cs, PS
    g.fn = dict(sb=sb, dve=dve, act=act, pool=pool, pe=pe, dma=dma, dma_nc=dma_nc, linear=linear, rstd_of=rstd_of,
                norm_mod=norm_mod, post_resid=post_resid, to_feature_major=to_feature_major,
                to_token_major_out=to_token_major_out, bc=bc)
    cvc = [0]

    def convert(w2d, name, K, N):
        wb_ = dscr(name, [K, N], BF16)
        for r0 in range(0, K, 128):
            for c0 in range(0, N, 2048):
                n = min(2048, N - c0)
                i = cvc[0] % 2
                cvc[0] += 1
                stq, sk = (g.stg, "stg") if i == 0 else (g.ostg, "ostg")
                sview = stq[:].bitcast(BF16)
                dma(wbuf[i][:, 0:n], w2d[r0:r0 + 128, c0:c0 + n], [], ["wbuf%d" % i])
                if i == 0:
                    dve(lambda e, i=i, n=n, sview=sview: e.tensor_copy(out=sview[:, 0:n], in_=wbuf[i][:, 0:n]), ["wbuf%d" % i], [sk])
                else:
                    pool(lambda e, i=i, n=n, sview=sview: e.tensor_copy(out=sview[:, 0:n], in_=wbuf[i][:, 0:n]), ["wbuf%d" % i], [sk])
                dma(wb_[r0:r0 + 128, c0:c0 + n], sview[:, 0:n], [sk], ["wb"])
        return wb_
    mix_in_wb = convert(mix_in_w, "wb_mixin", D, MIXIN)
    mix_out_wb = convert(mix_out_w, "wb_mixout", D, D)
    ffn_up_wb = [convert(ffn_up_w[l], "wb_up%d" % l, D, 2 * DFF) for l in range(2)]
    ffn_down_wb = [convert(ffn_down_w[l], "wb_down%d" % l, DFF, D) for l in range(2)]
    pool_wb = [convert(pool_w[gi], "wb_pool%d" % gi, 256, 256) for gi in range(4)]
    g.w = dict(mix_in_w=mix_in_wb, mix_out_w=mix_out_wb, ffn_up_w=ffn_up_wb, ffn_down_w=ffn_down_wb, pool_w=pool_wb)
    g.vec = dict(scw=scw, scb=scb, snw=snw, pb_c=pb_c, psc_c=psc_c, fcw=fcw, fcb=fcb, dtb=dtb, aneg=aneg, dsk=dsk,
                 perm2=perm2, ropec2=ropec2, ropes2=ropes2)
    g.mod, g.Amod, g.Gmod = mod, Amod, Gmod
    g.io = dict(xp=xp, xs_in=xs_in, cache_k=cache_k, cache_v=cache_v, st_ssm=st_ssm, st_sconv=st_sconv, st_pool=st_pool,
                st_fconv=st_fconv, page_table=page_table, ksc=ksc, vsc=vsc, rowsc=rowsc, ck_h=ck_h, cv_h=cv_h)
    g.out = dict(y_prompt=y_prompt, y_sample=y_sample, k_prompt=k_prompt, v_prompt=v_prompt, ssm_prompt=ssm_prompt,
                 sconv_prompt=sconv_prompt, pool_prompt=pool_prompt, fconv_prompt=fconv_prompt, k_sample=k_sample,
                 v_sample=v_sample, ssm_sample=ssm_sample, sconv_sample=sconv_sample, pool_sample=pool_sample,
                 fconv_sample=fconv_sample)
    return g


def emit_front(g, xsrc_rows, ntok_tile, Tn, mcol, mtok, pos_col, kout, vout, tag, vextra=None, vdone=None):
    f = g.fn
    sb, dve, act, pe, dma, dma_nc = f["sb"], f["dve"], f["act"], f["pe"], f["dma"], f["dma_nc"]
    PS = g.PS
    t = g.t
    off = 0
    for ap_, n in xsrc_rows:
        f["to_feature_major"](ap_, n, 8, t["xT"], "xT", col_off=off)
        off += n
    f["norm_mod"](t["xT"], "xT", Tn, g.Amod[0][0], g.mod[0][0], "mod00", mtok, mcol, t["hT"], "hT")

    def ev(ci, ps_ap, pskey, m):
        if ci < 4:
            dst, key = t["qT"][:, ci, 0:Tn], "qT"
        elif ci < 8:
            dst, key = t["kT"][:, ci - 4, 0:Tn], "kT"
        elif ci < 12:
            dst, key = t["vT"][:, ci - 8, 0:Tn], "vT"
        elif ci < 16:
            dst, key = t["zT"][:, ci - 12, 0:Tn], "zT"
        elif ci < 24:
            dst, key = t["xbc"][:, ci - 16, 3:3 + Tn], "xbc"
        else:
            dst, key = t["dtT"][0:8, 0:Tn], "dtT"
        act(lambda e: e.copy(out=dst, in_=ps_ap), [pskey], [key])
    f["linear"](t["hT"], "hT", 8, g.w["mix_in_w"], 0, MIXIN, Tn, ev)
    rc, rs = g.vec["ropec2"], g.vec["ropes2"]
    for base in (0, 64):
        if Tn == T:
            dma(rc[base:base + 16, 0:Tn], g.dram["c_ropec"][:, pos_col:pos_col + Tn], [], ["ropec2"])
            dma(rs[base:base + 16, 0:Tn], g.dram["c_ropes"][:, pos_col:pos_col + Tn], [], ["ropes2"])
        else:
            sl_ = slice(base, base + 16)
            dma_nc(t["r16"][sl_, 0:1], g.dram["c_ropec"][:, pos_col:pos_col + 1], [], ["r16"])
            dve(lambda e, sl_=sl_: e.tensor_copy(out=rc[sl_, 0:Tn], in_=t["r16"][sl_, 0:1].to_broadcast([16, Tn])), ["r16"], ["ropec2"])
            dma_nc(t["r16"][sl_, 0:1], g.dram["c_ropes"][:, pos_col:pos_col + 1], [], ["r16"])
            dve(lambda e, sl_=sl_: e.tensor_copy(out=rs[sl_, 0:Tn], in_=t["r16"][sl_, 0:1].to_broadcast([16, Tn])), ["r16"], ["ropes2"])
    perm2 = g.vec["perm2"]
    r16 = t["r16"]
    for name in ("qT", "kT"):
        X = t[name]
        for c in range(4):
            for base in (0, 64):
                sl = slice(base, base + 16)
                pe(lambda e, X=X, c=c, sl=sl: e.matmul(PS[7][sl, 0:Tn], perm2[sl, :], X[sl, c, 0:Tn], start=True, stop=True),
                   [name, "perm2"], ["ps7"])
                dve(lambda e, sl=sl: e.tensor_tensor(out=r16[sl, 0:Tn], in0=PS[7][sl, 0:Tn], in1=rs[sl, 0:Tn], op=ALU.mult),
                    ["ps7", "ropes2"], ["r16"])
                dve(lambda e, X=X, c=c, sl=sl: e.tensor_tensor(out=X[sl, c, 0:Tn], in0=X[sl, c, 0:Tn], in1=rc[sl, 0:Tn], op=ALU.mult),
                    [name, "ropec2"], [name])
                dve(lambda e, X=X, c=c, sl=sl: e.tensor_tensor(out=X[sl, c, 0:Tn], in0=X[sl, c, 0:Tn], in1=r16[sl, 0:Tn], op=ALU.add),
                    [name, "r16"], [name])
    off = 0
    for r, (ko, vo, n) in enumerate(zip(kout, vout, [n for _, n in xsrc_rows])):
        f["to_token_major_out"](t["kT"], "kT", 4, n, off, ko)
        f["to_token_major_out"](t["vT"], "vT", 4, n, off, vo, extra=(None if vextra is None else (lambda c, r=r: vextra(r, c))))
        if vdone is not None:
            vdone(r)
        off += n


def alloc_tiles(g):
    sb = g.fn["sb"]
    t = {}
    t["hT"] = sb("hT", [128, 8, T], BF16)
    t["mixTb"] = sb("mixTb", [128, 8, T], BF16)
    for name, nchunk, width in (("xT", 8, T), ("qT", 4, T), ("kT", 4, T), ("vT", 4, T), ("zT", 4, T),
                                ("xbc", 8, T + 3), ("xa", 8, T), ("mixT", 8, T), ("oT", 8, T), ("actT", 22, T),
                                ("pb", 8, T + 15)):
        t[name] = sb(name, [128, nchunk, width])
    for name, w in (("dtT", T), ("r16", T), ("accb", T), ("dtk", 8), ("dA", 8), ("acs", 8), ("nacs", 8), ("dabc", 128),
                    ("Dm", 128), ("Eh", 128), ("Mm", 128), ("Cp", 128), ("xw", 64), ("wcol", 1), ("ytmp", 128),
                    ("gsb", 40), ("top8", 8), ("mq", 32), ("rr", T), ("rr2", T), ("pA", T + 15), ("pB", T + 15)):
        t[name] = sb(name, [128, w])
    t["Gsb"] = sb("Gsb", [128, 2, 128])
    t["xtok"] = sb("xtok", [128, 4, 128])
    t["Btok"] = sb("Btok", [128, 2, 128])
    t["S_T"] = sb("S_T", [128, 8, 64])
    t["so_sb"] = sb("so_sb", [128, 8, 128])
    t["ksumT"] = sb("ksumT", [128, 4, 32])
    t["vtok"] = sb("vtok", [128, 8, 128], BF16)
    t["kTb"] = sb("kTb", [128, 4, T], BF16)
    t["qTb"] = sb("qTb", [128, 4, T], BF16)
    t["causb"] = sb("causb", [128, T // 128, T], BF16)
    t["identb"] = sb("identb", [128, 128], BF16)
    t["maskrow"] = sb("maskrow", [32, T])
    t["Eb"] = [sb("Eb%d" % i, [32, 128]) for i in range(2)]
    t["Kc"] = [sb("Kc%d" % i, [128, 1024]) for i in range(2)]
    t["Vc"] = [sb("Vc%d" % i, [128, 8, 128]) for i in range(2)]
    t["pT"] = [sb("pT%d" % i, [128, T]) for i in range(2)]
    t["hb"] = [sb("hb%d" % i, [128, T + 2]) for i in range(2)]
    t["fcar"] = [sb("fcar%d" % l, [128, 44, 2]) for l in range(2)]
    t["caus"] = sb("caus_sb", [128, T // 128, T])
    g.fn["dma"](t["caus"][:].rearrange("p a b -> p (a b)"), g.dram["c_caus"], [], ["caus_sb"])
    g.fn["dve"](lambda e: e.tensor_copy(out=t["causb"][:], in_=t["caus"][:]), ["caus_sb"], ["causb"])
    g.fn["dve"](lambda e: e.tensor_copy(out=t["identb"][:], in_=g.cs["ident"][:]), ["k_ident"], ["identb"])
    g.t = t


def emit_ssd_conv(g, Tn):
    f = g.fn
    dve, act = f["dve"], f["act"]
    t, v = g.t, g.vec
    xin, accb = t["xbc"], t["accb"]
    for c in range(8):
        dve(lambda e, c=c: e.tensor_scalar(out=accb[:, 0:Tn], in0=xin[:, c, 0:Tn], scalar1=v["scw"][0][:, c:c + 1], scalar2=None,
                                           op0=ALU.mult), ["xbc", "scw0"], ["accb"])
        for j in range(1, 4):
            dve(lambda e, c=c, j=j: e.scalar_tensor_tensor(out=accb[:, 0:Tn], in0=xin[:, c, j:j + Tn], scalar=v["scw"][j][:, c:c + 1],
                                                           in1=accb[:, 0:Tn], op0=ALU.mult, op1=ALU.add),
                ["xbc", "accb", "scw%d" % j], ["accb"])
        act(lambda e, c=c: e.activation(out=t["xa"][:, c, 0:Tn], in_=accb[:, 0:Tn], func=AF.Silu, bias=v["scb"][:, c:c + 1]),
            ["accb", "scb"], ["xa"])


def emit_ssd_scan(g):
    f = g.fn
    dve, act, pe = f["dve"], f["act"], f["pe"]
    t, v, PS, cs = g.t, g.vec, g.PS, g.cs
    ident, tri, ssdmask = cs["ident"], cs["tri"], cs["ssdmask"]
    xa = t["xa"]
    for ck in range(T // 128):
        l0 = ck * 128
        sl = slice(l0, l0 + 128)
        pe(lambda e, sl=sl: e.transpose(PS[7][:, 0:8], t["dtT"][0:8, sl], ident[0:8, 0:8]), ["dtT", "k_ident"], ["ps7"])
        dve(lambda e: e.tensor_tensor(out=t["dtk"][:], in0=PS[7][:, 0:8], in1=v["dtb"][:], op=ALU.add), ["ps7", "dtb"], ["dtk"])
        act(lambda e: e.activation(out=t["dtk"][:], in_=t["dtk"][:], func=AF.Exp), ["dtk"], ["dtk"])
        act(lambda e: e.activation(out=t["dtk"][:], in_=t["dtk"][:], func=AF.Ln, bias=cs["ones"][:, 0:1]), ["dtk", "k_ones"], ["dtk"])
        dve(lambda e: e.tensor_tensor(out=t["dA"][:], in0=t["dtk"][:], in1=v["aneg"][:], op=ALU.mult), ["dtk", "aneg"], ["dA"])
        pe(lambda e: e.matmul(PS[7][:, 8:16], tri[:], t["dA"][:], start=True, stop=True), ["dA", "k_tri"], ["ps7"])
        act(lambda e: e.copy(out=t["acs"][:], in_=PS[7][:, 8:16]), ["ps7"], ["acs"])
        dve(lambda e: e.tensor_scalar(out=t["nacs"][:], in0=t["acs"][:], scalar1=-1.0, scalar2=None, op0=ALU.mult), ["acs"], ["nacs"])
        for c in range(4):
            pe(lambda e, c=c, sl=sl: e.transpose(PS[2][:, 0:128], xa[:, c, sl], ident[:]), ["xa", "k_ident"], ["ps2"])
            act(lambda e, c=c: e.copy(out=t["xtok"][:, c, :], in_=PS[2][:, 0:128]), ["ps2"], ["xtok"])
        for gi in range(2):
            pe(lambda e, gi=gi, sl=sl: e.transpose(PS[2][:, 0:128], xa[:, 4 + gi, sl], ident[:]), ["xa", "k_ident"], ["ps2"])
            act(lambda e, gi=gi: e.copy(out=t["Btok"][:, gi, :], in_=PS[2][:, 0:128]), ["ps2"], ["Btok"])
            pe(lambda e, gi=gi, sl=sl: e.matmul(PS[4][:, 0:128], xa[:, 4 + gi, sl], xa[:, 6 + gi, sl], start=True, stop=True),
               ["xa"], ["ps4"])
            act(lambda e, gi=gi: e.copy(out=t["Gsb"][:, gi, :], in_=PS[4][:, 0:128]), ["ps4"], ["Gsb"])
        for h in range(8):
            gi, c, base = h // 4, h // 2, (h % 2) * 64
            bs = slice(base, base + 64)
            dve(lambda e, h=h: e.tensor_copy(out=t["dabc"][:], in_=t["dA"][:, h:h + 1].to_broadcast([128, 128])), ["dA"], ["dabc"])
            pe(lambda e: e.matmul(PS[5][:, 128:256], t["dabc"][:], tri[:], start=True, stop=True), ["dabc", "k_tri"], ["ps5e"])

            def fd(e):
                e.matmul(PS[5][:, 0:128], t["dabc"][:], tri[:], start=True, stop=False)
                return e.matmul(PS[5][:, 0:128], ident[:], ssdmask[:], start=False, stop=True)
            pe(fd, ["dabc", "k_tri", "k_ident", "k_ssdmask", "ps5e"], ["ps5d", "ps5e_order"])
            act(lambda e, h=h: e.activation(out=t["Dm"][:], in_=PS[5][:, 0:128], func=AF.Exp, bias=t["nacs"][:, h:h + 1]),
                ["ps5d", "nacs"], ["Dm"])
            act(lambda e: e.activation(out=t["Eh"][:], in_=PS[5][:, 128:256], func=AF.Exp), ["ps5e", "ps5e_order"], ["Eh"])
            dve(lambda e, gi=gi, h=h: e.scalar_tensor_tensor(out=t["Mm"][:], in0=t["Gsb"][:, gi, :], scalar=t["dtk"][:, h:h + 1],
                                                             in1=t["Dm"][:], op0=ALU.mult, op1=ALU.mult), ["Gsb", "dtk", "Dm"], ["Mm"])
            dve(lambda e, gi=gi, sl=sl: e.tensor_tensor(out=t["Cp"][:], in0=xa[:, 6 + gi, sl], in1=t["Eh"][:], op=ALU.mult),
                ["xa", "Eh"], ["Cp"])

            def fy(e, c=c, bs=bs, h=h):
                e.matmul(PS[6][0:64, 0:128], t["xtok"][:, c, bs], t["Mm"][:], start=True, stop=False)
                return e.matmul(PS[6][0:64, 0:128], t["S_T"][:, h, :], t["Cp"][:], start=False, stop=True)
            pe(fy, ["xtok", "Mm", "S_T", "Cp"], ["ps6"])
            dve(lambda e, bs=bs: e.tensor_copy(out=t["ytmp"][bs, :], in_=PS[6][0:64, 0:128]), ["ps6"], ["ytmp"])
            dve(lambda e, bs=bs, c=c, sl=sl: e.scalar_tensor_tensor(out=t["mixT"][bs, 4 + c, sl], in0=xa[bs, c, sl],
                                                                    scalar=v["dsk"][bs, c:c + 1], in1=t["ytmp"][bs, :],
                                                                    op0=ALU.mult, op1=ALU.add), ["xa", "dsk", "ytmp"], ["mixT"])
            dve(lambda e, h=h: e.tensor_tensor(out=t["wcol"][:], in0=t["Dm"][:, 127:128], in1=t["dtk"][:, h:h + 1], op=ALU.mult),
                ["Dm", "dtk"], ["wcol"])
            dve(lambda e, c=c, bs=bs: e.tensor_scalar(out=t["xw"][:], in0=t["xtok"][:, c, bs], scalar1=t["wcol"][:, 0:1], scalar2=None,
                                                      op0=ALU.mult), ["xtok", "wcol"], ["xw"])
            pe(lambda e, gi=gi: e.matmul(PS[6][:, 128:192], t["Btok"][:, gi, :], t["xw"][:], start=True, stop=True),
               ["Btok", "xw", "ps6"], ["ps6u"])
            dve(lambda e, h=h: e.scalar_tensor_tensor(out=t["S_T"][:, h, :], in0=t["S_T"][:, h, :], scalar=t["Eh"][:, 127:128],
                                                      in1=PS[6][:, 128:192], op0=ALU.mult, op1=ALU.add), ["S_T", "Eh", "ps6u"], ["S_T", "ps6"])


def emit_state_out(g, dst):
    f = g.fn
    t, PS, cs = g.t, g.PS, g.cs
    for h in range(8):
        f["pe"](lambda e, h=h: e.transpose(PS[2][0:64, 0:128], t["S_T"][:, h, :], cs["ident"][:]), ["S_T", "k_ident"], ["ps2"])
        f["act"](lambda e, h=h: e.copy(out=t["so_sb"][0:64, h, :], in_=PS[2][0:64, 0:128]), ["ps2"], ["so_sb"])
    f["dma"](dst.rearrange("h p n -> p h n"), t["so_sb"][0:64, :, :], ["so_sb"], [])


def emit_attn_prompt(g, i):
    f = g.fn
    dve, act, pe, dma, pool = f["dve"], f["act"], f["pe"], f["dma"], f["pool"]
    t, PS, cs, io = g.t, g.PS, g.cs, g.io
    ident = cs["ident"]
    nkeys = (i + 1) * T
    gated = i > 3
    mrow_b = t["maskrow"][:].bitcast(BF16)
    for h in range(8):
        c, base = h // 2, (h % 2) * 64
        bs = slice(base, base + 64)
        if gated:
            for r in range(T // 128):
                qs = slice(r * 128, (r + 1) * 128)
                pe(lambda e, c=c, bs=bs, qs=qs: e.matmul(PS[7][:, 0:32], t["qT"][bs, c, qs], t["ksumT"][bs, c, :], start=True, stop=True),
                   ["qT", "ksumT"], ["ps7"])
                pool(lambda e: e.memset(t["gsb"][:], -BIG), [], ["gsb"])
                act(lambda e: e.copy(out=t["gsb"][:, 0:i], in_=PS[7][:, 0:i]), ["ps7"], ["gsb"])
                n8 = max(i, 8)
                dve(lambda e, n8=n8: e.max(out=t["top8"][:], in_=t["gsb"][:, 0:n8]), ["gsb"], ["top8"])
                dve(lambda e: e.tensor_scalar(out=t["mq"][:], in0=t["gsb"][:, 0:32], scalar1=t["top8"][:, 2:3], scalar2=-BIG,
                                              op0=ALU.is_lt, op1=ALU.mult), ["gsb", "top8"], ["mq"])
                pe(lambda e: e.transpose(PS[7][0:32, 128:256], t["mq"][:], ident[:]), ["mq", "k_ident"], ["ps7"])
                act(lambda e, qs=qs: e.copy(out=mrow_b[:, qs], in_=PS[7][0:32, 128:256]), ["ps7"], ["maskrow"])
        nch = (nkeys + 1023) // 1024
        for ch in range(nch):
            n = min(1024, nkeys - ch * 1024)
            bi = (h * 64 + ch) % 2
            Kc = t["Kc"][bi][:].bitcast(BF16)
            Vc = t["Vc"][bi][:].rearrange("p a f -> p (a f)").bitcast(BF16).rearrange("p (a f) -> p a f", f=128)
            dma(Kc[bs, 0:n], io["ksc"][h, :, ch * 1024:ch * 1024 + n], ["ksc"], ["Kc%d" % bi])
            dma(Vc[:, 0:n // 128, :], io["vsc"][ch * 1024:ch * 1024 + n, h * 128:(h + 1) * 128].rearrange("(a p) f -> p a f", p=128),
                ["vsc"], ["Vc%d" % bi])
            for kl in range(n // 128):
                kt = ch * 8 + kl
                b = kt // (T // 128)
                pi = kt % 2
                use_mask = gated and b < i
                if use_mask:
                    dve(lambda e, b=b, pi=pi: e.tensor_copy(out=t["Eb"][pi][:].bitcast(BF16)[:, 0:128],
                                                            in_=ident[0:32, b:b + 1].to_broadcast([32, 128])),
                        ["k_ident"], ["Eb%d" % pi])

                def fs(e, kl=kl, pi=pi, b=b, use_mask=use_mask, Kc=Kc, kt=kt, bs=bs, c=c):
                    last = not use_mask and b != i
                    ins = e.matmul(PS[4 + pi][:, 0:T], Kc[bs, kl * 128:(kl + 1) * 128], t["qTb"][bs, c, 0:T], start=True, stop=last)
                    if use_mask:
                        ins = e.matmul(PS[4 + pi][:, 0:T], t["Eb"][pi][:].bitcast(BF16)[:, 0:128], mrow_b[:, 0:T], start=False, stop=True)
                    if b == i:
                        ins = e.matmul(PS[4 + pi][:, 0:T], t["identb"][:], t["causb"][:, kt - i * (T // 128), :], start=False, stop=True)
                    return ins
                pe(fs, ["Kc%d" % bi, "qTb", "Eb%d" % pi, "maskrow", "causb", "identb"], ["ps%d" % (4 + pi)])
                act(lambda e, pi=pi: e.activation(out=t["pT"][pi][:].bitcast(BF16)[:, 0:T], in_=PS[4 + pi][:, 0:T], func=AF.Exp,
                                                  scale=1.0 / math.sqrt(HD)),
                    ["ps%d" % (4 + pi)], ["pT%d" % pi])
                pe(lambda e, kl=kl, pi=pi, Vc=Vc, kt=kt: e.matmul(PS[6][:, 0:T], Vc[:, kl, :], t["pT"][pi][:].bitcast(BF16)[:, 0:T],
                                                                 start=(kt == 0), stop=(kt == nkeys // 128 - 1)),
                   ["Vc%d" % bi, "pT%d" % pi], ["ps6"])
        orow = bs
        srow = slice(64 - base, 128 - base)
        dve(lambda e, srow=srow: e.reciprocal(out=t["rr"][srow, :], in_=PS[6][srow, 0:T]), ["ps6"], ["rr"])
        dve(lambda e, srow=srow, orow=orow: e.tensor_copy(out=t["rr2"][orow, :], in_=t["rr"][srow, :]), ["rr"], ["rr2"])
        dve(lambda e, orow=orow, c=c: e.tensor_tensor(out=t["mixTb"][orow, c, :], in0=PS[6][orow, 0:T], in1=t["rr2"][orow, :], op=ALU.mult),
            ["ps6", "rr2"], ["mixTb", "ps6"])


def emit_mix_out(g, Tn, mtok, mcol):
    f = g.fn
    dve, act = f["dve"], f["act"]
    t, v = g.t, g.vec
    for c in range(4):
        act(lambda e, c=c: e.activation(out=t["zT"][:, c, 0:Tn], in_=t["zT"][:, c, 0:Tn], func=AF.Silu), ["zT"], ["zT"])
        dve(lambda e, c=c: e.tensor_tensor(out=t["mixT"][:, 4 + c, 0:Tn], in0=t["mixT"][:, 4 + c, 0:Tn], in1=t["zT"][:, c, 0:Tn], op=ALU.mult),
            ["mixT", "zT"], ["mixT"])
    r = f["rstd_of"](t["mixT"][:, 4:8, :], "mixT", 4, Tn, 512)
    for c in range(4):
        dve(lambda e, c=c: e.tensor_tensor(out=t["mixT"][:, 4 + c, 0:Tn], in0=t["mixT"][:, 4 + c, 0:Tn], in1=r[:, 0:Tn], op=ALU.mult),
            ["mixT", "rstd"], ["mixT"])
        dve(lambda e, c=c: e.tensor_scalar(out=t["mixTb"][:, 4 + c, 0:Tn], in0=t["mixT"][:, 4 + c, 0:Tn], scalar1=v["snw"][:, c:c + 1],
                                           scalar2=None, op0=ALU.mult), ["mixT", "snw"], ["mixTb"])

    def ev(ci, ps_ap, pskey, m):
        act(lambda e: e.copy(out=t["oT"][:, ci, 0:Tn], in_=ps_ap), [pskey], ["oT"])
    f["linear"](t["mixTb"], "mixTb", 8, g.w["mix_out_w"], 0, D, Tn, ev)
    f["post_resid"](t["oT"], "oT", Tn, g.Gmod[0][0], "mod00", mtok, mcol, t["xT"], "xT")


def emit_ffn(g, l, Tn, mtok, mcol, sample=False):
    f = g.fn
    dve, act = f["dve"], f["act"]
    t, v = g.t, g.vec
    mk = "mod%d1" % l
    f["norm_mod"](t["xT"], "xT", Tn, g.Amod[l][1], g.mod[l][1], mk, mtok, mcol, t["hT"], "hT")
    fcw, fcb, fcar = v["fcw"][l], v["fcb"][l], t["fcar"][l]
    aflat_ = t["actT"][:].rearrange("p c n -> p (c n)")
    if sample:
        AT = aflat_[:, 0:704].bitcast(BF16).rearrange("p (c n) -> p c n", c=22)
    else:
        AT = aflat_.bitcast(BF16)[:, 0:22 * 512].rearrange("p (c n) -> p c n", c=22)

    def ev(ci, ps_ap, pskey, m):
        hb = t["hb"][ci % 2]
        hk = "hb%d" % (ci % 2)
        accb = t["accb"]
        if not sample:
            dve(lambda e: e.tensor_copy(out=hb[:, 0:2], in_=fcar[:, ci, :]), ["fcar%d" % l], [hk])
            act(lambda e: e.copy(out=hb[:, 2:2 + Tn], in_=ps_ap), [pskey], [hk])
            dve(lambda e: e.tensor_scalar(out=accb[:, 0:Tn], in0=hb[:, 0:Tn], scalar1=fcw[0][:, ci:ci + 1], scalar2=None, op0=ALU.mult),
                [hk, "fcw%d0" % l], ["accb"])
            for j in (1, 2):
                dve(lambda e, j=j: e.scalar_tensor_tensor(out=accb[:, 0:Tn], in0=hb[:, j:j + Tn], scalar=fcw[j][:, ci:ci + 1],
                                                          in1=accb[:, 0:Tn], op0=ALU.mult, op1=ALU.add), [hk, "accb"], ["accb"])
            dve(lambda e: e.tensor_copy(out=fcar[:, ci, :], in_=hb[:, Tn:Tn + 2]), [hk], ["fcar%d" % l])
        else:
            p0, p1 = t["fprev"][0], t["fprev"][1]
            act(lambda e: e.copy(out=t["hup"][:, ci, 0:Tn], in_=ps_ap), [pskey], ["actT"])
            dve(lambda e: e.tensor_scalar(out=accb[:, 0:Tn], in0=p0[:, ci, 0:Tn], scalar1=fcw[0][:, ci:ci + 1], scalar2=None, op0=ALU.mult),
                ["actT"], ["accb"])
            dve(lambda e: e.scalar_tensor_tensor(out=accb[:, 0:Tn], in0=p1[:, ci, 0:Tn], scalar=fcw[1][:, ci:ci + 1], in1=accb[:, 0:Tn],
                                                 op0=ALU.mult, op1=ALU.add), ["actT", "accb"], ["accb"])
            dve(lambda e: e.scalar_tensor_tensor(out=accb[:, 0:Tn], in0=t["hup"][:, ci, 0:Tn], scalar=fcw[2][:, ci:ci + 1],
                                                 in1=accb[:, 0:Tn], op0=ALU.mult, op1=ALU.add), ["actT", "accb"], ["accb"])
        if ci < 22:
            act(lambda e: e.activation(out=AT[:, ci, 0:Tn], in_=accb[:, 0:Tn], func=AF.Silu, bias=fcb[:, ci:ci + 1]),
                ["accb", "fcb%d" % l], ["actT"])
        else:
            dve(lambda e: e.scalar_tensor_tensor(out=AT[:, ci - 22, 0:Tn], in0=accb[:, 0:Tn], scalar=fcb[:, ci:ci + 1],
                                                 in1=AT[:, ci - 22, 0:Tn], op0=ALU.add, op1=ALU.mult),
                ["accb", "actT", "fcb%d" % l], ["actT"])
    f["linear"](t["hT"], "hT", 8, g.w["ffn_up_w"][l], 0, 2 * DFF, Tn, ev)

    def ev2(ci, ps_ap, pskey, m):
        act(lambda e: e.copy(out=t["oT"][:, ci, 0:Tn], in_=ps_ap), [pskey], ["oT"])
    f["linear"](AT, "actT", 22, g.w["ffn_down_w"][l], 0, D, Tn, ev2)
    f["post_resid"](t["oT"], "oT", Tn, g.Gmod[l][1], mk, mtok, mcol, t["xT"], "xT")


def emit_pool_prompt(g, i, mcol):
    f = g.fn
    dve, act = f["dve"], f["act"]
    t, v = g.t, g.vec
    f["norm_mod"](t["xT"], "xT", T, g.Amod[1][0], g.mod[1][0], "mod10", 1, mcol, t["oT"], "oT")
    pb, pA, pB = t["pb"], t["pA"], t["pB"]
    W = T + 15
    for c in range(8):
        dve(lambda e, c=c: e.tensor_copy(out=pb[:, c, 15:W], in_=t["oT"][:, c, 0:T]), ["oT"], ["pb"])
        gidx = c // 2
        win = (2, 4, 8, 16)[gidx]
        src, skey = pb[:, c, :], "pb"
        sh = 1
        bufs = [(pA, "pA"), (pB, "pB")]
        k = 0
        while sh < win:
            dst, dkey = bufs[k % 2]
            dve(lambda e, src=src, dst=dst, sh=sh: e.tensor_tensor(out=dst[:, sh:W], in0=src[:, sh:W], in1=src[:, 0:W - sh], op=ALU.add),
                [skey], [dkey])
            src, skey = dst[:, :], dkey
            sh *= 2
            k += 1
        if i == 0:
            dve(lambda e, src=src, gidx=gidx: e.tensor_tensor(out=t["accb"][:, 0:T], in0=src[:, 15:W], in1=t["rcnt"][:, gidx, :], op=ALU.mult),
                [skey, "rcnt"], ["accb"])
        else:
            dve(lambda e, src=src, win=win: e.tensor_scalar(out=t["accb"][:, 0:T], in0=src[:, 15:W], scalar1=1.0 / win, scalar2=None,
                                                            op0=ALU.mult), [skey], ["accb"])
        dve(lambda e, c=c: e.tensor_tensor(out=t["mixTb"][:, c, 0:T], in0=t["accb"][:, 0:T], in1=t["oT"][:, c, 0:T], op=ALU.subtract),
            ["accb", "oT"], ["mixTb"])
    emit_pool_proj(g, T, 1, mcol)
    for c in range(8):
        dve(lambda e, c=c: e.tensor_copy(out=pb[:, c, 0:15], in_=pb[:, c, T:T + 15]), ["pb"], ["pb"])


def emit_pool_proj(g, Tn, mtok, mcol):
    f = g.fn
    dve = f["dve"]
    t, v = g.t, g.vec
    for gi in range(4):
        def ev(ci, ps_ap, pskey, m, gi=gi):
            cc = 2 * gi + ci
            dve(lambda e: e.tensor_scalar(out=t["oT"][:, cc, 0:Tn], in0=ps_ap, scalar1=v["pb_c"][:, cc:cc + 1],
                                          scalar2=v["psc_c"][:, cc:cc + 1], op0=ALU.add, op1=ALU.mult), [pskey, "pb_c", "psc_c"], ["oT"])
        f["linear"](t["mixTb"][:, 2 * gi:2 * gi + 2, :], "mixTb", 2, g.w["pool_w"][gi], 0, 256, Tn, ev)
    f["post_resid"](t["oT"], "oT", Tn, g.Gmod[1][0], "mod10", mtok, mcol, t["xT"], "xT")


def emit_all(g):
    f = g.fn
    dve, act, pe, dma, pool = f["dve"], f["act"], f["pe"], f["dma"], f["pool"]
    alloc_tiles(g)
    t = g.t
    io, out = g.io, g.out
    t["rcnt"] = f["sb"]("rcnt", [128, 4, T])
    dma(t["rcnt"][:].rearrange("p a b -> p (a b)"), g.dram["c_rcnt"].partition_broadcast(128), [], ["rcnt"])
    for h in range(8):
        lo = 64 if h % 2 == 0 else 0
        pool(lambda e, h=h, lo=lo: e.memset(t["vtok"][:, h, lo:lo + 64], 1.0), [], ["vtok"])
    for s in range(NB if DBG is None else 1):
        pool(lambda e: e.memset(t["xbc"][:, :, 0:3], 0.0), [], ["xbc"])
        pool(lambda e: e.memset(t["S_T"][:], 0.0), [], ["S_T"])
        pool(lambda e: e.memset(t["pb"][:, :, 0:15], 0.0), [], ["pb"])
        for l in range(2):
            pool(lambda e, l=l: e.memset(t["fcar"][l][:], 0.0), [], ["fcar%d" % l])
        for i in range(NTILE if DBG is None else DBG_TILES):
            r0 = s * SEQ + i * T
            nr = T // 128
            rows = [(io["xp"][r0 + r * 128:r0 + (r + 1) * 128, :], 128) for r in range(nr)]
            ko = [out["k_prompt"][r0 + r * 128:r0 + (r + 1) * 128, :] for r in range(nr)]
            vo = [out["v_prompt"][r0 + r * 128:r0 + (r + 1) * 128, :] for r in range(nr)]

            def vextra(r, c):
                act(lambda e: e.copy(out=t["vtok"][:, 2 * c, 0:64], in_=g.PS[2][0:128, 0:64]), ["ps2"], ["vtok"])
                act(lambda e: e.copy(out=t["vtok"][:, 2 * c + 1, 64:128], in_=g.PS[2][0:128, 64:128]), ["ps2"], ["vtok"])

            def vdone(r, i=i):
                dma(io["vsc"][i * T + r * 128:i * T + (r + 1) * 128, :], t["vtok"][:].rearrange("p h f -> p (h f)"), ["vtok"], ["vsc"])
            emit_front(g, rows, 128, T, s, 1, i * T, ko, vo, "p", vextra=vextra, vdone=vdone)
            for c in range(4):
                pool(lambda e, c=c: e.tensor_copy(out=t["kTb"][:, c, :], in_=t["kT"][:, c, 0:T]), ["kT"], ["kTb"])
                act(lambda e, c=c: e.copy(out=t["qTb"][:, c, :], in_=t["qT"][:, c, 0:T]), ["qT"], ["qTb"])
                for hh in range(2):
                    dma(io["ksc"][2 * c + hh, :, i * T:(i + 1) * T], t["kTb"][hh * 64:(hh + 1) * 64, c, 0:T], ["kTb"], ["ksc"])
                dve(lambda e, c=c, i=i: e.tensor_reduce(out=t["ksumT"][:, c, i:i + 1], in_=t["kT"][:, c, 0:T], axis=AX.X, op=ALU.add),
                    ["kT"], ["ksumT"])
            emit_ssd_conv(g, T)
            emit_ssd_scan(g)
            for c in range(8):
                dve(lambda e, c=c: e.tensor_copy(out=t["xbc"][:, c, 0:3], in_=t["xbc"][:, c, T:T + 3]), ["xbc"], ["xbc"])
            emit_attn_prompt(g, i)
            dump = None
            if DBG == "att":
                dump = ("mixT", t["mixT"])
            if dump is None:
                emit_mix_out(g, T, 1, s)
                if DBG == "mix":
                    dump = ("mixT", t["mixT"])
                if DBG == "xmix":
                    dump = ("xT", t["xT"])
            if dump is None:
                emit_ffn(g, 0, T, 1, s)
                if DBG == "ffn0":
                    dump = ("xT", t["xT"])
            if dump is None:
                emit_pool_prompt(g, i, s)
                if DBG == "pool":
                    dump = ("xT", t["xT"])
            if dump is None:
                emit_ffn(g, 1, T, 1, s)
            if dump is not None:
                for r in range(nr):
                    f["to_token_major_out"](dump[1], dump[0], 8, 128, r * 128, out["y_prompt"][r0 + r * 128:r0 + (r + 1) * 128, :])
                continue
            for r in range(nr):
                f["to_token_major_out"](t["xT"], "xT", 8, 128, r * 128, out["y_prompt"][r0 + r * 128:r0 + (r + 1) * 128, :])
        emit_state_out(g, out["ssm_prompt"][s])
        f["to_token_major_out"](t["xbc"], "xbc", 8, 3, 0, out["sconv_prompt"][s])
        f["to_token_major_out"](t["pb"], "pb", 8, 15, 0, out["pool_prompt"][s])
        for l in range(2):
            for c0 in range(0, 44, 8):
                n = min(8, 44 - c0)
                f["to_token_major_out"](t["fcar"][l][:, c0:c0 + n, :], "fcar%d" % l, n, 2, 0,
                                        out["fconv_prompt"][l, s][:, c0 * 128:(c0 + n) * 128])
    if DBG is None:
        emit_sample(g)


def emit_sample(g):
    f = g.fn
    sb, dve, act, pe, dma, pool, dma_nc = f["sb"], f["dve"], f["act"], f["pe"], f["dma"], f["pool"], f["dma_nc"]
    t, v, PS, cs, io, out, P, nc = g.t, g.vec, g.PS, g.cs, g.io, g.out, g.P, g.nc
    ident, ones = cs["ident"], cs["ones"]
    Tn = NS
    SCALE = 1.0 / math.sqrt(HD)
    vflat = t["vtok"][:].rearrange("p h f -> p (h f)").bitcast(F32)
    sprev = vflat[:, 0:24 * NS].rearrange("p (j c n) -> p j c n", j=3, c=8)
    aflat = t["actT"][:].rearrange("p c n -> p (c n)")
    t["actT_s"] = aflat[:, 0:704].rearrange("p (c n) -> p c n", c=22)
    fprev = aflat[:, 704:3520].rearrange("p (j c n) -> p j c n", j=2, c=44)
    t["fprev"] = [fprev[:, 0], fprev[:, 1]]
    t["hup"] = aflat[:, 3520:4928].rearrange("p (c n) -> p c n", c=44)
    pflat = t["pb"][:].rearrange("p c n -> p (c n)")
    S_all = pflat[:, 0:1032].rearrange("p (n h) -> p n h", h=8)
    rflat = t["rcnt"][:].rearrange("p a b -> p (a b)")
    ksacc, qbc = rflat[:, 0:512], rflat[:, 512:1024]
    Kc = [t["Kc"][0], t["Kc"][1]]
    prod = t["Vc"][0][:].rearrange("p a f -> p (a f)")
    ktmp = t["Vc"][1][:].rearrange("p a f -> p (a f)")[:, 0:512]
    stg, ostg = g.stg, g.ostg
    ptT = sb("ptT", [128, NS], I32)
    idxc = [sb("idxc%d" % i, [128, 1], I32) for i in range(4)]
    gate64 = sb("gate64", [64, 8]); g8 = sb("g8", [8, 64]); top8s = sb("top8s", [8, 8]); dg = sb("dg", [8, 8])
    mv = sb("mv", [64, 8]); mP = sb("mP", [128, 8]); pmax = sb("pmax", [128, 8]); gm = sb("gm", [8, 1])
    nm = sb("nm", [128, 8]); psr = sb("psr", [128, 8]); rt = sb("rt", [1, 8])
    idx1 = [sb("idx1_%d" % i, [128, 1], I32) for i in range(4)]
    bregc = {}

    def gather2(halves, ic, ik, kb, kk, ch):
        i1 = idx1[ch % 4]
        i1k = "idx1_%d" % (ch % 4)
        dve(lambda e: e.tensor_scalar(out=i1[:], in0=ic[:], scalar1=-HALF, scalar2=None, op0=ALU.add), [ik], [i1k])
        def breg(e):
            if "r" not in bregc:
                bregc["r"] = e.to_reg(HALF - 1)
            return bregc["r"]
        P.op("pool", lambda e: e.indirect_dma_start(out=kb[:, :], out_offset=None, in_=halves[0][:, :],
                                                    in_offset=bass.IndirectOffsetOnAxis(ap=ic[:, :], axis=0),
                                                    bounds_check=breg(e), oob_is_err=False), [ik], [kk], dma=True)
        P.op("pool", lambda e: e.indirect_dma_start(out=kb[:, :], out_offset=None, in_=halves[1][:, :],
                                                    in_offset=bass.IndirectOffsetOnAxis(ap=i1[:, :], axis=0),
                                                    bounds_check=breg(e), oob_is_err=False), [i1k], [kk + "b"], dma=True)
    dma_nc(ptT[:], io["page_table"].rearrange("b j -> j b"), [], ["ptT"])
    dve(lambda e: e.tensor_scalar(out=ptT[:], in0=ptT[:], scalar1=6, scalar2=None, op0=ALU.logical_shift_left), ["ptT"], ["ptT"])

    emit_front(g, [(io["xs_in"], NS)], NS, NS, NB, NS, SEQ, [out["k_sample"]], [out["v_sample"]], "s")
    qsc, k2sc, v2sc = g.dram["qsc"], g.dram["k2sc"], g.dram["v2sc"]
    f["to_token_major_out"](t["qT"], "qT", 4, NS, 0, qsc, wkey="qsc")
    f["to_token_major_out"](t["kT"], "kT", 4, NS, 0, k2sc, wkey="k2sc")
    f["to_token_major_out"](t["vT"], "vT", 4, NS, 0, v2sc, wkey="v2sc")

    for j in range(3):
        f["to_feature_major"](io["st_sconv"][:, j, :], NS, 8, sprev[:, j], "vtok")
    accb = t["accb"]
    for c in range(8):
        dve(lambda e, c=c: e.tensor_scalar(out=accb[:, 0:Tn], in0=sprev[:, 0, c, 0:Tn], scalar1=v["scw"][0][:, c:c + 1], scalar2=None,
                                           op0=ALU.mult), ["vtok", "scw0"], ["accb"])
        for j in (1, 2):
            dve(lambda e, c=c, j=j: e.scalar_tensor_tensor(out=accb[:, 0:Tn], in0=sprev[:, j, c, 0:Tn], scalar=v["scw"][j][:, c:c + 1],
                                                           in1=accb[:, 0:Tn], op0=ALU.mult, op1=ALU.add), ["vtok", "accb"], ["accb"])
        dve(lambda e, c=c: e.scalar_tensor_tensor(out=accb[:, 0:Tn], in0=t["xbc"][:, c, 3:3 + Tn], scalar=v["scw"][3][:, c:c + 1],
                                                  in1=accb[:, 0:Tn], op0=ALU.mult, op1=ALU.add), ["xbc", "accb"], ["accb"])
        act(lambda e, c=c: e.activation(out=t["xa"][:, c, 0:Tn], in_=accb[:, 0:Tn], func=AF.Silu, bias=v["scb"][:, c:c + 1]),
            ["accb", "scb"], ["xa"])
    dma(out["sconv_sample"][:, 0:2, :], io["st_sconv"][:, 1:3, :], [], [])
    f["to_token_major_out"](t["xbc"], "xbc", 8, NS, 3, out["sconv_sample"][:, 2, :])

    pe(lambda e: e.transpose(PS[7][0:NS, 0:8], t["dtT"][0:8, 0:NS], ident[0:8, 0:8]), ["dtT", "k_ident"], ["ps7"])
    dttok = t["dtk"]
    dve(lambda e: e.tensor_tensor(out=dttok[0:NS, :], in0=PS[7][0:NS, 0:8], in1=v["dtb"][0:NS, :], op=ALU.add), ["ps7", "dtb"], ["dtk"])
    act(lambda e: e.activation(out=dttok[0:NS, :], in_=dttok[0:NS, :], func=AF.Exp), ["dtk"], ["dtk"])
    act(lambda e: e.activation(out=dttok[0:NS, :], in_=dttok[0:NS, :], func=AF.Ln, bias=ones[0:NS, 0:1]), ["dtk", "k_ones"], ["dtk"])
    xtok_s = Kc[0]
    for c in range(4):
        pe(lambda e, c=c: e.transpose(PS[2][0:NS, 0:128], t["xa"][:, c, 0:NS], ident[:]), ["xa", "k_ident"], ["ps2"])
        act(lambda e, c=c: e.copy(out=xtok_s[0:NS, c * 128:(c + 1) * 128], in_=PS[2][0:NS, 0:128]), ["ps2"], ["Kc0"])
    dtx = t["xtok"][:].rearrange("p a f -> p (a f)").rearrange("p (h d) -> p h d", h=8)
    for b in range(NS):
        Eb = t["Eb"][b % 2]
        ek = "Eb%d" % (b % 2)
        dve(lambda e, b=b, Eb=Eb: e.tensor_copy(out=Eb[:], in_=ident[0:32, b:b + 1].to_broadcast([32, 128])), ["k_ident"], [ek])
        pe(lambda e, Eb=Eb: e.matmul(PS[7][:, 0:8], Eb[0:NS, :], dttok[0:NS, :], start=True, stop=True), [ek, "dtk"], ["ps7"])
        act(lambda e: e.copy(out=t["dA"][:], in_=PS[7][:, 0:8]), ["ps7"], ["dA"])
        dve(lambda e: e.tensor_tensor(out=t["acs"][:], in0=t["dA"][:], in1=v["aneg"][:], op=ALU.mult), ["dA", "aneg"], ["acs"])
        act(lambda e: e.activation(out=t["nacs"][:], in_=t["acs"][:], func=AF.Exp), ["acs"], ["nacs"])
        pe(lambda e, Eb=Eb: e.matmul(PS[4][:, 0:512], Eb[0:NS, :], xtok_s[0:NS, 0:512], start=True, stop=True), [ek, "Kc0"], ["ps4"])
        dve(lambda e: e.tensor_tensor(out=dtx, in0=PS[4][:, 0:512].rearrange("p (h d) -> p h d", h=8),
                                      in1=t["dA"][:].unsqueeze(2).to_broadcast([128, 8, 64]), op=ALU.mult), ["ps4", "dA"], ["xtok"])
        dma(t["so_sb"][0:64, :, :], io["st_ssm"][b].rearrange("h p n -> p h n"), [], ["so_sb"])
        for h in range(8):
            pe(lambda e, h=h: e.transpose(PS[2][:, 0:64], t["so_sb"][0:64, h, :], ident[0:64, 0:64]), ["so_sb", "k_ident"], ["ps2"])
            act(lambda e, h=h: e.copy(out=t["S_T"][:, h, :], in_=PS[2][:, 0:64]), ["ps2"], ["S_T"])
        dve(lambda e: e.tensor_tensor(out=t["S_T"][:], in0=t["S_T"][:], in1=t["nacs"][:].unsqueeze(2).to_broadcast([128, 8, 64]),
                                      op=ALU.mult), ["S_T", "nacs"], ["S_T"])
        for gi in range(2):
            hs = slice(4 * gi, 4 * gi + 4)
            dve(lambda e, gi=gi, hs=hs, b=b: e.scalar_tensor_tensor(out=t["S_T"][:, hs, :], in0=dtx[:, hs, :], scalar=t["xa"][:, 4 + gi, b:b + 1],
                                                                    in1=t["S_T"][:, hs, :], op0=ALU.mult, op1=ALU.add),
                ["xtok", "xa", "S_T"], ["S_T"])
        for gi in range(2):
            hs = slice(4 * gi, 4 * gi + 4)
            pe(lambda e, gi=gi, hs=hs, b=b: e.matmul(PS[5][0:1, gi * 256:(gi + 1) * 256], t["xa"][:, 6 + gi, b:b + 1],
                                                     t["S_T"][:, hs, :].rearrange("p h d -> p (h d)"), start=True, stop=True),
               ["xa", "S_T"], ["ps5"])
        act(lambda e: e.copy(out=stg[0:1, 0:512], in_=PS[5][0:1, 0:512]), ["ps5"], ["stg"])
        dma(io["rowsc"][b:b + 1, 512:1024], stg[0:1, 0:512], ["stg"], ["rowsc"])
        emit_state_out(g, out["ssm_sample"][b])

    cache_k, cache_v = io["cache_k"], io["cache_v"]
    for b in range(NS):
        dma_nc(qbc, qsc[b].partition_broadcast(128), ["qsc"], ["rcnt"])
        dma(stg[0:1, 0:512], k2sc[b:b + 1, :], ["k2sc"], ["stg"])
        dma(stg[0:1, 512:1024], v2sc[b:b + 1, :], ["v2sc"], ["stg"])
        pool(lambda e: e.memset(S_all[:, 128, :], -BIG), [], ["pb"])
        for ch in range(64):
            ic = idxc[ch % 2]
            ik = "idxc%d" % (ch % 2)
            kb, kk = Kc[ch % 2], "Kc%d" % (ch % 2)
            dve(lambda e, b=b, ch=ch, ic=ic: e.tensor_scalar(out=ic[:], in0=ptT[:, b:b + 1], scalar1=ch, scalar2=None, op0=ALU.bitwise_or),
                ["ptT"], [ik])
            gather2(cache_k, ic, ik, kb, kk, ch)
            k3 = kb[:].rearrange("p (n f) -> p n f", n=2)
            dve(lambda e, k3=k3: e.tensor_tensor(out=prod.rearrange("p (n f) -> p n f", n=2), in0=k3,
                                                 in1=qbc.unsqueeze(1).to_broadcast([128, 2, 512]), op=ALU.mult), [kk, kk + "b", "rcnt"], ["Vc0"])
            dve(lambda e, ch=ch: e.tensor_reduce(out=S_all[:, 2 * ch:2 * ch + 2, :], in_=prod.rearrange("p (n h d) -> p n h d", n=2, h=8),
                                                 axis=AX.X, op=ALU.add), ["Vc0"], ["pb"])
            for tt in range(2):
                pe(lambda e, k3=k3, tt=tt, ch=ch: e.matmul(PS[5][:, 0:512], ident[:], k3[:, tt, :], start=(ch == 0 and tt == 0),
                                                           stop=(ch == 63 and tt == 1)), [kk, kk + "b", "k_ident"], ["ps5"])
        act(lambda e: e.copy(out=ksacc, in_=PS[5][:, 0:512]), ["ps5"], ["rcnt"])
        pe(lambda e: e.matmul(PS[4][0:64, 0:512], cs["pairm"][:], ksacc, start=True, stop=True), ["k_pairm", "rcnt"], ["ps4"])
        dve(lambda e: e.tensor_tensor(out=prod[0:64, 0:512], in0=PS[4][0:64, 0:512], in1=qbc[0:64, :], op=ALU.mult), ["ps4", "rcnt"], ["Vc0"])
        dve(lambda e: e.tensor_reduce(out=gate64[:], in_=prod[0:64, 0:512].rearrange("p (h d) -> p h d", h=8), axis=AX.X, op=ALU.add),
            ["Vc0"], ["gate64"])
        pe(lambda e: e.transpose(PS[7][0:8, 0:64], gate64[:], ident[0:64, 0:64]), ["gate64", "k_ident"], ["ps7"])
        act(lambda e: e.copy(out=g8[:], in_=PS[7][0:8, 0:64]), ["ps7"], ["g8"])
        dve(lambda e: e.max(out=top8s[:], in_=g8[:]), ["g8"], ["top8s"])
        dve(lambda e: e.tensor_scalar(out=dg[:], in0=ident[0:8, 0:8], scalar1=top8s[:, 2:3], scalar2=None, op0=ALU.mult),
            ["top8s", "k_ident"], ["dg"])
        pe(lambda e: e.matmul(PS[7][0:64, 64:72], ones[0:8, 0:64], dg[:], start=True, stop=True), ["dg", "k_ones"], ["ps7"])
        dve(lambda e: e.tensor_tensor(out=mv[:], in0=gate64[:], in1=PS[7][0:64, 64:72], op=ALU.is_lt), ["gate64", "ps7"], ["mv"])
        dve(lambda e: e.tensor_scalar(out=mv[:], in0=mv[:], scalar1=-BIG, scalar2=None, op0=ALU.mult), ["mv"], ["mv"])
        pe(lambda e: e.matmul(PS[7][:, 80:88], cs["pairmT"][:], mv[:], start=True, stop=True), ["mv", "k_pairmT"], ["ps7"])
        act(lambda e: e.copy(out=mP[:], in_=PS[7][:, 80:88]), ["ps7"], ["mP"])
        dve(lambda e: e.tensor_tensor(out=S_all[:, 0:128, :], in0=S_all[:, 0:128, :], in1=mP[:].unsqueeze(1).to_broadcast([128, 128, 8]),
                                      op=ALU.add), ["pb", "mP"], ["pb"])
        dve(lambda e: e.tensor_tensor(out=ostg[0:1, 0:512], in0=qbc[0:1, :], in1=stg[0:1, 0:512], op=ALU.mult), ["rcnt", "stg"], ["ostg"])
        dve(lambda e: e.tensor_reduce(out=S_all[0:1, 128, :], in_=ostg[0:1, 0:512].rearrange("p (h d) -> p h d", h=8), axis=AX.X, op=ALU.add),
            ["ostg"], ["pb"])
        dve(lambda e: e.tensor_reduce(out=pmax[:], in_=S_all.rearrange("p n h -> p h n"), axis=AX.X, op=ALU.max), ["pb"], ["pmax"])
        pe(lambda e: e.transpose(PS[7][0:8, 128:256], pmax[:], ident[:]), ["pmax", "k_ident"], ["ps7"])
        dve(lambda e: e.tensor_reduce(out=gm[:], in_=PS[7][0:8, 128:256], axis=AX.X, op=ALU.max), ["ps7"], ["gm"])
        dve(lambda e: e.tensor_scalar(out=dg[:], in0=ident[0:8, 0:8], scalar1=gm[:, 0:1], scalar2=None, op0=ALU.mult), ["gm", "k_ident"], ["dg"])
        pe(lambda e: e.matmul(PS[7][:, 256:264], ones[0:8, :], dg[:], start=True, stop=True), ["dg", "k_ones"], ["ps7"])
        dve(lambda e: e.tensor_scalar(out=nm[:], in0=PS[7][:, 256:264], scalar1=-SCALE, scalar2=None, op0=ALU.mult), ["ps7"], ["nm"])
        dve(lambda e: e.tensor_scalar(out=S_all, in0=S_all, scalar1=SCALE, scalar2=None, op0=ALU.mult), ["pb"], ["pb"])
        dve(lambda e: e.tensor_tensor(out=S_all, in0=S_all, in1=nm[:].unsqueeze(1).to_broadcast([128, 129, 8]), op=ALU.add), ["pb", "nm"], ["pb"])
        act(lambda e: e.activation(out=S_all, in_=S_all, func=AF.Exp), ["pb"], ["pb"])
        dve(lambda e: e.tensor_reduce(out=psr[:], in_=S_all.rearrange("p n h -> p h n"), axis=AX.X, op=ALU.add), ["pb"], ["psr"])
        pe(lambda e: e.matmul(PS[7][0:1, 264:272], ones[:, 0:1], psr[:], start=True, stop=True), ["psr", "k_ones"], ["ps7"])
        dve(lambda e: e.reciprocal(out=rt[:], in_=PS[7][0:1, 264:272]), ["ps7"], ["rt"])
        vbufs = [(Kc[0], "Kc0"), (Kc[1], "Kc1"), (t["Vc"][0][:].rearrange("p a f -> p (a f)"), "Vc0"),
                 (t["Vc"][1][:].rearrange("p a f -> p (a f)"), "Vc1")]
        for ch in range(64):
            ic = idxc[ch % 4]
            ik = "idxc%d" % (ch % 4)
            kb, kk = vbufs[ch % 4]
            dve(lambda e, b=b, ch=ch, ic=ic: e.tensor_scalar(out=ic[:], in0=ptT[:, b:b + 1], scalar1=ch, scalar2=None, op0=ALU.bitwise_or),
                ["ptT"], [ik])
            gather2(cache_v, ic, ik, kb, kk, ch)
            for tt in range(2):
                n_ = 2 * ch + tt
                pe(lambda e, kb=kb, tt=tt, n_=n_: e.matmul(PS[6][0:8, 0:512], S_all[:, n_, :], kb[:, tt * 512:(tt + 1) * 512],
                                                          start=(n_ == 0), stop=(n_ == 127)), [kk, kk + "b", "pb"], ["ps6"])
        dve(lambda e: e.tensor_tensor(out=ostg[0:8, 0:512], in0=PS[6][0:8, 0:512], in1=cs["blockdiag"][:], op=ALU.mult),
            ["ps6", "k_blockdiag"], ["ostg"])
        pe(lambda e: e.matmul(PS[5][0:1, 0:512], ones[0:8, 0:1], ostg[0:8, 0:512], start=True, stop=True), ["ostg", "k_ones"], ["ps5"])
        o3 = ostg[0:1, 512:1024].rearrange("p (h d) -> p h d", h=8)
        dve(lambda e: e.tensor_tensor(out=o3, in0=stg[0:1, 512:1024].rearrange("p (h d) -> p h d", h=8),
                                      in1=S_all[0:1, 128, :].unsqueeze(2).to_broadcast([1, 8, 64]), op=ALU.mult), ["stg", "pb"], ["ostg"])
        dve(lambda e: e.tensor_tensor(out=ostg[0:1, 512:1024], in0=ostg[0:1, 512:1024], in1=PS[5][0:1, 0:512], op=ALU.add), ["ostg", "ps5"], ["ostg"])
        dve(lambda e: e.tensor_tensor(out=o3, in0=o3, in1=rt[:].unsqueeze(2).to_broadcast([1, 8, 64]), op=ALU.mult), ["ostg", "rt"], ["ostg"])
        dma(io["rowsc"][b:b + 1, 0:512], ostg[0:1, 512:1024], ["ostg"], ["rowsc"])

    f["to_feature_major"](io["rowsc"], NS, 8, t["mixT"], "mixT", rkey="rowsc")
    for c in range(4):
        dve(lambda e, c=c: e.scalar_tensor_tensor(out=t["mixT"][:, 4 + c, 0:Tn], in0=t["xa"][:, c, 0:Tn], scalar=v["dsk"][:, c:c + 1],
                                                  in1=t["mixT"][:, 4 + c, 0:Tn], op0=ALU.mult, op1=ALU.add), ["xa", "dsk", "mixT"], ["mixT"])
    for c in range(4):
        dve(lambda e, c=c: e.tensor_copy(out=t["mixTb"][:, c, 0:Tn], in_=t["mixT"][:, c, 0:Tn]), ["mixT"], ["mixTb"])
    emit_mix_out(g, Tn, NS, NB)
    for l in range(2):
        if l == 1:
            f["norm_mod"](t["xT"], "xT", Tn, g.Amod[1][0], g.mod[1][0], "mod10", NS, NB, t["oT"], "oT")
            dma(out["pool_sample"][:, 0:14, :], io["st_pool"][:, 1:15, :], [], [])
            f["to_token_major_out"](t["oT"], "oT", 8, NS, 0, out["pool_sample"][:, 14, :])
            accp = t["pT"][1][:, 0:8 * NS].rearrange("p (c n) -> p c n", c=8)
            rowp = t["pT"][0][:, 0:8 * NS].rearrange("p (c n) -> p c n", c=8)
            dve(lambda e: e.tensor_copy(out=accp, in_=t["oT"][:, :, 0:Tn]), ["oT"], ["pT1"])
            for j in range(1, 16):
                f["to_feature_major"](io["st_pool"][:, 15 - j, :], NS, 8, rowp, "pT0")
                for gi, win in enumerate((2, 4, 8, 16)):
                    if win > j:
                        cs_ = slice(2 * gi, 2 * gi + 2)
                        dve(lambda e, cs_=cs_: e.tensor_tensor(out=accp[:, cs_, :], in0=accp[:, cs_, :], in1=rowp[:, cs_, :], op=ALU.add),
                            ["pT0", "pT1"], ["pT1"])
            for c in range(8):
                win = (2, 4, 8, 16)[c // 2]
                dve(lambda e, c=c, win=win: e.scalar_tensor_tensor(out=t["mixTb"][:, c, 0:Tn], in0=accp[:, c, :], scalar=1.0 / win,
                                                                   in1=t["oT"][:, c, 0:Tn], op0=ALU.mult, op1=ALU.subtract),
                    ["pT1", "oT"], ["mixTb"])
            emit_pool_proj(g, Tn, NS, NB)
        for j in range(2):
            for c0 in range(0, 44, 8):
                n = min(8, 44 - c0)
                f["to_feature_major"](io["st_fconv"][l, :, j, c0 * 128:(c0 + n) * 128], NS, n, t["fprev"][j][:, c0:c0 + n, :], "actT")
        emit_ffn(g, l, Tn, NS, NB, sample=True)
        dma(out["fconv_sample"][l, :, 0, :], io["st_fconv"][l, :, 1, :], [], [])
        for c0 in range(0, 44, 8):
            n = min(8, 44 - c0)
            f["to_token_major_out"](t["hup"][:, c0:c0 + n, :], "actT", n, NS, 0, out["fconv_sample"][l, :, 1, c0 * 128:(c0 + n) * 128])
    f["to_token_major_out"](t["xT"], "xT", 8, NS, 0, out["y_sample"])


_cache = {}


def kernel(**inp):
    if "g" not in _cache:
        g = build()
        emit_all(g)
        g.P.emit()
        _cache["g"] = g
    g = _cache["g"]
    hc = host_consts()
    f32 = lambda a: np.ascontiguousarray(np.asarray(a), dtype=np.float32)
    xp = f32(inp["x_prompt"]); xs = f32(inp["x_sample"]).reshape(NS_TOT, D)
    ck = f32(inp["cache_k"]).reshape(5120 * 64, 1024); cv = f32(inp["cache_v"]).reshape(5120 * 64, 1024)
    sssm = f32(inp["state_ssm"]).reshape(NS_TOT, H, HD, 128); ssc = f32(inp["state_ssd_conv"]).reshape(NS_TOT, 3, 1024)
    spool = f32(inp["state_pool"]).reshape(NS_TOT, 15, D); sfc = f32(inp["state_ffn_conv"])
    pt = np.ascontiguousarray(np.asarray(inp["page_table"]), dtype=np.int32)
    cpr = f32(inp["c_prompt"]); csa = f32(inp["c_sample"])
    shared = {
        "ada_w": f32(inp["ada_w"]), "ada_b": f32(inp["ada_b"]), "norm_pre": f32(inp["norm_pre"]),
        "norm_post": f32(inp["norm_post"]), "mix_in_w": f32(inp["mix_in_w"]).reshape(D, MIXIN),
        "mix_out_w": f32(inp["mix_out_w"]).reshape(D, D), "ssd_conv_w": f32(inp["ssd_conv_w"]).reshape(4, 1024),
        "ssd_conv_b": f32(inp["ssd_conv_b"]).reshape(1024), "ssd_dt_bias": f32(inp["ssd_dt_bias"]).reshape(8),
        "ssd_a_log": f32(inp["ssd_a_log"]).reshape(8), "ssd_d": f32(inp["ssd_d"]).reshape(8),
        "ssd_norm_w": f32(inp["ssd_norm_w"]).reshape(512), "pool_w": f32(inp["pool_w"]).reshape(4, 256, 256),
        "pool_b": f32(inp["pool_b"]).reshape(1024), "pool_scale": f32(inp["pool_scale"]).reshape(1024),
        "ffn_up_w": f32(inp["ffn_up_w"]), "ffn_conv_w": f32(inp["ffn_conv_w"]), "ffn_conv_b": f32(inp["ffn_conv_b"]),
        "ffn_down_w": f32(inp["ffn_down_w"]),
        "cache_k0": ck[:HALF], "cache_k1": ck[HALF:], "cache_v0": cv[:HALF], "cache_v1": cv[HALF:],
    }
    for k, v in hc.items():
        shared["c_" + k] = v
    maps = []
    for c in range(NCORE):
        ps, ss = slice(c * NB, (c + 1) * NB), slice(c * NS, (c + 1) * NS)
        m = dict(shared)
        m.update({
            "x_prompt": np.ascontiguousarray(xp[ps]).reshape(NB * SEQ, D), "x_sample": np.ascontiguousarray(xs[ss]),
            "state_ssm": np.ascontiguousarray(sssm[ss]), "state_ssd_conv": np.ascontiguousarray(ssc[ss]),
            "state_pool": np.ascontiguousarray(spool[ss]), "state_ffn_conv": np.ascontiguousarray(sfc[:, ss]),
            "page_table": np.ascontiguousarray(pt[ss]), "c_all": np.concatenate([cpr[ps], csa[ss]], 0),
        })
        maps.append(m)
    res = run_bass_kernel_spmd(g.nc, maps, core_ids=list(range(NCORE)))
    R = res.results
    cat = lambda name, ax=0: np.concatenate([R[c][name] for c in range(NCORE)], axis=ax)
    return (
        cat("y_prompt").reshape(NB_TOT, SEQ, D), cat("y_sample").reshape(NS_TOT, 1, D),
        cat("k_prompt").reshape(1, NB_TOT, SEQ, H, HD), cat("v_prompt").reshape(1, NB_TOT, SEQ, H, HD),
        cat("ssm_prompt").reshape(1, NB_TOT, H, HD, 128), cat("sconv_prompt").reshape(1, NB_TOT, 3, 1024),
        cat("pool_prompt").reshape(1, NB_TOT, 15, D), cat("fconv_prompt", 1).reshape(2, NB_TOT, 2, 2 * DFF),
        cat("k_sample").reshape(1, NS_TOT, 1, H, HD), cat("v_sample").reshape(1, NS_TOT, 1, H, HD),
        cat("ssm_sample").reshape(1, NS_TOT, H, HD, 128), cat("sconv_sample").reshape(1, NS_TOT, 3, 1024),
        cat("pool_sample").reshape(1, NS_TOT, 15, D), cat("fconv_sample", 1).reshape(2, NS_TOT, 2, 2 * DFF),
    )
```
